# Optimizing a Trainium2 kernel written in Bass

```python
import math
import jax
import jax.numpy as jnp
from jax import lax
import numpy as np

D_MODEL = 1024
BATCH = 8
SEQ = 2048
DEPTH = 4
DEC_BATCH = 32
DEC_SEQ = 8
PAST_LEN = 8192
PAGE_SIZE = 128

N_EVEN = (DEPTH + 1) // 2
N_ODD = DEPTH // 2
RMS_EPS = 1e-6
D_FF = 4 * D_MODEL

D_A = D_MODEL // 2
DH_A = 64
H_A = D_A // DH_A
BRANCHES = ((128, 1), (512, 4), (2048, 16))
WINDOW = max(w for w, _ in BRANCHES)
ATTN_QBLK = 128

D_B = D_MODEL - D_A
CH_B = 16
G_B = D_B // CH_B
P_B = 64

D_INNER = 2 * D_MODEL
P_C = 64
H_C = D_INNER // P_C
G_C = 4
J_C = H_C // G_C
N_C = 128
CONV_K = 4
CONV_DIM = D_INNER + 2 * G_C * N_C
SSD_CHUNK = 128

kernel_name = 'hybrid_dilated_s5_ssd_decoder_step'


def _rmsnorm(x, w):
    xf = x.astype(jnp.float32)
    y = xf * lax.rsqrt(jnp.mean(xf * xf, axis=-1, keepdims=True) + RMS_EPS)
    return (y * w.astype(jnp.float32)).astype(x.dtype)


def _sq_relu_mlp(h, w_up, w_down):
    a = jax.nn.relu(h @ w_up)
    return (a * a) @ w_down


def _alibi_slopes(n):
    return jnp.asarray(np.power(2.0, -8.0 * np.arange(1, n + 1) / n), dtype=jnp.float32)


def _dilated_block(q, pos_q, k_all, v_all, base):
    n_keys = k_all.shape[1]
    slopes = _alibi_slopes(H_A)
    qf = q.astype(jnp.float32) * (DH_A ** -0.5)
    outs, lses = [], []
    for win, dil in BRANCHES:
        offs = jnp.arange(win // dil + 1, dtype=jnp.int32) * dil
        kpos = pos_q[:, None] - offs[None, :]
        idx = kpos - base
        valid = (kpos >= 0) & (idx >= 0)
        idx = jnp.clip(idx, 0, n_keys - 1)
        kg = k_all[:, idx].astype(jnp.float32)
        vg = v_all[:, idx].astype(jnp.float32)
        s = jnp.einsum('bthd,btkhd->bhtk', qf, kg) - slopes[:, None, None] * offs.astype(jnp.float32)
        s = jnp.where(valid[None, None], s, -jnp.inf)
        lse = jax.nn.logsumexp(s, axis=-1)
        prob = jnp.exp(s - lse[..., None])
        outs.append(jnp.einsum('bhtk,btkhd->bthd', prob, vg))
        lses.append(lse)
    wts = jax.nn.softmax(jnp.stack(lses), axis=0)
    wts = jnp.swapaxes(wts, 2, 3)[..., None]
    return jnp.sum(jnp.stack(outs) * wts, axis=0)


def _dilated_attention(q, pos_q, k_all, v_all, base):
    b, t = q.shape[:2]
    qb = ATTN_QBLK if t % ATTN_QBLK == 0 else t
    nb = t // qb
    if nb == 1:
        return _dilated_block(q, pos_q, k_all, v_all, base)
    qs = jnp.swapaxes(q.reshape(b, nb, qb, H_A, DH_A), 0, 1)
    ps = pos_q.reshape(nb, qb)
    out = lax.map(lambda args: _dilated_block(args[0], args[1], k_all, v_all, base), (qs, ps))
    return jnp.swapaxes(out, 0, 1).reshape(b, t, H_A, DH_A)


def _s5(u, h0, lam_re, lam_im, log_dt, b_re, b_im, c_re, c_im, d_skip, w_glu, b_glu):
    f32 = jnp.float32
    bsz, l, _ = u.shape
    uf = u.astype(f32).reshape(bsz, l, G_B, CH_B)
    lam = lax.complex(lam_re.astype(f32), lam_im.astype(f32))
    dt = jnp.exp(log_dt.astype(f32))[:, None]
    lam_bar = jnp.exp(lam * dt)
    b_bar = ((lam_bar - 1.0) / lam)[..., None] * lax.complex(b_re.astype(f32), b_im.astype(f32))
    c_mat = lax.complex(c_re.astype(f32), c_im.astype(f32))
    bu = jnp.einsum('blgc,gpc->blgp', uf.astype(jnp.complex64), b_bar)
    a = jnp.broadcast_to(lam_bar, bu.shape)

    def combine(e1, e2):
        return e2[0] * e1[0], e2[0] * e1[1] + e2[1]

    a_cum, h = lax.associative_scan(combine, (a, bu), axis=1)
    if h0 is not None:
        h0f = h0.astype(f32)
        h = h + a_cum * lax.complex(h0f[..., 0], h0f[..., 1])[:, None]
    y = jnp.einsum('blgp,gcp->blgc', h, c_mat).real + d_skip.astype(f32) * uf
    g = jax.nn.gelu(y.reshape(bsz, l, D_B))
    out = g * jax.nn.sigmoid(g @ w_glu.astype(f32) + b_glu.astype(f32))
    h_last = h[:, -1]
    return out.astype(u.dtype), jnp.stack([h_last.real, h_last.imag], axis=-1).astype(u.dtype)


def _even_mixer(h, k_past, v_past, s5_h0, pos0, w_in, w_out, s5p):
    bsz, l, _ = h.shape
    proj = h @ w_in
    q = proj[..., :D_A].reshape(bsz, l, H_A, DH_A)
    k = proj[..., D_A:2 * D_A].reshape(bsz, l, H_A, DH_A)
    v = proj[..., 2 * D_A:3 * D_A].reshape(bsz, l, H_A, DH_A)
    u = proj[..., 3 * D_A:]
    if k_past is None:
        k_all, v_all, base = k, v, 0
    else:
        k_all = jnp.concatenate([k_past.astype(k.dtype), k], axis=1)
        v_all = jnp.concatenate([v_past.astype(v.dtype), v], axis=1)
        base = pos0 - k_past.shape[1]
    pos_q = pos0 + jnp.arange(l, dtype=jnp.int32)
    o_a = _dilated_attention(q, pos_q, k_all, v_all, base).reshape(bsz, l, D_A).astype(h.dtype)
    o_b, s5_state = _s5(u, s5_h0, *s5p)
    out = jnp.concatenate([o_a, o_b], axis=-1) @ w_out
    return out, k, v, s5_state


def _ssd(xdt, dA, bm, cm, h0, chunk):
    b, l = xdt.shape[:2]
    c = l // chunk
    X = xdt.reshape(b, c, chunk, G_C, J_C, P_C)
    A = dA.reshape(b, c, chunk, G_C, J_C)
    Bc = bm.reshape(b, c, chunk, G_C, N_C)
    Cc = cm.reshape(b, c, chunk, G_C, N_C)
    a_cs = jnp.cumsum(A, axis=2)
    tri = jnp.tril(jnp.ones((chunk, chunk), bool))
    seg = a_cs[:, :, :, None] - a_cs[:, :, None, :]
    decay = jnp.exp(jnp.where(tri[None, None, :, :, None, None], seg, -jnp.inf))
    cb = jnp.einsum('bclgn,bcsgn->bclsg', Cc, Bc)
    y_diag = jnp.einsum('bclsg,bclsgj,bcsgjp->bclgjp', cb, decay, X)
    decay_to_end = jnp.exp(a_cs[:, :, -1:] - a_cs)
    chunk_states = jnp.einsum('bcsgn,bcsgj,bcsgjp->bcgjpn', Bc, decay_to_end, X)
    states = jnp.concatenate([h0[:, None], chunk_states], axis=1)
    tot = jnp.pad(a_cs[:, :, -1], ((0, 0), (1, 0), (0, 0), (0, 0)))
    tot_cs = jnp.cumsum(tot, axis=1)
    tri_c = jnp.tril(jnp.ones((c + 1, c + 1), bool))
    seg_c = tot_cs[:, :, None] - tot_cs[:, None, :]
    decay_c = jnp.exp(jnp.where(tri_c[None, :, :, None, None], seg_c, -jnp.inf))
    states = jnp.einsum('bzcgj,bcgjpn->bzgjpn', decay_c, states)
    prev, h_last = states[:, :-1], states[:, -1]
    y_off = jnp.einsum('bclgn,bcgjpn,bclgj->bclgjp', Cc, prev, jnp.exp(a_cs))
    return (y_diag + y_off).reshape(b, l, G_C, J_C, P_C), h_last


def _mamba2(h, conv_buf, ssm_h0, w_in, conv_w, conv_b, dt_bias, a_log, d_skip, gnorm_w, w_out):
    f32 = jnp.float32
    bsz, l, _ = h.shape
    zxbcdt = h @ w_in
    z = zxbcdt[..., :D_INNER]
    xbc = zxbcdt[..., D_INNER:D_INNER + CONV_DIM]
    dt = zxbcdt[..., D_INNER + CONV_DIM:]
    if conv_buf is None:
        left = jnp.zeros((bsz, CONV_K - 1, CONV_DIM), xbc.dtype)
    else:
        left = conv_buf.astype(xbc.dtype)
    xbc_full = jnp.concatenate([left, xbc], axis=1)
    conv = lax.conv_general_dilated(xbc_full, conv_w[:, None, :].astype(xbc.dtype), window_strides=(1,), padding='VALID', dimension_numbers=('NWC', 'WIO', 'NWC'), feature_group_count=CONV_DIM)
    xbc_c = jax.nn.silu(conv.astype(f32) + conv_b.astype(f32))
    x = xbc_c[..., :D_INNER].reshape(bsz, l, G_C, J_C, P_C)
    bm = xbc_c[..., D_INNER:D_INNER + G_C * N_C].reshape(bsz, l, G_C, N_C)
    cm = xbc_c[..., D_INNER + G_C * N_C:].reshape(bsz, l, G_C, N_C)
    dt = jax.nn.softplus(dt.astype(f32) + dt_bias.astype(f32)).reshape(bsz, l, G_C, J_C)
    a = -jnp.exp(a_log.astype(f32)).reshape(G_C, J_C)
    if ssm_h0 is None:
        h0 = jnp.zeros((bsz, G_C, J_C, P_C, N_C), f32)
    else:
        h0 = ssm_h0.astype(f32).reshape(bsz, G_C, J_C, P_C, N_C)
    chunk = SSD_CHUNK if l % SSD_CHUNK == 0 else l
    y, h_last = _ssd(x * dt[..., None], dt * a, bm, cm, h0, chunk)
    y = (y + d_skip.astype(f32).reshape(G_C, J_C)[..., None] * x).reshape(bsz, l, D_INNER)
    g = (y * jax.nn.silu(z.astype(f32))).reshape(bsz, l, G_C, D_INNER // G_C)
    g = g * lax.rsqrt(jnp.mean(g * g, axis=-1, keepdims=True) + RMS_EPS)
    g = g.reshape(bsz, l, D_INNER) * gnorm_w.astype(f32)
    out = g.astype(h.dtype) @ w_out
    return out, xbc_full[:, -(CONV_K - 1):], h_last.reshape(bsz, H_C, P_C, N_C).astype(h.dtype)


def _trunk(x, pos0, p, cache_k, cache_v, state_s5, state_conv, state_ssm):
    new_k, new_v, new_s5, new_conv, new_ssm = [], [], [], [], []
    for i in range(DEPTH):
        j = i // 2
        h = _rmsnorm(x, p['norm_mix_pre'][i])
        if i % 2 == 0:
            s5p = (p['s5_lambda_re'][j], p['s5_lambda_im'][j], p['s5_log_dt'][j], p['s5_b_re'][j], p['s5_b_im'][j], p['s5_c_re'][j], p['s5_c_im'][j], p['s5_d'][j], p['s5_w_glu'][j], p['s5_b_glu'][j])
            m, k_new, v_new, s5_st = _even_mixer(h, None if cache_k is None else cache_k[j], None if cache_v is None else cache_v[j], None if state_s5 is None else state_s5[j], pos0, p['w_in_even'][j], p['w_out_even'][j], s5p)
            if cache_k is None:
                keep = min(WINDOW, k_new.shape[1])
                k_new, v_new = k_new[:, -keep:], v_new[:, -keep:]
            new_k.append(k_new)
            new_v.append(v_new)
            new_s5.append(s5_st)
        else:
            m, conv_st, ssm_st = _mamba2(h, None if state_conv is None else state_conv[j], None if state_ssm is None else state_ssm[j], p['w_in_odd'][j], p['conv_w'][j], p['conv_b'][j], p['dt_bias'][j], p['a_log'][j], p['d_skip'][j], p['gnorm_w'][j], p['w_out_odd'][j])
            new_conv.append(conv_st)
            new_ssm.append(ssm_st)
        x = x + _rmsnorm(m, p['norm_mix_post'][i])
        h = _rmsnorm(x, p['norm_mlp_pre'][i])
        x = x + _rmsnorm(_sq_relu_mlp(h, p['w_mlp_up'][i], p['w_mlp_down'][i]), p['norm_mlp_post'][i])
    return x, jnp.stack(new_k), jnp.stack(new_v), jnp.stack(new_s5), jnp.stack(new_conv), jnp.stack(new_ssm)


def setup_inputs(seed: int = 0) -> dict:
    key = jax.random.key(seed)
    keys = iter(jax.random.split(key, 48))
    f32 = jnp.float32

    def nrm(shape, scale=1.0):
        return scale * jax.random.normal(next(keys), shape, f32)

    def unif(shape, lo, hi):
        return jax.random.uniform(next(keys), shape, f32, lo, hi)

    w_buf = min(WINDOW, PAST_LEN)
    dt0 = jnp.exp(unif((N_ODD, H_C), math.log(1e-3), math.log(1e-1)))
    return {
        'x_prompt': nrm((BATCH, SEQ, D_MODEL)),
        'x_sample': nrm((DEC_BATCH, DEC_SEQ, D_MODEL)),
        'cache_k': nrm((N_EVEN, DEC_BATCH, w_buf, H_A, DH_A)),
        'cache_v': nrm((N_EVEN, DEC_BATCH, w_buf, H_A, DH_A)),
        'state_s5': nrm((N_EVEN, DEC_BATCH, G_B, P_B, 2), 0.1),
        'state_conv': nrm((N_ODD, DEC_BATCH, CONV_K - 1, CONV_DIM)),
        'state_ssm': nrm((N_ODD, DEC_BATCH, H_C, P_C, N_C), 0.1),
        'norm_mix_pre': 1.0 + nrm((DEPTH, D_MODEL), 0.05),
        'norm_mix_post': 1.0 + nrm((DEPTH, D_MODEL), 0.05),
        'norm_mlp_pre': 1.0 + nrm((DEPTH, D_MODEL), 0.05),
        'norm_mlp_post': 1.0 + nrm((DEPTH, D_MODEL), 0.05),
        'w_mlp_up': nrm((DEPTH, D_MODEL, D_FF), D_MODEL ** -0.5),
        'w_mlp_down': nrm((DEPTH, D_FF, D_MODEL), D_FF ** -0.5),
        'w_in_even': nrm((N_EVEN, D_MODEL, 3 * D_A + D_B), D_MODEL ** -0.5),
        'w_out_even': nrm((N_EVEN, D_A + D_B, D_MODEL), (D_A + D_B) ** -0.5),
        's5_lambda_re': -0.5 + nrm((N_EVEN, G_B, P_B), 0.01),
        's5_lambda_im': jnp.pi * jnp.arange(P_B, dtype=f32) + nrm((N_EVEN, G_B, P_B), 0.01),
        's5_log_dt': unif((N_EVEN, G_B), math.log(1e-3), math.log(1e-1)),
        's5_b_re': nrm((N_EVEN, G_B, P_B, CH_B), (2 * CH_B) ** -0.5),
        's5_b_im': nrm((N_EVEN, G_B, P_B, CH_B), (2 * CH_B) ** -0.5),
        's5_c_re': nrm((N_EVEN, G_B, CH_B, P_B), (2 * P_B) ** -0.5),
        's5_c_im': nrm((N_EVEN, G_B, CH_B, P_B), (2 * P_B) ** -0.5),
        's5_d': nrm((N_EVEN, G_B, CH_B)),
        's5_w_glu': nrm((N_EVEN, D_B, D_B), D_B ** -0.5),
        's5_b_glu': nrm((N_EVEN, D_B), 0.01),
        'w_in_odd': nrm((N_ODD, D_MODEL, D_INNER + CONV_DIM + H_C), D_MODEL ** -0.5),
        'conv_w': nrm((N_ODD, CONV_K, CONV_DIM), 0.5),
        'conv_b': nrm((N_ODD, CONV_DIM), 0.01),
        'dt_bias': dt0 + jnp.log(-jnp.expm1(-dt0)),
        'a_log': jnp.log(unif((N_ODD, H_C), 1.0, 16.0)),
        'd_skip': 1.0 + nrm((N_ODD, H_C), 0.01),
        'gnorm_w': 1.0 + nrm((N_ODD, D_INNER), 0.05),
        'w_out_odd': nrm((N_ODD, D_INNER, D_MODEL), D_INNER ** -0.5),
    }


def reference(x_prompt, x_sample, cache_k, cache_v, state_s5, state_conv, state_ssm,
              norm_mix_pre, norm_mix_post, norm_mlp_pre, norm_mlp_post, w_mlp_up, w_mlp_down,
              w_in_even, w_out_even, s5_lambda_re, s5_lambda_im, s5_log_dt, s5_b_re, s5_b_im,
              s5_c_re, s5_c_im, s5_d, s5_w_glu, s5_b_glu,
              w_in_odd, conv_w, conv_b, dt_bias, a_log, d_skip, gnorm_w, w_out_odd):
    p = dict(norm_mix_pre=norm_mix_pre, norm_mix_post=norm_mix_post, norm_mlp_pre=norm_mlp_pre,
             norm_mlp_post=norm_mlp_post, w_mlp_up=w_mlp_up, w_mlp_down=w_mlp_down,
             w_in_even=w_in_even, w_out_even=w_out_even, s5_lambda_re=s5_lambda_re,
             s5_lambda_im=s5_lambda_im, s5_log_dt=s5_log_dt, s5_b_re=s5_b_re, s5_b_im=s5_b_im,
             s5_c_re=s5_c_re, s5_c_im=s5_c_im, s5_d=s5_d, s5_w_glu=s5_w_glu, s5_b_glu=s5_b_glu,
             w_in_odd=w_in_odd, conv_w=conv_w, conv_b=conv_b, dt_bias=dt_bias, a_log=a_log,
             d_skip=d_skip, gnorm_w=gnorm_w, w_out_odd=w_out_odd)
    y_prompt, k_p, v_p, s5_p, conv_p, ssm_p = _trunk(x_prompt, 0, p, None, None, None, None, None)
    y_sample, k_s, v_s, s5_s, conv_s, ssm_s = _trunk(x_sample, PAST_LEN, p, cache_k, cache_v, state_s5, state_conv, state_ssm)
    return (y_prompt, y_sample, k_p, v_p, s5_p, conv_p, ssm_p, k_s, v_s, s5_s, conv_s, ssm_s)
```

```python
import numpy as np
from contextlib import ExitStack
import concourse.bass as bass
import concourse.mybir as mybir
from concourse.bass_utils import run_bass_kernel_spmd

F32 = mybir.dt.float32
BF16 = mybir.dt.bfloat16
ALU = mybir.AluOpType
AF = mybir.ActivationFunctionType
AX = mybir.AxisListType

ENGS = ['pe', 'act', 'dve', 'pool', 'sp']


class Op:
    __slots__ = ('eng', 'fn', 'deps', 'marked', 'semval', 'dsem', 'dval', 'idx', 'ptail')

    def __init__(self, eng, fn):
        self.eng = eng
        self.fn = fn
        self.deps = []
        self.marked = False
        self.semval = 0
        self.dsem = None
        self.dval = 0


class Prog:
    def __init__(self, nc, stack, arena_bytes=210944):
        self.nc = nc
        self.stack = stack
        self.ops = {e: [] for e in ENGS}
        self.sems = {e: stack.enter_context(nc.semaphore('s_' + e)) for e in ENGS}
        self.dsems = {}
        self.last_w = {}
        self.readers = {}
        self.phase_op = None
        self.last_of = {}
        self.sync_same_engine = True
        self.hoist_floor = 0
        self.hoist_prev = None
        self.pool_tail = None
        self.unchained = set()
        arena = nc.alloc_sbuf_tensor('arena', [128, arena_bytes // 4], F32)
        self.base = nc.lookup_mloc(arena).addr
        self.limit = self.base + arena_bytes
        self.pers = self.base
        self.cur = None
        self.nid = 0
        self.peak = 0

    def _alloc(self, off, shape, dtype, name):
        self.nid += 1
        return self.nc.alloc_sbuf_tensor_at('%s_%d' % (name or 't', self.nid), list(shape), dtype, offset=off)

    @staticmethod
    def _bytes(shape, dtype):
        n = 1
        for s in shape[1:]:
            n *= s
        n *= mybir.dt.size(dtype)
        return (n + 63) // 64 * 64

    def sbp(self, shape, dtype, name=None):
        assert self.cur is None
        off = self.pers
        self.pers += self._bytes(shape, dtype)
        assert self.pers <= self.limit, 'sbuf overflow (persistent)'
        return self._alloc(off, shape, dtype, name)

    def phase_begin(self, keep=0):
        self.barrier()
        self.cur = self.pers + keep

    def sb(self, shape, dtype, name=None):
        off = self.cur
        self.cur += self._bytes(shape, dtype)
        assert self.cur <= self.limit, 'sbuf overflow (phase) need %d' % (self.cur - self.limit)
        self.peak = max(self.peak, self.cur - self.base)
        return self._alloc(off, shape, dtype, name)

    def _stream(self, o):
        return o.dsem if o.dsem is not None else o.eng

    def _val(self, d):
        return d.dval if d.dsem is not None else d.idx

    def op(self, eng, fn, r=(), w=(), dsem=None, hoist=False, chain=True):
        o = Op(eng, fn)
        deps = {}
        pr = [x for x in r if isinstance(x, str) and x.startswith('ps')]
        if pr:
            r = [x for x in r if x not in pr]
            w = list(w) + pr

        def add(d):
            if d is None:
                return
            s = self._stream(d)
            cur = deps.get(s)
            if cur is None or self._val(d) > self._val(cur):
                deps[s] = d

        for k in r:
            add(self.last_w.get(k))
        for k in w:
            add(self.last_w.get(k))
            for rd in self.readers.get(k, {}).values():
                add(rd)
        add(self.phase_op)
        if dsem is not None and not chain:
            self.unchained.add(dsem)
            deps.pop(dsem, None)
        for sname in list(deps.keys()):
            if sname in self.unchained and sname != dsem:
                deps[sname] = self.dsems[sname][2]
        if dsem is not None:
            ent = self.dsems.get(dsem)
            if ent is None:
                ent = [self.stack.enter_context(self.nc.semaphore('d_' + dsem)), 0, None]
                self.dsems[dsem] = ent
            if chain:
                add(ent[2])
            ent[1] += 16
            o.dsem = dsem
            o.dval = ent[1]
            ent[2] = o
        o.idx = len(self.ops[eng])
        final = []
        for s, d in deps.items():
            if d.dsem is None:
                if d.eng == eng and (eng == 'pe' or not self.sync_same_engine):
                    continue
                d.marked = True
            final.append(d)
        o.deps = final
        st = self._stream(o)
        for k in r:
            self.readers.setdefault(k, {})[st] = o
        for k in w:
            self.last_w[k] = o
            self.readers[k] = {}
        o.ptail = self.pool_tail
        if hoist and eng == 'pool':
            lst = self.ops[eng]
            pos = self.hoist_floor
            cands = [self.hoist_prev] + [d.ptail for d in final]
            for c in cands:
                if c is not None:
                    pos = max(pos, lst.index(c) + 1)
            lst.insert(pos, o)
            self.hoist_prev = o
        else:
            self.ops[eng].append(o)
            if eng == 'pool':
                self.pool_tail = o
        self.last_of[st] = o
        return o

    def barrier(self):
        self.hoist_floor = len(self.ops['pool'])
        self.hoist_prev = None
        o = Op('sp', lambda e: e.nop())
        o.ptail = self.pool_tail
        o.idx = len(self.ops['sp'])
        deps = []
        for s, d in self.last_of.items():
            if d.dsem is None:
                d.marked = True
            deps.append(d)
        o.deps = deps
        o.marked = True
        self.ops['sp'].append(o)
        self.last_of['sp'] = o
        self.phase_op = o
        self.last_w = {}
        self.readers = {}
        return o

    def emit(self):
        nc = self.nc
        for e in ENGS:
            c = 0
            for o in self.ops[e]:
                if o.dsem is None and o.marked:
                    c += 1
                    o.semval = c

        def run(ename, eng):
            waited = {}
            for o in self.ops[ename]:
                for d in o.deps:
                    if d.dsem is not None:
                        sem, val, key = self.dsems[d.dsem][0], d.dval, 'd_' + d.dsem
                    else:
                        sem, val, key = self.sems[d.eng], d.semval, d.eng
                    if waited.get(key, 0) >= val:
                        continue
                    waited[key] = val
                    eng.wait_ge(sem, val)
                ins = o.fn(eng)
                if o.dsem is not None:
                    ins.then_inc(self.dsems[o.dsem][0], 16)
                elif o.marked:
                    ins.then_inc(self.sems[ename], 1)

        with nc.Block() as block:
            @block.tensor
            def _(eng):
                run('pe', eng)

            @block.scalar
            def _(eng):
                run('act', eng)

            @block.vector
            def _(eng):
                run('dve', eng)

            @block.gpsimd
            def _(eng):
                run('pool', eng)

            @block.sync
            def _(eng):
                run('sp', eng)


D = 1024
SEQ = 2048
NT = SEQ // 128
DEPTH = 4
NSS = 4
LS = 8
NSTOK = NSS * LS
DFF = 4096
EPS = 1e-6
NCORES = 8


class K:
    pass


def build(cfg):
    nc = bass.Bass("TRN2", target_bir_lowering=False)
    depth = cfg.get('depth', DEPTH)
    do_mix = cfg.get('mix', 7)
    k = K()
    k.nc = nc
    k.cfg = cfg
    k.scratch = {}

    def din(name, shape):
        return nc.dram_tensor(name, list(shape), F32, kind="ExternalInput").ap()

    def dout(name, shape):
        return nc.dram_tensor(name, list(shape), F32, kind="ExternalOutput").ap()

    k.x_p = din('x_p', [SEQ, D])
    k.x_s = din('x_s', [NSTOK, D])
    k.norms = din('norms', [4, DEPTH, D])
    k.w_up = din('w_mlp_up', [DEPTH, D, DFF])
    k.w_down = din('w_mlp_down', [DEPTH, DFF, D])
    k.identf = din('identf', [128, 128])
    k.w_in_odd = din('w_in_odd', [2, D, 5152])
    k.w_out_odd = din('w_out_odd', [2, DIN, D])
    k.conv_w = din('conv_w', [2, 4, NXBC])
    k.conv_b = din('conv_b', [2, NXBC])
    k.dt_bias = din('dt_bias', [2, 32])
    k.a_log = din('a_log', [2, 32])
    k.d_skip = din('d_skip', [2, 32])
    k.gnorm_w = din('gnorm_w', [2, DIN])
    k.state_conv = din('state_conv', [2, NSS, 3, NXBC])
    k.state_ssm = din('state_ssm', [2, NSS, 32, 64, 128])
    k.c_maskp = din('c_maskp', [128, 128])
    k.c_masks = din('c_masks', [NSTOK, NSTOK])
    k.c_selendp = din('c_selendp', [128, 128])
    k.c_selends = din('c_selends', [NSTOK, NSTOK])
    k.c_selendBs = din('c_selendBs', [NSTOK, NSS, 128])
    k.c_seqcol = din('c_seqcol', [128, NSS, NSTOK])
    k.c_seqrow = din('c_seqrow', [NSTOK, NSS])
    k.c_negp = din('c_negp', [128, 128])
    k.c_negs = din('c_negs', [NSTOK, NSTOK])
    k.w_in_even = din('w_in_even', [2, D, 2048])
    k.w_out_even = din('w_out_even', [2, D, D])
    k.s5_lambda_re = din('s5_lambda_re', [2, 32, 64])
    k.s5_lambda_im = din('s5_lambda_im', [2, 32, 64])
    k.s5_log_dt = din('s5_log_dt', [2, 32])
    k.s5_b_re = din('s5_b_re', [2, 32, 64, 16])
    k.s5_b_im = din('s5_b_im', [2, 32, 64, 16])
    k.s5_c_re = din('s5_c_re', [2, 32, 16, 64])
    k.s5_c_im = din('s5_c_im', [2, 32, 16, 64])
    k.s5_d = din('s5_d', [2, 32, 16])
    k.s5_w_glu = din('s5_w_glu', [2, 512, 512])
    k.s5_b_glu = din('s5_b_glu', [2, 512])
    k.state_s5 = din('state_s5', [2, NSS, 32, 64, 2])
    k.cache_k = din('cache_k', [2, NSS, 2048, 512])
    k.cache_v = din('cache_v', [2, NSS, 2048, 512])
    k.c_G = din('c_G', [128, 2304])
    k.c_kaug = din('c_kaug', [4, 128])
    k.c_qaug = din('c_qaug', [4, 512])
    k.c_qscale = din('c_qscale', [128, 4])
    k.c_swp = din('c_swp', [128, 128])
    k.c_rowmask = din('c_rowmask', [128, 8])
    k.c_iota1 = din('c_iota1', [128, 512])
    k.c_Gs = din('c_Gs', [128, 17, 64])
    k.c_qaug_s = din('c_qaug_s', [4, 64])
    k.c_kaug_s = din('c_kaug_s', [4, 17, 128])
    k.c_Gsn = din('c_Gsn', [NSTOK, NSS, 64])
    k.y_p = dout('y_p', [SEQ, D])
    k.y_s = dout('y_s', [NSTOK, D])
    k.k_p = dout('k_p', [2, SEQ, 512])
    k.v_p = dout('v_p', [2, SEQ, 512])
    k.s5_p = dout('s5_p', [2, 32, 64, 2])
    k.k_s = dout('k_s', [2, NSTOK, 512])
    k.v_s = dout('v_s', [2, NSTOK, 512])
    k.s5_s = dout('s5_s', [2, NSS, 32, 64, 2])
    k.conv_p = dout('conv_p', [2, 3, NXBC])
    k.ssm_p = dout('ssm_p', [2, 32, 64, 128])
    k.conv_s = dout('conv_s', [2, NSS, 3, NXBC])
    k.ssm_s = dout('ssm_s', [2, NSS, 32, 64, 128])

    with ExitStack() as st:
        P = Prog(nc, st)
        k.P = P
        k.X = P.sbp([128, NT, D], F32, 'X')
        k.Xs = P.sbp([128, 1, D], F32, 'Xs')
        k.ident = P.sbp([128, 128], BF16, 'ident')
        k.identF = P.sbp([128, 128], F32, 'identF')
        k.wctr = 0
        k.small = P.sbp([128, 64], F32, 'small')
        k.smctr = 0
        k.ps = [nc.alloc_psum_tensor('ps%d' % i, [128, 512], F32) for i in range(8)]
        cp = {'mask': P.sbp([128, 128], F32, 'c_maskp'), 'selend': P.sbp([128, 128], F32, 'c_selendp')}
        cp['selendB'] = cp['selend'][:].rearrange("p (b n) -> p b n", b=1)
        cs = {'mask': P.sbp([128, NSTOK], F32, 'c_masks'), 'selend': P.sbp([128, NSTOK], F32, 'c_selends'),
              'selendB': P.sbp([128, NSS, 128], F32, 'c_selendBs'), 'seqcol': P.sbp([128, NSS, NSTOK], F32, 'c_seqcol'),
              'seqrow': P.sbp([128, NSS], F32, 'c_seqrow')}
        k.cst_p, k.cst_s = cp, cs
        k.qscale = P.sbp([128, 4], F32, 'qscale')
        k.swp = P.sbp([128, 128], BF16, 'swp')
        k.rowmask = P.sbp([128, 8], F32, 'rowmask')
        k.iota1 = P.sbp([128, 512], F32, 'iota1')
        P.op('sp', lambda e: e.dma_start(out=k.qscale[:], in_=k.c_qscale[:, :]), w=['qscale'], dsem='cst', chain=False)
        P.op('sp', lambda e: e.dma_start(out=k.rowmask[:], in_=k.c_rowmask[:, :]), w=['rowmask'], dsem='cst', chain=False)
        P.op('sp', lambda e: e.dma_start(out=k.iota1[:], in_=k.c_iota1[:, :]), w=['iota1'], dsem='cst', chain=False)
        P.op('pool', lambda e: e.dma_start(out=k.swp[:], in_=k.c_swp[:, :]), w=['swp'], dsem='swp')
        P.op('sp', lambda e: e.dma_start(out=cp['mask'][:], in_=k.c_maskp[:, :]), w=['cst'], dsem='cst', chain=False)
        P.op('sp', lambda e: e.dma_start(out=cp['selend'][:], in_=k.c_selendp[:, :]), w=['cst'], dsem='cst', chain=False)
        P.op('sp', lambda e: e.dma_start(out=cs['mask'][0:NSTOK, :], in_=k.c_masks[:, :]), w=['cst'], dsem='cst', chain=False)
        P.op('sp', lambda e: e.dma_start(out=cs['selend'][0:NSTOK, :], in_=k.c_selends[:, :]), w=['cst'], dsem='cst', chain=False)
        P.op('sp', lambda e: e.dma_start(out=cs['selendB'][0:NSTOK, :, :], in_=k.c_selendBs[:, :, :]), w=['cst'], dsem='cst', chain=False)
        P.op('sp', lambda e: e.dma_start(out=cs['seqcol'][:, :, :], in_=k.c_seqcol[:, :, :]), w=['cst'], dsem='cst', chain=False)
        P.op('sp', lambda e: e.dma_start(out=cs['seqrow'][0:NSTOK, :], in_=k.c_seqrow[:, :]), w=['cst'], dsem='cst', chain=False)
        k.psb = [p[:].bitcast(BF16) for p in k.ps]

        P.op('pool', lambda e: e.dma_start(out=k.ident[:], in_=k.identf[:, :]), w=['ident'], dsem='ident')
        P.op('sp', lambda e: e.dma_start(out=k.identF[:], in_=k.identf[:, :]), w=['identF'], dsem='identF')
        xv = k.x_p.rearrange("(i p) d -> p i d", p=128)
        for q in range(4):
            P.op('sp', lambda e, q=q: e.dma_start(out=k.X[:, 4 * q:4 * q + 4, :], in_=xv[:, 4 * q:4 * q + 4, :]),
                 w=['X%d' % i for i in range(4 * q, 4 * q + 4)], dsem='xload%d' % q)
        P.op('sp', lambda e: e.dma_start(out=k.Xs[0:NSTOK, 0, :], in_=k.x_s[:, :]), w=['Xs0'], dsem='xsload')

        k.ptiles = [(k.X, i, 128, 'X%d' % i) for i in range(NT)]
        k.stile = (k.Xs, 0, NSTOK, 'Xs0')

        for l in range(depth):
            if (do_mix & 2) and l % 2 == 0:
                s5_phase(k, l, l // 2, 'p')
                attn_phase_p(k, l, l // 2)
                if do_mix & 4:
                    s5_phase(k, l, l // 2, 's')
                    attn_phase_s(k, l, l // 2)
            if (do_mix & 1) and l % 2 == 1:
                mamba(k, l, l // 2, 'p')
                mamba(k, l, l // 2, 's')
            mlp(k, l)

        yv = k.y_p.rearrange("(i p) d -> p i d", p=128)
        for q in range(4):
            P.op('sp', lambda e, q=q: e.dma_start(out=yv[:, 4 * q:4 * q + 4, :], in_=k.X[:, 4 * q:4 * q + 4, :]),
                 r=['X%d' % i for i in range(4 * q, 4 * q + 4)], dsem='ystore%d' % q)
        P.op('sp', lambda e: e.dma_start(out=k.y_s[:, :], in_=k.Xs[0:NSTOK, 0, :]), r=['Xs0'], dsem='ysstore')
        P.barrier()
        P.emit()
    k.stats = {e: len(P.ops[e]) for e in ENGS}
    k.peak = P.peak
    return nc, k


def cfg_get(k, name, default):
    return k.cfg.get(name, default)


def phase_common(k, l, widxs, nwbuf=3):
    P = k.P
    wbc = P.sb([128, 2, D], F32, 'wbc')
    k.wbc = wbc
    for jj, j in enumerate(widxs):
        P.op('sp', lambda e, j=j, jj=jj: e.dma_start(out=wbc[:, jj, :], in_=k.norms[j, l:l + 1, :].to_broadcast([128, D])),
             w=['wbc%d' % jj], dsem='wbc%d' % jj)
    k.wbuf = [P.sb([128, 4096], BF16, 'wbuf%d' % i) for i in range(nwbuf)]


def small_slot(k, n=2):
    c = (k.smctr % 32) * 2
    k.smctr += 1
    return c


def wload(k, P, skey, wv, wkey, src):
    if not k.cfg.get('scratch', True):
        P.op('pool', lambda e: e.dma_start(out=wv, in_=src), w=[wkey], dsem=wkey, hoist=True)
        return
    sc = k.scratch.get(skey)
    if sc is None:
        name = 'wsc_%d' % len(k.scratch)
        sc = k.nc.dram_tensor(name, [128, 8, 512], BF16, kind="Internal").ap()
        k.scratch[skey] = sc
        P.op('pool', lambda e: e.dma_start(out=wv, in_=src), w=[wkey], dsem=wkey, hoist=True)
        P.op('sp', lambda e: e.dma_start(out=sc[:, :, :], in_=wv), r=[wkey], w=[('sc', skey)], dsem='st_' + wkey)
    else:
        P.op('sp', lambda e: e.dma_start(out=wv, in_=sc[:, :, :]), r=[('sc', skey)], w=[wkey], dsem='h_' + wkey)


def next_wbuf(k):
    i = k.wctr % len(k.wbuf)
    k.wctr += 1
    return k.wbuf[i], 'wbuf%d' % i


def rstd_from_ss(k, np_, ss_ap, out_ap, rkeys, wkeys):
    P = k.P
    P.op('act', lambda e: e.activation(out=out_ap, in_=ss_ap, func=AF.Ln, scale=1.0 / D, bias=EPS), r=rkeys, w=wkeys)
    P.op('act', lambda e: e.activation(out=out_ap, in_=out_ap, func=AF.Exp, scale=-0.5), r=wkeys, w=wkeys)


def norm_transpose(k, tiles, widx, hT, hTkey, junk, hn):
    P = k.P
    wbc = k.wbc
    col = 0
    for ti, (X, i, np_, xkey) in enumerate(tiles):
        s = small_slot(k, 2)
        sk = 'sm%d' % s
        ss = k.small[0:np_, s:s + 1]
        rs = k.small[0:np_, s + 1:s + 2]
        jslot = ti % 2
        P.op('act', lambda e, X=X, i=i, np_=np_, ss=ss, jslot=jslot: e.activation(
            out=junk[jslot][0:np_, :], in_=X[0:np_, i, :], func=AF.Square, accum_out=ss),
            r=[xkey], w=[('junk', id(junk[jslot])), sk])
        rstd_from_ss(k, np_, ss, rs, [sk], [sk + 'r'])
        P.op('dve', lambda e, X=X, i=i, np_=np_, rs=rs, jslot=jslot: e.scalar_tensor_tensor(
            out=hn[jslot][0:np_, :], in0=X[0:np_, i, :], scalar=rs, in1=wbc[0:np_, widx, :],
            op0=ALU.mult, op1=ALU.mult), r=[xkey, sk + 'r', 'wbc%d' % widx], w=['hn%d' % (jslot if hn[0] is not hn[1] else 0)])
        pb = 6 + (ti % 2)
        for kc in range(8):
            P.op('pe', lambda e, kc=kc, np_=np_, jslot=jslot, pb=pb: e.transpose(
                k.psb[pb][:, kc * 128:kc * 128 + np_], hn[jslot][0:np_, kc * 128:(kc + 1) * 128], k.ident[0:np_, 0:np_]),
                r=['hn%d' % (jslot if hn[0] is not hn[1] else 0), 'ident'], w=['ps%d' % pb])
        src = k.psb[pb].rearrange("p (c t) -> p c t", c=8)[:, :, 0:np_]
        eng = 'act' if ti % 2 == 0 else 'dve'
        if eng == 'act':
            P.op('act', lambda e, src=src, col=col, np_=np_: e.activation(out=hT[:, :, col:col + np_], in_=src, func=AF.Copy),
                 r=['ps%d' % pb], w=[hTkey])
        else:
            P.op('dve', lambda e, src=src, col=col, np_=np_: e.tensor_copy(hT[:, :, col:col + np_], src),
                 r=['ps%d' % pb], w=[hTkey])
        col += np_
    return col


def post_norm_add(k, tiles, widx, mtmp, ssh, tmp_t):
    P = k.P
    wbc = k.wbc
    for ti, (X, i, np_, xkey) in enumerate(tiles):
        s = small_slot(k, 2)
        sk = 'sm%d' % s
        ss = k.small[0:np_, s:s + 1]
        rs = k.small[0:np_, s + 1:s + 2]
        P.op('dve', lambda e, np_=np_, ss=ss, ti=ti: e.tensor_tensor(out=ss, in0=ssh[0:np_, 2 * ti:2 * ti + 1],
                                                                     in1=ssh[0:np_, 2 * ti + 1:2 * ti + 2], op=ALU.add),
             r=['ssh%d' % ti], w=[sk])
        rstd_from_ss(k, np_, ss, rs, [sk], [sk + 'r'])
        tt = tmp_t[ti % 2]
        tk = 'tmpt%d' % ((ti % 2) if tmp_t[0] is not tmp_t[1] else 0)
        P.op('dve', lambda e, np_=np_, rs=rs, ti=ti, tt=tt: e.scalar_tensor_tensor(
            out=tt[0:np_, :], in0=mtmp[0:np_, ti, :], scalar=rs, in1=wbc[0:np_, widx, :], op0=ALU.mult, op1=ALU.mult),
            r=['mtmp%d' % ti, sk + 'r', 'wbc%d' % widx], w=[tk])
        P.op('pool', lambda e, X=X, i=i, np_=np_, tt=tt: e.tensor_tensor(out=X[0:np_, i, :], in0=X[0:np_, i, :], in1=tt[0:np_, :], op=ALU.add),
             r=[tk, xkey], w=[xkey])


def chunks_of(n):
    out = []
    c = 0
    while c < n:
        m = min(512, n - c)
        out.append((c, m))
        c += m
    return out


def mlp(k, l):
    P = k.P
    P.phase_begin()
    phase_common(k, l, (2, 3), nwbuf=cfg_get(k, "mlp_nwbuf", 5))
    NC = 512 + NSTOK
    hT = P.sb([128, 8, NC], BF16, 'hT')
    aT = P.sb([128, 32, NC], BF16, 'aT')
    junk = [P.sb([128, D], BF16, 'junk%d' % i) for i in range(2)]
    hn = [P.sb([128, D], BF16, 'hn%d' % i) for i in range(2)]
    rt = [P.sb([128, 512], BF16, 'rt%d' % i) for i in range(2)]
    mtmp = P.sb([128, 5, D], F32, 'mtmp')
    ssh = P.sb([128, 16], F32, 'ssh')
    tmp_t = [P.sb([128, D], F32, 'tmpt%d' % i) for i in range(2)]
    wupv = k.w_up[l].rearrange("(kc p) n -> p kc n", p=128)
    wdnv = k.w_down[l].rearrange("(kc p) n -> p kc n", p=128)
    junkN = [P.sb([128, D], BF16, 'junkN')] * 2

    def tiles_of(b):
        t = k.ptiles[4 * b:4 * b + 4]
        return t + [k.stile] if b == 3 else t
    ncols = {0: norm_transpose(k, tiles_of(0), 0, hT, 'hT', junkN, hn)}
    for b in range(4):
        tiles = tiles_of(b)
        ncol = ncols[b]
        chs = chunks_of(ncol)
        cnt = 0
        for fb in range(8):
            wb, wkey = next_wbuf(k)
            wv = wb[:].rearrange("p (kc n) -> p kc n", kc=8)
            wload(k, P, ('up', l, fb), wv, wkey, wupv[:, :, fb * 512:(fb + 1) * 512])
            for m in range(4):
                for ci, (c0, n) in enumerate(chs):
                    if ci == 0:
                        pb = cnt % 2
                        pst = k.ps[pb][:, 0:n]
                        pkey = 'ps%d' % pb
                    else:
                        pst = k.ps[7][:, 256 * (cnt % 2):256 * (cnt % 2) + n]
                        pkey = 'ps7'
                    for kc in range(8):
                        P.op('pe', lambda e, pst=pst, wv=wv, kc=kc, m=m, c0=c0, n=n: e.matmul(
                            pst, lhsT=wv[:, kc, m * 128:(m + 1) * 128], rhs=hT[:, kc, c0:c0 + n], start=(kc == 0), stop=(kc == 7)),
                            r=[wkey, 'hT'], w=[pkey])
                    rslot = cnt % 2
                    P.op('act', lambda e, pst=pst, rslot=rslot, n=n: e.activation(out=rt[rslot][:, 0:n], in_=pst, func=AF.Relu),
                         r=[pkey], w=['rt%d' % rslot])
                    P.op('pool', lambda e, rslot=rslot, n=n, fb=fb, m=m, c0=c0: e.tensor_tensor(
                        out=aT[:, fb * 4 + m, c0:c0 + n], in0=rt[rslot][:, 0:n], in1=rt[rslot][:, 0:n], op=ALU.mult),
                        r=['rt%d' % rslot], w=['aT'])
                    cnt += 1
        if b < 3:
            ncols[b + 1] = norm_transpose(k, tiles_of(b + 1), 0, hT, 'hT', junkN, hn)
        proj_out(k, tiles, aT, 'aT', wdnv, 32, mtmp, ssh, junk, wtag=('dn', l))
        post_norm_add(k, tiles, 1, mtmp, ssh, tmp_t)


def proj_out(k, tiles, srcT, skey, wview, nK, mtmp, ssh, junk, wtag=None):
    P = k.P
    accb = [2, 3, 4, 5, 7]
    nkb = nK // 8
    for half in range(2):
        for kb in range(nkb):
            wb, wkey = next_wbuf(k)
            wv = wb[:].rearrange("p (kc n) -> p kc n", kc=8)
            wload(k, P, (wtag, half, kb), wv, wkey, wview[:, kb * 8:kb * 8 + 8, half * 512:(half + 1) * 512])
            col = 0
            for ti, (X, i, np_, xkey) in enumerate(tiles):
                for m in range(8):
                    if callable(srcT):
                        lhs, lk = srcT(kb * 8 + m, col, np_)
                    else:
                        lhs, lk = srcT[:, kb * 8 + m, col:col + np_], skey
                    P.op('pe', lambda e, ti=ti, np_=np_, kb=kb, m=m, wv=wv, lhs=lhs: e.matmul(
                        k.ps[accb[ti]][0:np_, :], lhsT=lhs, rhs=wv[:, m, :],
                        start=(kb == 0 and m == 0), stop=(kb == nkb - 1 and m == 7)),
                        r=[wkey, lk], w=['ps%d' % accb[ti]])
                col += np_
        for ti, (X, i, np_, xkey) in enumerate(tiles):
            pk = 'ps%d' % accb[ti]
            jslot = ti % 2
            P.op('act', lambda e, ti=ti, np_=np_, half=half, jslot=jslot: e.activation(
                out=junk[jslot][0:np_, 0:512], in_=k.ps[accb[ti]][0:np_, :], func=AF.Square,
                accum_out=ssh[0:np_, 2 * ti + half:2 * ti + half + 1]),
                r=[pk], w=['junk%d' % (jslot if junk[0] is not junk[1] else 0), 'ssh%d' % ti])
            P.op('dve', lambda e, ti=ti, np_=np_, half=half: e.tensor_copy(
                mtmp[0:np_, ti, half * 512:(half + 1) * 512], k.ps[accb[ti]][0:np_, :]),
                r=[pk], w=['mtmp%d' % ti])


DIN = 2048
NXBC = 3072
ZOFF, XOFF, DTOFF = 0, 2048, 5120


def mamba(k, l, j, grp):
    P = k.P
    nc = k.nc
    P.phase_begin()
    phase_common(k, l, (0, 1), nwbuf=2)
    prompt = grp == 'p'
    nseq = 1 if prompt else NSS
    SBT = 2
    Lb = 128 * SBT if prompt else LS
    NCOL = nseq * Lb
    T = 128 if prompt else NSTOK
    nchunk = NCOL // T
    nsb = NT // SBT if prompt else 1
    ntile = SBT if prompt else 1
    cst = k.cst_p if prompt else k.cst_s
    conv_out = k.conv_p if prompt else k.conv_s
    ssm_out = k.ssm_p if prompt else k.ssm_s

    hT = P.sb([128, 8, NCOL], BF16, 'hT')
    zs = P.sb([128, ntile, DIN], BF16, 'zs')
    xbcT = P.sb([128, 24, nseq, 4 + Lb], BF16, 'xbcT')
    xlast = P.sb([128, 24, nseq, 3], F32, 'xlast')
    xcT = [P.sb([128, NCOL], BF16, 'xcT%d' % i) for i in range(2)]
    accs = [P.sb([128, NCOL], F32, 'cacc%d' % i) for i in range(2)]
    BT = P.sb([128, 4, NCOL], BF16, 'BT')
    CT = P.sb([128, 4, NCOL], BF16, 'CT')
    x_tok = P.sb([128, ntile, DIN], BF16, 'x_tok')
    B_tok = P.sb([128, ntile, 512], BF16, 'B_tok')
    gnT = P.sb([128, 16, NCOL], BF16, 'gnT')
    ST = [P.sb([128, 4, 512], F32, 'ST%d' % b) for b in range(nseq)]
    STb = [P.sb([128, 512], BF16, 'STbs%d' % i) for i in range(2)]
    stctr = [0]
    gw_bc = P.sb([128, DIN], F32, 'gw_bc')
    D_bc = P.sb([128, 32], F32, 'D_bc')
    cw = P.sb([128, 24, 4], F32, 'cw')
    cbias = P.sb([128, 24], F32, 'cbias')
    vec = P.sb([64, 4], F32, 'vec')
    PK = P.sb([64, NCOL], F32, 'PK')
    dA = P.sb([64, NCOL], F32, 'dA')
    ones = P.sb([64, 128], F32, 'ones')
    tokpk = P.sb([128, nchunk, 64], F32, 'tokpk')
    Et = P.sb([128, 32], F32, 'Et')
    wend = P.sb([128, 32], F32, 'wend')
    Dtot = P.sb([128, nseq, 32], F32, 'Dtot')
    xdt = P.sb([128, 512], BF16, 'xdt')
    xw = P.sb([128, 512], BF16, 'xw')
    xD = P.sb([128, 512], BF16, 'xD')
    negm = P.sb([128, 128], F32, 'negm')
    dhb = P.sb([128, 512], F32, 'dhb')
    onesF = P.sb([128, 128], F32, 'onesF')
    e4 = [P.sb([128, 512], BF16, 'e4%d' % i) for i in range(2)]
    MT4 = [P.sb([128, 512], BF16, 'MT4%d' % i) for i in range(2)]
    nacs = P.sb([128, 32], F32, 'nacs')
    t1 = P.sb([128, 512], F32, 't1')
    gg = P.sb([128, 512], F32, 'gg')
    gn = P.sb([128, 512], BF16, 'gn')
    junk = [P.sb([128, D], BF16, 'junk0')] * 2
    hn = [P.sb([128, D], BF16, 'hn%d' % i) for i in range(2)]
    mtmp = P.sb([128, ntile, D], F32, 'mtmp')
    ssh = P.sb([128, 16], F32, 'ssh')
    tmp_t = [P.sb([128, D], F32, 'tmpt0')] * 2
    sout = P.sb([128, 4, 128], F32, 'sout')
    if not prompt:
        CTm = P.sb([128, NSS, NSTOK], BF16, 'CTm')
        Bm = P.sb([128, NSS, 128], BF16, 'Bm')
        sst = P.sb([128, 16, 128], F32, 'sst')
        cst_st = P.sb([128, 24, NSS, 3], F32, 'cst_st')

    P.op('sp', lambda e: e.dma_start(out=gw_bc[:], in_=k.gnorm_w[j:j + 1, :].to_broadcast([128, DIN])), w=['gw_bc'], dsem='gw_bc')
    P.op('sp', lambda e: e.dma_start(out=D_bc[:], in_=k.d_skip[j:j + 1, :].to_broadcast([128, 32])), w=['D_bc'], dsem='D_bc')
    with nc.allow_non_contiguous_dma(reason="small param relayout"):
        for kk in range(4):
            P.op('sp', lambda e, kk=kk: e.dma_start(out=cw[:, :, kk], in_=k.conv_w[j, kk].rearrange("(ft p) -> p ft", p=128), allow_slow_non_contiguous=True),
                 w=['cw'], dsem='cw', chain=False)
        P.op('sp', lambda e: e.dma_start(out=cbias[:], in_=k.conv_b[j].rearrange("(ft p) -> p ft", p=128), allow_slow_non_contiguous=True), w=['cbias'], dsem='cbias', chain=False)
        for half in range(2):
            P.op('sp', lambda e, half=half: e.dma_start(out=vec[32 * half:32 * half + 32, 0:1], in_=k.dt_bias[j].rearrange("(p o) -> p o", o=1), allow_slow_non_contiguous=True),
                 w=['vec'], dsem='vec', chain=False)
            P.op('sp', lambda e, half=half: e.dma_start(out=vec[32 * half:32 * half + 32, 1:2], in_=k.a_log[j].rearrange("(p o) -> p o", o=1), allow_slow_non_contiguous=True),
                 w=['vec'], dsem='vec', chain=False)
    P.op('act', lambda e: e.activation(out=vec[0:64, 2:3], in_=vec[0:64, 1:2], func=AF.Exp), r=['vec'], w=['vec2'])
    P.op('dve', lambda e: e.tensor_scalar(out=vec[0:64, 2:3], in0=vec[0:64, 2:3], scalar1=-1.0, scalar2=None, op0=ALU.mult), r=['vec2'], w=['vec2'])
    P.op('dve', lambda e: e.memset(ones[:], 1.0), w=['ones'])
    P.op('sp', lambda e: e.dma_start(out=negm[0:T, 0:T], in_=(k.c_negp if prompt else k.c_negs)[:, :]), w=['negm'], dsem='negm')
    P.op('dve', lambda e: e.memset(onesF[:], 1.0), w=['onesF'])
    if prompt:
        P.op('dve', lambda e: e.memset(xbcT[:, :, :, 0:4], 0.0), w=['xbcT'])
        P.op('dve', lambda e: e.memset(ST[0][:], 0.0), w=['ST0'])
    else:
        with nc.allow_non_contiguous_dma(reason="conv state relayout"):
            for b in range(NSS):
                for r_ in range(3):
                    P.op('sp', lambda e, b=b, r_=r_: e.dma_start(out=cst_st[:, :, b, r_], in_=k.state_conv[j, b, r_].rearrange("(ft p) -> p ft", p=128),
                                                                 allow_slow_non_contiguous=True), w=['cst_st'], dsem='cst_st', chain=False)
        P.op('dve', lambda e: e.tensor_copy(xbcT[:, :, :, 0:3], cst_st[:]), r=['cst_st'], w=['xbcT'])
        for b in range(NSS):
            sv = k.state_ssm[j, b].rearrange("h p n -> (h p) n").rearrange("(q r) n -> r q n", r=128)
            P.op('sp', lambda e, sv=sv: e.dma_start(out=sst[:], in_=sv), w=['sst'], dsem='sst')
            for gq in range(4):
                pb = 2 + gq % 2
                for qq in range(4):
                    P.op('pe', lambda e, gq=gq, qq=qq, pb=pb: e.transpose(k.ps[pb][:, qq * 128:(qq + 1) * 128], sst[:, gq * 4 + qq, :], k.identF[:]),
                         r=['sst', 'identF'], w=['ps%d' % pb])
                P.op('dve', lambda e, b=b, gq=gq, pb=pb: e.tensor_copy(ST[b][:, gq, :], k.ps[pb][:, :]), r=['ps%d' % pb], w=['ST%d' % b])

    wv_in = k.w_in_odd[j].rearrange("(kc p) n -> p kc n", p=128)
    wov = k.w_out_odd[j].rearrange("(kc p) n -> p kc n", p=128)
    maskc = cst['mask']
    for sb in range(nsb):
        tiles = k.ptiles[SBT * sb:SBT * sb + SBT] if prompt else [k.stile]
        last = sb == nsb - 1
        norm_transpose(k, tiles, 0, hT, 'hT', junk, hn)
        cnt = 0
        for cb in range(4):
            wb, wkey = next_wbuf(k)
            wv = wb[:].rearrange("p (kc n) -> p kc n", kc=8)
            wload(k, P, ('oddz', j, cb), wv, wkey, wv_in[:, :, ZOFF + cb * 512:ZOFF + (cb + 1) * 512])
            col = 0
            for ti, (X, i, np_, xkey) in enumerate(tiles):
                pb = cnt % 2
                cnt += 1
                for kc in range(8):
                    P.op('pe', lambda e, pb=pb, np_=np_, kc=kc, col=col, wv=wv: e.matmul(
                        k.ps[pb][0:np_, :], lhsT=hT[:, kc, col:col + np_], rhs=wv[:, kc, :], start=(kc == 0), stop=(kc == 7)),
                        r=[wkey, 'hT'], w=['ps%d' % pb])
                P.op('act', lambda e, pb=pb, np_=np_, ti=ti, cb=cb: e.activation(
                    out=zs[0:np_, ti, cb * 512:(cb + 1) * 512], in_=k.ps[pb][0:np_, :], func=AF.Silu), r=['ps%d' % pb], w=['zs'])
                col += np_
        for cb in range(6):
            wb, wkey = next_wbuf(k)
            wv = wb[:].rearrange("p (kc n) -> p kc n", kc=8)
            wload(k, P, ('oddx', j, cb), wv, wkey, wv_in[:, :, XOFF + cb * 512:XOFF + (cb + 1) * 512])
            for m in range(4):
                ft = cb * 4 + m
                pb = 2 + ft % 2
                for kc in range(8):
                    P.op('pe', lambda e, pb=pb, kc=kc, m=m, wv=wv: e.matmul(
                        k.ps[pb][:, 0:NCOL], lhsT=wv[:, kc, m * 128:(m + 1) * 128], rhs=hT[:, kc, 0:NCOL], start=(kc == 0), stop=(kc == 7)),
                        r=[wkey, 'hT'], w=['ps%d' % pb])
                src = k.ps[pb][:, 0:NCOL].rearrange("p (b t) -> p b t", b=nseq)
                if ft % 2 == 0:
                    P.op('dve', lambda e, ft=ft, src=src: e.tensor_copy(xbcT[:, ft, :, 3:3 + Lb], src), r=['ps%d' % pb], w=['xbcT'])
                else:
                    P.op('act', lambda e, ft=ft, src=src: e.activation(out=xbcT[:, ft, :, 3:3 + Lb], in_=src, func=AF.Copy), r=['ps%d' % pb], w=['xbcT'])
                if last:
                    P.op('dve', lambda e, ft=ft, src=src: e.tensor_copy(xlast[:, ft, :, :], src[:, :, Lb - 3:Lb]), r=['ps%d' % pb], w=['xlast'])
        wb, wkey = next_wbuf(k)
        wv = wb[:, 0:512].rearrange("p (kc n) -> p kc n", kc=8)
        for half in range(2):
            P.op('pool', lambda e, wv=wv, half=half: e.dma_start(out=wv[:, :, 32 * half:32 * half + 32], in_=wv_in[:, :, DTOFF:DTOFF + 32]),
                 w=[wkey], dsem=wkey, hoist=True)
        for kc in range(8):
            P.op('pe', lambda e, kc=kc, wv=wv: e.matmul(k.ps[4][0:64, 0:NCOL], lhsT=wv[:, kc, 0:64], rhs=hT[:, kc, 0:NCOL], start=(kc == 0), stop=(kc == 7)),
                 r=[wkey, 'hT'], w=['ps4'])
        P.op('act', lambda e: e.activation(out=PK[0:64, :], in_=k.ps[4][0:64, 0:NCOL], func=AF.Exp, bias=vec[0:64, 0:1]), r=['ps4', 'vec'], w=['PK'])
        P.op('act', lambda e: e.activation(out=PK[0:64, :], in_=PK[0:64, :], func=AF.Ln, bias=1.0), r=['PK'], w=['PK'])
        P.op('dve', lambda e: e.tensor_scalar(out=dA[32:64, :], in0=PK[32:64, :], scalar1=vec[32:64, 2:3], scalar2=None, op0=ALU.mult),
             r=['PK', 'vec2'], w=['dA'])
        slen = 128 if prompt else LS
        for s0 in range(0, NCOL, slen):
            P.op('dve', lambda e, s0=s0: e.tensor_tensor_scan(out=PK[32:64, s0:s0 + slen], data0=ones[32:64, 0:slen], data1=dA[32:64, s0:s0 + slen],
                                                             initial=0.0, op0=ALU.mult, op1=ALU.add), r=['dA', 'ones', 'PK'], w=['PK'])
        for c in range(nchunk):
            P.op('pe', lambda e, c=c: e.transpose(k.ps[5][0:T, 0:64], PK[0:64, c * T:(c + 1) * T], k.identF[0:64, 0:64]), r=['PK', 'identF'], w=['ps5'])
            P.op('dve', lambda e, c=c: e.tensor_copy(tokpk[0:T, c, :], k.ps[5][0:T, 0:64]), r=['ps5'], w=['tokpk'])
        for ft in range(24):
            a3 = accs[ft % 2][:, 0:NCOL].rearrange("p (b t) -> p b t", b=nseq)
            akey = 'cacc%d' % (ft % 2)
            P.op('pool', lambda e, a3=a3, ft=ft: e.tensor_scalar(out=a3, in0=xbcT[:, ft, :, 3:3 + Lb], scalar1=cw[:, ft, 3:4], scalar2=0.0,
                                                                 op0=ALU.mult, op1=ALU.add), r=['xbcT', 'cw'], w=[akey])
            for kk in (2, 1, 0):
                P.op('dve', lambda e, a3=a3, ft=ft, kk=kk: e.scalar_tensor_tensor(out=a3, in0=xbcT[:, ft, :, kk:kk + Lb], scalar=cw[:, ft, kk:kk + 1],
                                                                                   in1=a3, op0=ALU.mult, op1=ALU.add), r=['xbcT', 'cw', akey], w=[akey])
            if ft < 16:
                dst, dkey = xcT[ft % 2][:, 0:NCOL], 'xcT%d' % (ft % 2)
            elif ft < 20:
                dst, dkey = BT[:, ft - 16, 0:NCOL], 'BT'
            else:
                dst, dkey = CT[:, ft - 20, 0:NCOL], 'CT'
            P.op('act', lambda e, dst=dst, ft=ft: e.activation(out=dst, in_=accs[ft % 2][:, 0:NCOL], func=AF.Silu, bias=cbias[:, ft:ft + 1]),
                 r=[akey, 'cbias'], w=[dkey])
            if ft < 20:
                col = 0
                for ti, (X, i, np_, xkey) in enumerate(tiles):
                    P.op('pe', lambda e, ti=ti, np_=np_, col=col, dst=dst, ft=ft: e.transpose(
                        k.psb[ti][0:np_, (ft % 8) * 128:(ft % 8 + 1) * 128], dst[:, col:col + np_], k.ident[:, :]),
                        r=[dkey, 'ident'], w=['ps%d' % ti])
                    col += np_
                if ft % 8 == 7 or ft == 19:
                    for ti, (X, i, np_, xkey) in enumerate(tiles):
                        if ft < 16:
                            o_ap = x_tok[0:np_, ti, (ft // 8) * 1024:(ft // 8 + 1) * 1024]
                            i_ap = k.psb[ti][0:np_, :]
                            okey = 'x_tok'
                        else:
                            o_ap = B_tok[0:np_, ti, :]
                            i_ap = k.psb[ti][0:np_, 0:512]
                            okey = 'B_tok'
                        if ti % 2 == 0:
                            P.op('dve', lambda e, o_ap=o_ap, i_ap=i_ap: e.tensor_copy(o_ap, i_ap), r=['ps%d' % ti], w=[okey])
                        else:
                            P.op('act', lambda e, o_ap=o_ap, i_ap=i_ap: e.activation(out=o_ap, in_=i_ap, func=AF.Copy), r=['ps%d' % ti], w=[okey])
        if not last:
            P.op('dve', lambda e: e.tensor_copy(xbcT[:, :, :, 0:3], xbcT[:, :, :, Lb:Lb + 3]), r=['xbcT'], w=['xbcT'])
        for c in range(nchunk):
            ti = c if prompt else 0
            c0 = c * T
            dt_tok = tokpk[0:T, c, 0:32]
            acs = tokpk[0:T, c, 32:64]
            P.op('act', lambda e, acs=acs: e.activation(out=Et[0:T, :], in_=acs, func=AF.Exp), r=['tokpk'], w=['Et'])
            P.op('dve', lambda e, acs=acs: e.tensor_scalar(out=nacs[0:T, :], in0=acs, scalar1=-1.0, scalar2=None, op0=ALU.mult), r=['tokpk'], w=['nacs'])
            P.op('pe', lambda e, acs=acs: e.matmul(k.ps[5][0:T, 0:32], lhsT=cst['selend'][0:T, 0:T], rhs=acs, start=True, stop=True),
                 r=['tokpk', 'cst'], w=['ps5'])
            P.op('dve', lambda e, acs=acs: e.tensor_tensor(out=wend[0:T, :], in0=k.ps[5][0:T, 0:32], in1=acs, op=ALU.subtract), r=['ps5', 'tokpk'], w=['wend'])
            P.op('act', lambda e: e.activation(out=wend[0:T, :], in_=wend[0:T, :], func=AF.Exp), r=['wend'], w=['wend'])
            for b in range(nseq):
                P.op('pe', lambda e, acs=acs, b=b: e.matmul(k.ps[5][:, 64 + 32 * b:96 + 32 * b], lhsT=cst['selendB'][0:T, b, :], rhs=acs, start=True, stop=True),
                     r=['tokpk', 'cst'], w=['ps5'])
            P.op('act', lambda e: e.activation(out=Dtot[:, :, :], in_=k.ps[5][:, 64:64 + 32 * nseq].rearrange("p (b h) -> p b h", b=nseq), func=AF.Exp),
                 r=['ps5'], w=['Dtot'])
            psA = (3, 4)

            def stageA(gq, acs=acs):
                hs = gq * 8
                for q in range(2):
                    ab = psA[q]
                    dh3 = dhb[0:T, :].rearrange("p (j l) -> p j l", l=128)[:, :, 0:T]
                    P.op('dve', lambda e, dh3=dh3, q=q, hs=hs, acs=acs: e.tensor_tensor(
                        out=dh3, in0=k.identF[0:T, 0:T].unsqueeze(1).to_broadcast([T, 4, T]),
                        in1=acs[:, hs + 4 * q:hs + 4 * q + 4].unsqueeze(2).to_broadcast([T, 4, T]), op=ALU.mult), r=['identF', 'tokpk', 'dhb'], w=['dhb'])
                    if T == 128:
                        P.op('pe', lambda e, ab=ab: e.matmul(k.ps[ab][0:T, :], lhsT=onesF[0:T, 0:T], rhs=dhb[0:T, :], start=True, stop=False), r=['onesF', 'dhb'], w=['ps%d' % ab])
                    for jj in range(4):
                        o_ap = k.ps[ab][0:T, jj * 128:jj * 128 + T]
                        if T != 128:
                            P.op('pe', lambda e, o_ap=o_ap, jj=jj: e.matmul(o_ap, lhsT=onesF[0:T, 0:T], rhs=dhb[0:T, jj * 128:jj * 128 + T], start=True, stop=False),
                                 r=['onesF', 'dhb'], w=['ps%d' % ab])
                        P.op('pe', lambda e, o_ap=o_ap, jj=jj: e.matmul(o_ap, lhsT=k.identF[0:T, 0:T], rhs=negm[0:T, 0:T], start=False, stop=(T != 128 or jj == 3)),
                             r=['identF', 'negm'], w=['ps%d' % ab])

            for gq in range(4):
                hs = gq * 8
                xg = x_tok[0:T, ti, gq * 512:(gq + 1) * 512].rearrange("p (h d) -> p h d", h=8)
                P.op('dve', lambda e, xg=xg, dt_tok=dt_tok, hs=hs: e.tensor_tensor(
                    out=xdt[0:T, :].rearrange("p (h d) -> p h d", h=8), in0=xg, in1=dt_tok[:, hs:hs + 8].unsqueeze(2).to_broadcast([T, 8, 64]), op=ALU.mult),
                    r=['x_tok', 'tokpk'], w=['xdt'])
                P.op('dve', lambda e, hs=hs: e.tensor_tensor(
                    out=xw[0:T, :].rearrange("p (h d) -> p h d", h=8), in0=xdt[0:T, :].rearrange("p (h d) -> p h d", h=8),
                    in1=wend[0:T, hs:hs + 8].unsqueeze(2).to_broadcast([T, 8, 64]), op=ALU.mult), r=['xdt', 'wend'], w=['xw'])
                P.op('pool', lambda e, xg=xg, hs=hs: e.tensor_tensor(
                    out=xD[0:T, :].rearrange("p (h d) -> p h d", h=8), in0=xg, in1=D_bc[0:T, hs:hs + 8].unsqueeze(2).to_broadcast([T, 8, 64]), op=ALU.mult),
                    r=['x_tok', 'D_bc'], w=['xD'])
                P.op('pe', lambda e, gq=gq, c0=c0: e.matmul(k.ps[2][0:T, 0:T], lhsT=BT[:, gq, c0:c0 + T], rhs=CT[:, gq, c0:c0 + T], start=True, stop=True),
                     r=['BT', 'CT'], w=['ps2'])
                P.op('pe', lambda e: e.matmul(k.ps[0][0:T, :], lhsT=k.ident[0:T, 0:T], rhs=xD[0:T, :], start=True, stop=False), r=['ident', 'xD'], w=['ps0'])
                if gq == 0:
                    stageA(0)
                for jh in range(8):
                    h = hs + jh
                    ab = psA[jh // 4]
                    q = jh // 4
                    P.op('act', lambda e, h=h, ab=ab, q=q, jh=jh: e.activation(out=e4[q][0:T, (jh % 4) * 128:(jh % 4) * 128 + T],
                                                                             in_=k.ps[ab][0:T, (jh % 4) * 128:(jh % 4) * 128 + T], func=AF.Exp, bias=nacs[0:T, h:h + 1]),
                         r=['ps%d' % ab, 'nacs'], w=['e4%d' % q])
                if gq < 3:
                    stageA(gq + 1)
                for q in range(2):
                    P.op('dve', lambda e, q=q: e.tensor_tensor(
                        out=MT4[q][0:T, :].rearrange("p (j l) -> p j l", l=128)[:, :, 0:T], in0=e4[q][0:T, :].rearrange("p (j l) -> p j l", l=128)[:, :, 0:T],
                        in1=k.ps[2][0:T, 0:T].unsqueeze(1).to_broadcast([T, 4, T]), op=ALU.mult), r=['e4%d' % q, 'ps2'], w=['MT4%d' % q])
                for jh in range(8):
                    q = jh // 4
                    P.op('pe', lambda e, q=q, jh=jh: e.matmul(k.ps[0][0:T, jh * 64:(jh + 1) * 64], lhsT=MT4[q][0:T, (jh % 4) * 128:(jh % 4) * 128 + T],
                                                             rhs=xdt[0:T, jh * 64:(jh + 1) * 64], start=False, stop=(jh == 7)), r=['MT4%d' % q, 'xdt'], w=['ps0'])
                def st_bf16(b, gq):
                    sl_ = stctr[0] % 2
                    stctr[0] += 1
                    P.op('act', lambda e, b=b, gq=gq, sl_=sl_: e.activation(out=STb[sl_][:, :], in_=ST[b][:, gq, :], func=AF.Copy),
                         r=['ST%d' % b], w=['STbs%d' % sl_])
                    return STb[sl_], 'STbs%d' % sl_
                if prompt:
                    stb, stk = st_bf16(0, gq)
                    P.op('pe', lambda e, gq=gq, c0=c0, stb=stb: e.matmul(k.ps[1][0:T, :], lhsT=CT[:, gq, c0:c0 + T], rhs=stb[:, :], start=True, stop=True),
                         r=['CT', stk], w=['ps1'])
                else:
                    P.op('dve', lambda e, gq=gq: e.tensor_tensor(out=CTm[:, :, :], in0=CT[:, gq, 0:NSTOK].unsqueeze(1).to_broadcast([128, NSS, NSTOK]),
                                                                 in1=cst['seqcol'][:, :, :], op=ALU.mult), r=['CT', 'cst'], w=['CTm'])
                    for b in range(NSS):
                        stb, stk = st_bf16(b, gq)
                        P.op('pe', lambda e, gq=gq, b=b, stb=stb: e.matmul(k.ps[1][0:T, :], lhsT=CTm[:, b, :], rhs=stb[:, :], start=(b == 0), stop=(b == NSS - 1)),
                             r=['CTm', stk], w=['ps1'])
                P.op('dve', lambda e, hs=hs: e.tensor_tensor(out=t1[0:T, :].rearrange("p (h d) -> p h d", h=8), in0=k.ps[1][0:T, :].rearrange("p (h d) -> p h d", h=8),
                                                            in1=Et[0:T, hs:hs + 8].unsqueeze(2).to_broadcast([T, 8, 64]), op=ALU.mult), r=['ps1', 'Et'], w=['t1'])
                P.op('dve', lambda e: e.tensor_tensor(out=t1[0:T, :], in0=t1[0:T, :], in1=k.ps[0][0:T, :], op=ALU.add), r=['t1', 'ps0'], w=['t1'])
                P.op('pool', lambda e, ti=ti, gq=gq: e.tensor_tensor(out=gg[0:T, :], in0=t1[0:T, :], in1=zs[0:T, ti, gq * 512:(gq + 1) * 512], op=ALU.mult),
                     r=['t1', 'zs'], w=['gg'])
                s_ = small_slot(k)
                sk = 'sm%d' % s_
                ss = k.small[0:T, s_:s_ + 1]
                rs = k.small[0:T, s_ + 1:s_ + 2]
                P.op('act', lambda e, ss=ss: e.activation(out=junk[0][0:T, 0:512], in_=gg[0:T, :], func=AF.Square, accum_out=ss), r=['gg'], w=['junk0', sk])
                P.op('act', lambda e, ss=ss, rs=rs: e.activation(out=rs, in_=ss, func=AF.Ln, scale=1.0 / 512, bias=EPS), r=[sk], w=[sk + 'r'])
                P.op('act', lambda e, rs=rs: e.activation(out=rs, in_=rs, func=AF.Exp, scale=-0.5), r=[sk + 'r'], w=[sk + 'r'])
                P.op('dve', lambda e, rs=rs, gq=gq: e.scalar_tensor_tensor(out=gn[0:T, :], in0=gg[0:T, :], scalar=rs, in1=gw_bc[0:T, gq * 512:(gq + 1) * 512],
                                                                          op0=ALU.mult, op1=ALU.mult), r=['gg', sk + 'r', 'gw_bc'], w=['gn'])
                pb = 6 + gq % 2
                for q in range(4):
                    P.op('pe', lambda e, q=q, pb=pb: e.transpose(k.psb[pb][:, q * 128:q * 128 + T], gn[0:T, q * 128:(q + 1) * 128], k.ident[0:T, 0:T]),
                         r=['gn', 'ident'], w=['ps%d' % pb])
                P.op('act', lambda e, gq=gq, c0=c0, pb=pb: e.activation(out=gnT[:, gq * 4:gq * 4 + 4, c0:c0 + T],
                                                                      in_=k.psb[pb][:, 0:512].rearrange("p (q t) -> p q t", q=4)[:, :, 0:T], func=AF.Copy),
                     r=['ps%d' % pb], w=['gnT'])
                for b in range(nseq):
                    if prompt:
                        lhs = B_tok[0:T, ti, gq * 128:(gq + 1) * 128]
                        lkey = 'B_tok'
                    else:
                        P.op('dve', lambda e, b=b, gq=gq: e.tensor_scalar(out=Bm[0:T, b, :], in0=B_tok[0:T, 0, gq * 128:(gq + 1) * 128],
                                                                         scalar1=cst['seqrow'][0:T, b:b + 1], scalar2=None, op0=ALU.mult),
                             r=['B_tok', 'cst'], w=['Bm%d' % b])
                        lhs = Bm[0:T, b, :]
                        lkey = 'Bm%d' % b
                    P.op('pe', lambda e, lhs=lhs: e.matmul(k.ps[5][:, :], lhsT=lhs, rhs=xw[0:T, :], start=True, stop=True), r=[lkey, 'xw'], w=['ps5'])
                    stv = ST[b][:, gq, :].rearrange("p (h d) -> p h d", h=8)
                    P.op('dve', lambda e, stv=stv, b=b, hs=hs: e.tensor_tensor(out=stv, in0=stv, in1=Dtot[:, b, hs:hs + 8].unsqueeze(2).to_broadcast([128, 8, 64]), op=ALU.mult),
                         r=['ST%d' % b, 'Dtot'], w=['ST%d' % b])
                    P.op('dve', lambda e, b=b, gq=gq: e.tensor_tensor(out=ST[b][:, gq, :], in0=ST[b][:, gq, :], in1=k.ps[5][:, :], op=ALU.add),
                         r=['ST%d' % b, 'ps5'], w=['ST%d' % b])
        proj_out(k, tiles, gnT, 'gnT', wov, 16, mtmp, ssh, junk, wtag=('oddo', j))
        post_norm_add(k, tiles, 1, mtmp, ssh, tmp_t)
    with nc.allow_non_contiguous_dma(reason="conv state relayout"):
        for b in range(nseq):
            for r_ in range(3):
                dst = (conv_out[j, r_] if prompt else conv_out[j, b, r_]).rearrange("(ft p) -> p ft", p=128)
                P.op('sp', lambda e, dst=dst, b=b, r_=r_: e.dma_start(out=dst, in_=xlast[:, :, b, r_], allow_slow_non_contiguous=True), r=['xlast'], dsem='xlast', chain=False)
    for b in range(nseq):
        dv = (ssm_out[j] if prompt else ssm_out[j, b]).rearrange("h p n -> (h p) n")
        for gq in range(4):
            pb = 2 + gq % 2
            for qq in range(4):
                P.op('pe', lambda e, b=b, gq=gq, qq=qq, pb=pb: e.transpose(k.ps[pb][:, qq * 128:(qq + 1) * 128], ST[b][:, gq, qq * 128:(qq + 1) * 128], k.identF[:]),
                     r=['ST%d' % b, 'identF'], w=['ps%d' % pb])
            P.op('dve', lambda e, pb=pb: e.tensor_copy(sout[:].rearrange("p q n -> p (q n)"), k.ps[pb][:, :]), r=['ps%d' % pb], w=['sout'])
            P.op('sp', lambda e, dv=dv, gq=gq: e.dma_start(out=dv[gq * 512:(gq + 1) * 512, :].rearrange("(q r) n -> r q n", r=128), in_=sout[:]),
                 r=['sout'], dsem='sout')


TWO_PI = 6.283185307179586
GELU_C = 0.044715
GELU_S = 1.5957691216057308
I32 = mybir.dt.int32


def s5_phase(k, l, j, grp):
    P = k.P
    P.phase_begin()
    prompt = grp == 'p'
    nseq = 1 if prompt else NSS
    Lseq = SEQ if prompt else LS
    NTOK = nseq * Lseq
    Lh = 512 if prompt else LS
    nsegs = Lseq // Lh
    SC = nseq * Lh if not prompt else Lh
    nstep = NTOK // SC
    o_bT = P.sb([128, 4, NTOK], BF16, 'o_bT')
    k.o_bT = o_bT
    k.o_bT_bytes = P.cur - P.pers
    phase_common(k, l, (0,), nwbuf=1)
    ncolb = 512 if prompt else NSTOK
    hT = P.sb([128, 8, ncolb], BF16, 'hT')
    uT = P.sb([128, 4, NTOK], BF16, 'uT')
    junk = [P.sb([128, D], BF16, 'junk0')] * 2
    hn = [P.sb([128, D], BF16, 'hn0')] * 2
    LR = P.sb([128, 32], F32, 'LR')
    LI = P.sb([128, 32], F32, 'LI')
    DT = P.sb([128, 32], F32, 'DT')
    RHO = P.sb([128, 32], F32, 'RHO')
    TH = P.sb([128, 32], F32, 'TH')
    K1 = P.sb([128, 32], F32, 'K1')
    K2 = P.sb([128, 32], F32, 'K2')
    K1s = P.sb([128, 32], F32, 'K1s')
    K2s = P.sb([128, 32], F32, 'K2s')
    pt = [P.sb([128, 32], F32, 'pt%d' % i) for i in range(8)]
    pti = P.sb([128, 32], I32, 'pti')
    X1 = P.sb([128, 32, 16], F32, 'X1')
    X2 = P.sb([128, 32, 16], F32, 'X2')
    BB = P.sb([128, 32, 16], F32, 'BB')
    BBs = P.sb([128, 32, 16], F32, 'BBs')
    Bblk = P.sb([128, 32, 128], BF16, 'Bblk')
    Bblks = P.sb([128, 32, 128], BF16, 'Bblks')
    Cnat = P.sb([128, 4, 2, 64], F32, 'Cnat')
    CC = P.sb([128, 4, 128], F32, 'CC')
    Cblk = P.sb([128, 32, 128], BF16, 'Cblk')
    Dv = P.sb([128, 4], F32, 'Dv')
    bglu = P.sb([128, 4], F32, 'bglu')
    H0 = P.sb([128, nseq, 32], F32, 'H0')
    Hlast = P.sb([128, nseq, 32], F32, 'Hlast')
    carry = P.sb([128, 2], F32, 'carry')
    COSs = [P.sb([128, Lh], F32, 'COS%d' % i) for i in range(2)]
    SINs = [P.sb([128, Lh], F32, 'SIN%d' % i) for i in range(2)]
    RHs = [P.sb([128, Lh], F32, 'RH%d' % i) for i in range(2)]
    phi = P.sb([128, Lh], F32, 'phi')
    qi = P.sb([128, Lh], I32, 'qi')
    qf = P.sb([128, Lh], F32, 'qf')
    s2 = P.sb([128, Lh], F32, 's2')
    s4 = P.sb([128, Lh], F32, 's4')
    a1 = X1[:].rearrange("p g c -> p (g c)")
    a2 = X2[:].rearrange("p g c -> p (g c)")
    Sp = P.sb([128, SC], F32, 'Sp')
    gb = P.sb([128, SC], BF16, 'gb')
    Hb = P.sb([128, SC], BF16, 'Hb')
    gl = [P.sb([128, ncolb], F32, 'gl%d' % i) for i in range(2)]
    swp = k.swp

    for half in range(2):
        rows = slice(64 * half, 64 * half + 64)
        P.op('sp', lambda e, rows=rows: e.dma_start(out=LR[rows, :], in_=k.s5_lambda_re[j].rearrange("g p -> p g"), allow_slow_non_contiguous=True), w=['LR'], dsem='s5p', chain=False)
        P.op('sp', lambda e, rows=rows: e.dma_start(out=LI[rows, :], in_=k.s5_lambda_im[j].rearrange("g p -> p g"), allow_slow_non_contiguous=True), w=['LI'], dsem='s5p', chain=False)
        bre = k.s5_b_re[j].rearrange("g p c -> p g c")
        bim = k.s5_b_im[j].rearrange("g p c -> p g c")
        P.op('sp', lambda e, rows=rows, half=half, bre=bre, bim=bim: e.dma_start(out=X1[rows, :, :], in_=(bre if half == 0 else bim)), w=['X1'], dsem='s5p', chain=False)
        P.op('sp', lambda e, rows=rows, half=half, bre=bre, bim=bim: e.dma_start(out=X2[rows, :, :], in_=(bim if half == 0 else bre)), w=['X2'], dsem='s5p', chain=False)
    P.op('sp', lambda e: e.dma_start(out=DT[:], in_=k.s5_log_dt[j:j + 1, :].to_broadcast([128, 32])), w=['DT'], dsem='s5p', chain=False)
    P.op('sp', lambda e: e.dma_start(out=Cnat[:, :, 0, :], in_=k.s5_c_re[j].rearrange("g c p -> (g c) p").rearrange("(m r) p -> r m p", r=128)), w=['Cnat'], dsem='s5p', chain=False)
    P.op('sp', lambda e: e.dma_start(out=Cnat[:, :, 1, :], in_=k.s5_c_im[j].rearrange("g c p -> (g c) p").rearrange("(m r) p -> r m p", r=128)), w=['Cnat'], dsem='s5p', chain=False)
    P.op('sp', lambda e: e.dma_start(out=Dv[:], in_=k.s5_d[j].rearrange("g c -> (g c)").rearrange("(m r) -> r m", r=128), allow_slow_non_contiguous=True), w=['Dv'], dsem='s5p', chain=False)
    P.op('sp', lambda e: e.dma_start(out=bglu[:], in_=k.s5_b_glu[j].rearrange("(m r) -> r m", r=128), allow_slow_non_contiguous=True), w=['bglu'], dsem='s5p', chain=False)
    if prompt:
        P.op('dve', lambda e: e.memset(H0[:], 0.0), w=['H0'])
    else:
        for b in range(NSS):
            for ri in range(2):
                P.op('sp', lambda e, b=b, ri=ri: e.dma_start(out=H0[64 * ri:64 * ri + 64, b, :], in_=k.state_s5[j, b].rearrange("g p r -> p g r")[:, :, ri],
                                                             allow_slow_non_contiguous=True), w=['H0'], dsem='s5p', chain=False)

    def reduce_sincos(eng_note, ang, n, qi_, qf_, r_, s2_, s4_, cos_out, sin_out, keys):
        P.op('dve', lambda e: e.tensor_scalar(out=qi_, in0=ang, scalar1=1.0 / TWO_PI, scalar2=None, op0=ALU.mult), r=keys['ang'], w=keys['qi'])
        P.op('dve', lambda e: e.tensor_copy(qf_, qi_), r=keys['qi'], w=keys['qf'])
        P.op('dve', lambda e: e.scalar_tensor_tensor(out=r_, in0=qf_, scalar=-TWO_PI, in1=ang, op0=ALU.mult, op1=ALU.add), r=keys['qf'] + keys['ang'], w=keys['r'])
        P.op('act', lambda e: e.activation(out=s4_, in_=r_, func=AF.Sin, scale=0.25), r=keys['r'], w=keys['s4'])
        P.op('act', lambda e: e.activation(out=s2_, in_=r_, func=AF.Sin, scale=0.5), r=keys['r'], w=keys['s2'])
        P.op('pool', lambda e: e.tensor_tensor(out=s4_, in0=s4_, in1=s4_, op=ALU.mult), r=keys['s4'], w=keys['s4'])
        P.op('pool', lambda e: e.tensor_scalar(out=s4_, in0=s4_, scalar1=-2.0, scalar2=1.0, op0=ALU.mult, op1=ALU.add), r=keys['s4'], w=keys['s4'])
        P.op('dve', lambda e: e.scalar_tensor_tensor(out=sin_out, in0=s2_, scalar=2.0, in1=s4_, op0=ALU.mult, op1=ALU.mult), r=keys['s2'] + keys['s4'], w=keys['sin'])
        P.op('pool', lambda e: e.tensor_tensor(out=s2_, in0=s2_, in1=s2_, op=ALU.mult), r=keys['s2'] + keys['sin'], w=keys['s2'])
        P.op('pool', lambda e: e.tensor_scalar(out=cos_out, in0=s2_, scalar1=-2.0, scalar2=1.0, op0=ALU.mult, op1=ALU.add), r=keys['s2'], w=keys['cos'])

    P.op('act', lambda e: e.activation(out=DT[:], in_=DT[:], func=AF.Exp), r=['DT'], w=['DT'])
    P.op('dve', lambda e: e.tensor_tensor(out=pt[0][:], in0=LR[:], in1=DT[:], op=ALU.mult), r=['LR', 'DT'], w=['pt0'])
    P.op('dve', lambda e: e.tensor_tensor(out=TH[:], in0=LI[:], in1=DT[:], op=ALU.mult), r=['LI', 'DT'], w=['TH'])
    P.op('act', lambda e: e.activation(out=RHO[:], in_=pt[0][:], func=AF.Exp), r=['pt0'], w=['RHO'])
    kk = {'ang': ['TH'], 'qi': ['pti'], 'qf': ['pt1'], 'r': ['pt2'], 's4': ['pt3'], 's2': ['pt4'], 'sin': ['pt5'], 'cos': ['pt6']}
    reduce_sincos('p', TH[:], 32, pti[:], pt[1][:], pt[2][:], pt[4][:], pt[3][:], pt[6][:], pt[5][:], kk)
    P.op('dve', lambda e: e.tensor_tensor(out=pt[0][:], in0=RHO[:], in1=pt[6][:], op=ALU.mult), r=['RHO', 'pt6'], w=['pt0'])
    P.op('dve', lambda e: e.tensor_scalar(out=pt[0][:], in0=pt[0][:], scalar1=-1.0, scalar2=None, op0=ALU.add), r=['pt0'], w=['pt0'])
    P.op('dve', lambda e: e.tensor_tensor(out=pt[1][:], in0=RHO[:], in1=pt[5][:], op=ALU.mult), r=['RHO', 'pt5', 'pt1'], w=['pt1'])
    P.op('dve', lambda e: e.tensor_tensor(out=pt[2][:], in0=LR[:], in1=LR[:], op=ALU.mult), r=['LR', 'pt2'], w=['pt2'])
    P.op('dve', lambda e: e.tensor_tensor(out=pt[3][:], in0=LI[:], in1=LI[:], op=ALU.mult), r=['LI', 'pt3'], w=['pt3'])
    P.op('dve', lambda e: e.tensor_tensor(out=pt[2][:], in0=pt[2][:], in1=pt[3][:], op=ALU.add), r=['pt2', 'pt3'], w=['pt2'])
    P.op('dve', lambda e: e.reciprocal(out=pt[2][:], in_=pt[2][:]), r=['pt2'], w=['pt2'])
    P.op('dve', lambda e: e.tensor_tensor(out=pt[3][:], in0=pt[0][:], in1=LR[:], op=ALU.mult), r=['pt0', 'LR', 'pt3'], w=['pt3'])
    P.op('dve', lambda e: e.tensor_tensor(out=pt[4][:], in0=pt[1][:], in1=LI[:], op=ALU.mult), r=['pt1', 'LI', 'pt4'], w=['pt4'])
    P.op('dve', lambda e: e.tensor_tensor(out=pt[3][:], in0=pt[3][:], in1=pt[4][:], op=ALU.add), r=['pt3', 'pt4'], w=['pt3'])
    P.op('dve', lambda e: e.tensor_tensor(out=K1[:], in0=pt[3][:], in1=pt[2][:], op=ALU.mult), r=['pt3', 'pt2'], w=['K1'])
    P.op('dve', lambda e: e.tensor_tensor(out=pt[3][:], in0=pt[1][:], in1=LR[:], op=ALU.mult), r=['pt1', 'LR', 'pt3'], w=['pt3'])
    P.op('dve', lambda e: e.tensor_tensor(out=pt[4][:], in0=pt[0][:], in1=LI[:], op=ALU.mult), r=['pt0', 'LI', 'pt4'], w=['pt4'])
    P.op('dve', lambda e: e.tensor_tensor(out=pt[3][:], in0=pt[3][:], in1=pt[4][:], op=ALU.subtract), r=['pt3', 'pt4'], w=['pt3'])
    P.op('dve', lambda e: e.tensor_tensor(out=pt[7][:], in0=pt[3][:], in1=pt[2][:], op=ALU.mult), r=['pt3', 'pt2'], w=['pt7'])
    P.op('dve', lambda e: e.tensor_scalar(out=K2[0:64, :], in0=pt[7][0:64, :], scalar1=-1.0, scalar2=None, op0=ALU.mult), r=['pt7'], w=['K2'])
    P.op('dve', lambda e: e.tensor_copy(K2[64:128, :], pt[7][64:128, :]), r=['pt7'], w=['K2'])
    P.op('dve', lambda e: e.tensor_scalar(out=K1s[0:64, :], in0=K1[0:64, :], scalar1=-1.0, scalar2=None, op0=ALU.mult), r=['K1'], w=['K1s'])
    P.op('dve', lambda e: e.tensor_copy(K1s[64:128, :], K1[64:128, :]), r=['K1'], w=['K1s'])
    P.op('dve', lambda e: e.tensor_scalar(out=K2s[:], in0=pt[7][:], scalar1=-1.0, scalar2=None, op0=ALU.mult), r=['pt7'], w=['K2s'])

    def bc(t):
        return t[:].unsqueeze(2).to_broadcast([128, 32, 16])
    P.op('dve', lambda e: e.tensor_tensor(out=BB[:], in0=X1[:], in1=bc(K1), op=ALU.mult), r=['X1', 'K1'], w=['BB'])
    P.op('dve', lambda e: e.tensor_tensor(out=BBs[:], in0=X2[:], in1=bc(K2), op=ALU.mult), r=['X2', 'K2'], w=['BBs'])
    P.op('dve', lambda e: e.tensor_tensor(out=BB[:], in0=BB[:], in1=BBs[:], op=ALU.add), r=['BB', 'BBs'], w=['BB'])
    P.op('dve', lambda e: e.tensor_tensor(out=BBs[:], in0=X2[:], in1=bc(K1s), op=ALU.mult), r=['X2', 'K1s', 'BBs'], w=['BBs'])
    P.op('dve', lambda e: e.tensor_tensor(out=X2[:], in0=X1[:], in1=bc(K2s), op=ALU.mult), r=['X1', 'K2s', 'X2'], w=['X2'])
    P.op('dve', lambda e: e.tensor_tensor(out=BBs[:], in0=BBs[:], in1=X2[:], op=ALU.add), r=['BBs', 'X2'], w=['BBs'])
    P.op('pool', lambda e: e.memset(Cblk[:], 0.0), w=['Cblk'])
    for src, dstt, dkey in ((BB, Bblk, 'Bblk'), (BBs, Bblks, 'Bblks')):
        for m in range(4):
            pb = m % 2
            P.op('pe', lambda e, src=src, m=m, pb=pb: e.transpose(k.ps[pb][:, 0:128], src[:, 8 * m:8 * m + 8, :].rearrange("p g c -> p (g c)"), k.identF[:]),
                 r=[('BB' if src is BB else 'BBs'), 'identF'], w=['ps%d' % pb])
            for gl_ in range(8):
                P.op('act', lambda e, dstt=dstt, m=m, gl_=gl_, pb=pb: e.activation(out=dstt[:, 8 * m + gl_, :], in_=k.ps[pb][:, 0:128], func=AF.Copy,
                                                                                  scale=k.rowmask[:, gl_:gl_ + 1]), r=['ps%d' % pb, 'rowmask'], w=[dkey])
    for m in range(4):
        pb = 2 + m % 2
        P.op('pe', lambda e, m=m, pb=pb: e.transpose(k.ps[pb][:, 0:128], Cnat[:, m, :, :].rearrange("p r q -> p (r q)"), k.identF[:]), r=['Cnat', 'identF'], w=['ps%d' % pb])
        P.op('act', lambda e, m=m, pb=pb: e.activation(out=CC[0:64, m, :], in_=k.ps[pb][0:64, 0:128], func=AF.Copy), r=['ps%d' % pb], w=['CC'])
        P.op('act', lambda e, m=m, pb=pb: e.activation(out=CC[64:128, m, :], in_=k.ps[pb][64:128, 0:128], func=AF.Copy, scale=-1.0), r=['ps%d' % pb], w=['CC'])
        for gl_ in range(8):
            P.op('dve', lambda e, m=m, gl_=gl_: e.tensor_copy(Cblk[:, 8 * m + gl_, 16 * gl_:16 * gl_ + 16], CC[:, m, 16 * gl_:16 * gl_ + 16]), r=['CC'], w=['Cblk'])
    wv_in = k.w_in_even[j].rearrange("(kc p) n -> p kc n", p=128)
    nblk = NTOK // ncolb
    for bi in range(nblk):
        tiles = k.ptiles[4 * bi:4 * bi + 4] if prompt else [k.stile]
        norm_transpose(k, tiles, 0, hT, 'hT', junk, hn)
        wb, wkey = next_wbuf(k)
        wv = wb[:].rearrange("p (kc n) -> p kc n", kc=8)
        wload(k, P, ('evu', j), wv, wkey, wv_in[:, :, 1536:2048])
        for m in range(4):
            pb = m % 2
            for kc in range(8):
                P.op('pe', lambda e, pb=pb, kc=kc, m=m, wv=wv: e.matmul(k.ps[pb][:, 0:ncolb], lhsT=wv[:, kc, m * 128:(m + 1) * 128], rhs=hT[:, kc, 0:ncolb],
                                                                       start=(kc == 0), stop=(kc == 7)), r=[wkey, 'hT'], w=['ps%d' % pb])
            if m % 2 == 0:
                P.op('act', lambda e, pb=pb, m=m, bi=bi: e.activation(out=uT[:, m, bi * ncolb:(bi + 1) * ncolb], in_=k.ps[pb][:, 0:ncolb], func=AF.Copy), r=['ps%d' % pb], w=['uT'])
            else:
                P.op('dve', lambda e, pb=pb, m=m, bi=bi: e.tensor_copy(uT[:, m, bi * ncolb:(bi + 1) * ncolb], k.ps[pb][:, 0:ncolb]), r=['ps%d' % pb], w=['uT'])
    psy = [4, 5, 6, 7]

    def tables(g):
        tb = g % 2
        COS, SIN, RH = COSs[tb], SINs[tb], RHs[tb]
        ck, sk_, rk = 'COS%d' % tb, 'SIN%d' % tb, 'RH%d' % tb
        P.op('dve', lambda e, g=g: e.tensor_scalar(out=phi[:], in0=k.iota1[:, 0:Lh], scalar1=TH[:, g:g + 1], scalar2=None, op0=ALU.mult), r=['iota1', 'TH'], w=['phi'])
        kk = {'ang': ['phi'], 'qi': ['qi'], 'qf': ['qf'], 'r': ['qf'], 's4': ['s4'], 's2': ['s2'], 'sin': [sk_], 'cos': [ck]}
        reduce_sincos('g', phi[:], Lh, qi[:], qf[:], qf[:], s2[:], s4[:], COS[:], SIN[:], kk)
        P.op('pool', lambda e, g=g, RH=RH: e.tensor_scalar(out=RH[:], in0=k.iota1[:, 0:Lh], scalar1=0.0, scalar2=RHO[:, g:g + 1], op0=ALU.mult, op1=ALU.add), r=['iota1', 'RHO'], w=[rk])
    tables(0)
    for g in range(32):
        m = g // 8
        tb = g % 2
        COS, SIN, RH = COSs[tb], SINs[tb], RHs[tb]
        ck, sk_, rk = 'COS%d' % tb, 'SIN%d' % tb, 'RH%d' % tb
        cosb = COS[:, 0:Lh].unsqueeze(1).to_broadcast([128, SC // Lh, Lh])
        sinb = SIN[:, 0:Lh].unsqueeze(1).to_broadcast([128, SC // Lh, Lh])

        def v3(ap):
            return ap.rearrange("p (b t) -> p b t", t=Lh)

        def s_mm(st, g=g, m=m):
            pa, pb_ = (0, 1) if st % 2 == 0 else (2, 3)
            c0 = st * SC
            P.op('pe', lambda e, c0=c0, pa=pa: e.matmul(k.ps[pa][:, 0:SC], lhsT=Bblk[:, g, :], rhs=uT[:, m, c0:c0 + SC], start=True, stop=True), r=['Bblk', 'uT'], w=['ps%d' % pa])
            P.op('pe', lambda e, c0=c0, pb_=pb_: e.matmul(k.ps[pb_][:, 0:SC], lhsT=Bblks[:, g, :], rhs=uT[:, m, c0:c0 + SC], start=True, stop=True), r=['Bblks', 'uT'], w=['ps%d' % pb_])
        s_mm(0)
        for st in range(nstep):
            pa, pb_ = (0, 1) if st % 2 == 0 else (2, 3)
            P.op('dve', lambda e, pa=pa, cosb=cosb: e.tensor_tensor(out=v3(a1[:, 0:SC]), in0=v3(k.ps[pa][:, 0:SC]), in1=cosb, op=ALU.mult), r=['ps%d' % pa, ck], w=['a1'])
            P.op('dve', lambda e, pb_=pb_, sinb=sinb: e.tensor_tensor(out=v3(a2[:, 0:SC]), in0=v3(k.ps[pb_][:, 0:SC]), in1=sinb, op=ALU.mult), r=['ps%d' % pb_, sk_], w=['a2'])
            P.op('dve', lambda e: e.tensor_tensor(out=Sp[:, 0:SC], in0=a1[:, 0:SC], in1=a2[:, 0:SC], op=ALU.subtract), r=['a1', 'a2'], w=['Sp'])
            if st + 1 < nstep:
                s_mm(st + 1)
            if st == 0 and g + 1 < 32:
                tables(g + 1)
            for b in range(SC // Lh):
                if prompt:
                    init = H0[:, 0, g:g + 1] if st == 0 else carry[:, 0:1]
                    ikey = 'H0' if st == 0 else 'carry'
                else:
                    init = H0[:, b, g:g + 1]
                    ikey = 'H0'
                P.op('dve', lambda e, b=b, init=init, RH=RH: e.tensor_tensor_scan(out=gb[:, b * Lh:(b + 1) * Lh], data0=RH[:, 0:Lh], data1=Sp[:, b * Lh:(b + 1) * Lh],
                                                                           initial=init, op0=ALU.mult, op1=ALU.add), r=[rk, 'Sp', ikey], w=['gb'])
            P.op('pe', lambda e, pa=pa: e.matmul(k.ps[pa][:, 0:SC], lhsT=swp[:, :], rhs=gb[:, 0:SC], start=True, stop=True), r=['gb', 'swp'], w=['ps%d' % pa])
            P.op('pool', lambda e, cosb=cosb: e.tensor_tensor(out=v3(a2[:, 0:SC]), in0=v3(gb[:, 0:SC]), in1=cosb, op=ALU.mult), r=['gb', ck], w=['a2'])
            P.op('dve', lambda e, pa=pa, sinb=sinb: e.tensor_tensor(out=v3(a1[:, 0:SC]), in0=v3(k.ps[pa][:, 0:SC]), in1=sinb, op=ALU.mult), r=['ps%d' % pa, sk_], w=['a1'])
            P.op('dve', lambda e: e.tensor_tensor(out=Hb[:, 0:SC], in0=a1[:, 0:SC], in1=a2[:, 0:SC], op=ALU.add), r=['a1', 'a2'], w=['Hb'])
            if prompt:
                if st < nstep - 1:
                    P.op('dve', lambda e: e.tensor_tensor(out=carry[:, 0:1], in0=a1[:, SC - 1:SC], in1=a2[:, SC - 1:SC], op=ALU.add), r=['a1', 'a2'], w=['carry'])
                else:
                    P.op('dve', lambda e, g=g: e.tensor_tensor(out=Hlast[:, 0, g:g + 1], in0=a1[:, SC - 1:SC], in1=a2[:, SC - 1:SC], op=ALU.add), r=['a1', 'a2'], w=['Hlast'])
            else:
                P.op('dve', lambda e, g=g: e.tensor_tensor(out=Hlast[:, :, g:g + 1], in0=v3(a1[:, 0:SC])[:, :, Lh - 1:Lh], in1=v3(a2[:, 0:SC])[:, :, Lh - 1:Lh], op=ALU.add),
                     r=['a1', 'a2'], w=['Hlast'])
            P.op('pe', lambda e, g=g, st=st: e.matmul(k.ps[psy[st]][:, 0:SC], lhsT=Cblk[:, g, :], rhs=Hb[:, 0:SC], start=(g % 8 == 0), stop=(g % 8 == 7)),
                 r=['Cblk', 'Hb'], w=['ps%d' % psy[st]])
        if g % 8 == 7:
            for st in range(nstep):
                c0 = st * SC
                ga, gbk = gl[0][:, 0:SC], gl[1][:, 0:SC]
                P.op('dve', lambda e, m=m, st=st, c0=c0, ga=ga: e.scalar_tensor_tensor(out=ga, in0=uT[:, m, c0:c0 + SC], scalar=Dv[:, m:m + 1], in1=k.ps[psy[st]][:, 0:SC],
                                                                                      op0=ALU.mult, op1=ALU.add), r=['uT', 'Dv', 'ps%d' % psy[st]], w=['gl0'])
                P.op('pool', lambda e, ga=ga, gbk=gbk: e.tensor_tensor(out=gbk, in0=ga, in1=ga, op=ALU.mult), r=['gl0'], w=['gl1'])
                P.op('pool', lambda e, gbk=gbk: e.tensor_scalar(out=gbk, in0=gbk, scalar1=GELU_C, scalar2=1.0, op0=ALU.mult, op1=ALU.add), r=['gl1'], w=['gl1'])
                P.op('pool', lambda e, ga=ga, gbk=gbk: e.tensor_tensor(out=gbk, in0=gbk, in1=ga, op=ALU.mult), r=['gl1', 'gl0'], w=['gl1'])
                P.op('act', lambda e, gbk=gbk: e.activation(out=gbk, in_=gbk, func=AF.Sigmoid, scale=GELU_S), r=['gl1'], w=['gl1'])
                P.op('dve', lambda e, m=m, c0=c0, ga=ga, gbk=gbk: e.tensor_tensor(out=uT[:, m, c0:c0 + SC], in0=ga, in1=gbk, op=ALU.mult), r=['gl0', 'gl1'], w=['uT'])
    wb, wkey = next_wbuf(k)
    wg = wb[:, 0:2048].rearrange("p (kc n) -> p kc n", kc=4)
    P.op('pool', lambda e: e.dma_start(out=wg, in_=k.s5_w_glu[j].rearrange("(kc p) n -> p kc n", p=128)), w=[wkey], dsem=wkey, hoist=True)
    cnt = 0
    for c0 in range(0, NTOK, ncolb):
        for mo in range(4):
            pb = cnt % 2
            cnt += 1
            for kc in range(4):
                P.op('pe', lambda e, pb=pb, kc=kc, mo=mo, c0=c0: e.matmul(k.ps[pb][:, 0:ncolb], lhsT=wg[:, kc, mo * 128:(mo + 1) * 128], rhs=uT[:, kc, c0:c0 + ncolb],
                                                                        start=(kc == 0), stop=(kc == 3)), r=[wkey, 'uT'], w=['ps%d' % pb])
            gs = gl[pb][:, 0:ncolb]
            P.op('act', lambda e, pb=pb, mo=mo, gs=gs: e.activation(out=gs, in_=k.ps[pb][:, 0:ncolb], func=AF.Sigmoid, bias=bglu[:, mo:mo + 1]), r=['ps%d' % pb, 'bglu'], w=['gl%d' % pb])
            P.op('dve', lambda e, mo=mo, c0=c0, gs=gs: e.tensor_tensor(out=o_bT[:, mo, c0:c0 + ncolb], in0=gs, in1=uT[:, mo, c0:c0 + ncolb], op=ALU.mult), r=['gl%d' % pb, 'uT'], w=['o_bT'])
    s5o = k.s5_p if prompt else k.s5_s
    for b in range(nseq):
        for ri in range(2):
            dst = (s5o[j] if prompt else s5o[j, b]).rearrange("g p r -> p g r")[:, :, ri]
            P.op('sp', lambda e, dst=dst, b=b, ri=ri: e.dma_start(out=dst, in_=Hlast[64 * ri:64 * ri + 64, b, :], allow_slow_non_contiguous=True), r=['Hlast'], dsem='s5o', chain=False)
    return o_bT


def attn_phase_p(k, l, j):
    P = k.P
    P.phase_begin(keep=k.o_bT_bytes)
    o_bT = k.o_bT
    phase_common(k, l, (0, 1), nwbuf=2)
    KT = P.sb([128, 4, SEQ], BF16, 'KT')
    Vb = P.sb([128, NT, 512], BF16, 'Vb')
    QT = P.sb([128, 4, 512], BF16, 'QT')
    oaT = P.sb([128, 4, 512], BF16, 'oaT')
    hT = P.sb([128, 8, 512], BF16, 'hT')
    G = P.sb([128, 2304], BF16, 'G')
    kaug = P.sb([4, 128], BF16, 'kaug')
    qaug = P.sb([4, 512], BF16, 'qaug')
    ones_b = P.sb([128, 128], BF16, 'ones_b')
    Pt = [P.sb([128, 512], BF16, 'Pt%d' % i) for i in range(3)]
    kst = [P.sb([128, 512], F32, 'kst%d' % i) for i in range(2)]
    rb = P.sb([128, 512], F32, 'rb')
    junk = [P.sb([128, D], BF16, 'junk0')] * 2
    hn = [P.sb([128, D], BF16, 'hn0')] * 2
    mtmp = P.sb([128, 4, D], F32, 'mtmp')
    ssh = P.sb([128, 16], F32, 'ssh')
    tmp_t = [P.sb([128, D], F32, 'tmpt0')] * 2
    P.op('pool', lambda e: e.dma_start(out=G[:], in_=k.c_G[:, :]), w=['G'], dsem='G')
    P.op('pool', lambda e: e.dma_start(out=kaug[:], in_=k.c_kaug[:, :]), w=['kaug'], dsem='kaug')
    P.op('pool', lambda e: e.dma_start(out=qaug[:], in_=k.c_qaug[:, :]), w=['qaug'], dsem='qaug')
    P.op('dve', lambda e: e.memset(ones_b[:], 1.0), w=['ones_b'])
    wv_in = k.w_in_even[j].rearrange("(kc p) n -> p kc n", p=128)
    wov = k.w_out_even[j].rearrange("(kc p) n -> p kc n", p=128)
    sctr = 0
    for qc in range(4):
        tiles = k.ptiles[4 * qc:4 * qc + 4]
        norm_transpose(k, tiles, 0, hT, 'hT', junk, hn)
        wb, wkey = next_wbuf(k)
        wv = wb[:].rearrange("p (kc n) -> p kc n", kc=8)
        wload(k, P, ('evqkv', j, 0), wv, wkey, wv_in[:, :, 0:512])
        for m in range(4):
            pb = m % 2
            for kc in range(8):
                P.op('pe', lambda e, pb=pb, kc=kc, m=m, wv=wv: e.matmul(k.ps[pb][:, :], lhsT=wv[:, kc, m * 128:(m + 1) * 128], rhs=hT[:, kc, :], start=(kc == 0), stop=(kc == 7)),
                     r=[wkey, 'hT'], w=['ps%d' % pb])
            P.op('act', lambda e, pb=pb, m=m: e.activation(out=QT[:, m, :], in_=k.ps[pb][:, :], func=AF.Copy, scale=k.qscale[:, m:m + 1]), r=['ps%d' % pb, 'qscale'], w=['QT'])
        wb, wkey = next_wbuf(k)
        wv = wb[:].rearrange("p (kc n) -> p kc n", kc=8)
        wload(k, P, ('evqkv', j, 1), wv, wkey, wv_in[:, :, 512:1024])
        for m in range(4):
            pb = m % 2
            for kc in range(8):
                P.op('pe', lambda e, pb=pb, kc=kc, m=m, wv=wv: e.matmul(k.ps[pb][:, :], lhsT=wv[:, kc, m * 128:(m + 1) * 128], rhs=hT[:, kc, :], start=(kc == 0), stop=(kc == 7)),
                     r=[wkey, 'hT'], w=['ps%d' % pb])
            P.op('dve', lambda e, pb=pb, m=m, qc=qc: e.tensor_copy(KT[:, m, qc * 512:(qc + 1) * 512], k.ps[pb][:, :]), r=['ps%d' % pb], w=['KT'])
        for ti in range(4):
            pb = 2 + ti % 2
            for kc in range(8):
                P.op('pe', lambda e, pb=pb, kc=kc, ti=ti, wv=wv: e.matmul(k.ps[pb][:, :], lhsT=hT[:, kc, ti * 128:(ti + 1) * 128], rhs=wv[:, kc, :], start=(kc == 0), stop=(kc == 7)),
                     r=[wkey, 'hT'], w=['ps%d' % pb])
            ks = kst[ti % 2]
            P.op('dve', lambda e, pb=pb, ks=ks: e.tensor_copy(ks[:, :], k.ps[pb][:, :]), r=['ps%d' % pb], w=['kst%d' % (ti % 2)])
            row0 = (4 * qc + ti) * 128
            P.op('sp', lambda e, ks=ks, row0=row0: e.dma_start(out=k.k_p[j, row0:row0 + 128, :], in_=ks[:, :]), r=['kst%d' % (ti % 2)], dsem='kst%d' % (ti % 2))
        wb, wkey = next_wbuf(k)
        wv = wb[:].rearrange("p (kc n) -> p kc n", kc=8)
        wload(k, P, ('evqkv', j, 2), wv, wkey, wv_in[:, :, 1024:1536])
        for ti in range(4):
            pb = 2 + ti % 2
            for kc in range(8):
                P.op('pe', lambda e, pb=pb, kc=kc, ti=ti, wv=wv: e.matmul(k.ps[pb][:, :], lhsT=hT[:, kc, ti * 128:(ti + 1) * 128], rhs=wv[:, kc, :], start=(kc == 0), stop=(kc == 7)),
                     r=[wkey, 'hT'], w=['ps%d' % pb])
            ks = kst[ti % 2]
            P.op('dve', lambda e, pb=pb, ks=ks: e.tensor_copy(ks[:, :], k.ps[pb][:, :]), r=['ps%d' % pb], w=['kst%d' % (ti % 2)])
            P.op('act', lambda e, pb=pb, ti=ti, qc=qc: e.activation(out=Vb[:, 4 * qc + ti, :], in_=k.ps[pb][:, :], func=AF.Copy), r=['ps%d' % pb], w=['Vb'])
            row0 = (4 * qc + ti) * 128
            P.op('sp', lambda e, ks=ks, row0=row0: e.dma_start(out=k.v_p[j, row0:row0 + 128, :], in_=ks[:, :]), r=['kst%d' % (ti % 2)], dsem='kst%d' % (ti % 2))
        nj = 4 * qc + 4
        items = [(h, jt) for h in range(8) for jt in range(nj)]

        def geom(h, jt):
            t0 = max(512 * qc, 128 * jt)
            off = t0 - 512 * qc
            return off, 512 - off, 128 * jt - 512 * qc, t0 - 128 * jt + 128

        def scores(idx):
            h, jt = items[idx]
            m, hh = h // 2, h % 2
            rows = slice(64 * hh, 64 * hh + 64)
            off, N, cj, x0 = geom(h, jt)
            sb = idx % 3
            P.op('pe', lambda e, sb=sb, m=m, rows=rows, jt=jt, off=off, N=N: e.matmul(
                k.ps[sb][:, 0:N], lhsT=KT[rows, m, jt * 128:(jt + 1) * 128], rhs=QT[rows, m, off:512], start=True, stop=False),
                r=['KT', 'QT'], w=['ps%d' % sb])
            P.op('pe', lambda e, sb=sb, off=off, N=N: e.matmul(k.ps[sb][:, 0:N], lhsT=kaug[0:4, :], rhs=qaug[0:4, off:512], start=False, stop=True),
                 r=['kaug', 'qaug'], w=['ps%d' % sb])
        scores(0)
        for idx, (h, jt) in enumerate(items):
            if idx + 1 < len(items):
                scores(idx + 1)
            m, hh = h // 2, h % 2
            rows = slice(64 * hh, 64 * hh + 64)
            slope = 2.0 ** (-(h + 1))
            pv = 3 + hh
            pd = 5 + hh
            off, N, cj, x0 = geom(h, jt)
            sb = idx % 3
            pt = Pt[sb]
            P.op('act', lambda e, sb=sb, pt=pt, N=N, slope=slope, cj=cj: e.activation(out=pt[:, 0:N], in_=k.ps[sb][:, 0:N], func=AF.Exp, scale=slope, bias=float(slope * cj)),
                 r=['ps%d' % sb], w=['Pt%d' % sb])
            P.op('pool', lambda e, pt=pt, N=N, x0=x0: e.tensor_tensor(out=pt[:, 0:N], in0=pt[:, 0:N], in1=G[:, x0:x0 + N], op=ALU.mult), r=['Pt%d' % sb, 'G'], w=['Pt%d' % sb])
            f0, f1 = (jt == 0), (jt == nj - 1)
            P.op('pe', lambda e, pv=pv, pt=pt, jt=jt, m=m, off=off, N=N, f0=f0, f1=f1: e.matmul(k.ps[pv][:, off:512], lhsT=Vb[:, jt, m * 128:(m + 1) * 128], rhs=pt[:, 0:N],
                                                                                               start=f0, stop=f1), r=['Vb', 'Pt%d' % sb], w=['ps%d' % pv])
            P.op('pe', lambda e, pd=pd, pt=pt, off=off, N=N, f0=f0, f1=f1: e.matmul(k.ps[pd][:, off:512], lhsT=ones_b[:, :], rhs=pt[:, 0:N], start=f0, stop=f1),
                 r=['ones_b', 'Pt%d' % sb], w=['ps%d' % pd])
            if jt == nj - 1:
                P.op('dve', lambda e, pd=pd, rows=rows: e.reciprocal(out=rb[rows, :], in_=k.ps[pd][rows, :]), r=['ps%d' % pd], w=['rb%d' % hh])
                P.op('dve', lambda e, pv=pv, rows=rows, m=m: e.tensor_tensor(out=oaT[rows, m, :], in0=k.ps[pv][rows, :], in1=rb[rows, :], op=ALU.mult),
                     r=['ps%d' % pv, 'rb%d' % hh], w=['oaT'])

        def src(kc, col, np_, qc=qc):
            if kc < 4:
                return oaT[:, kc, col:col + np_], 'oaT'
            return o_bT[:, kc - 4, qc * 512 + col:qc * 512 + col + np_], 'o_bT'
        proj_out(k, tiles, src, None, wov, 8, mtmp, ssh, junk, wtag=('evo', j))
        post_norm_add(k, tiles, 1, mtmp, ssh, tmp_t)


def attn_phase_s(k, l, j):
    P = k.P
    P.phase_begin(keep=k.o_bT_bytes)
    o_bT = k.o_bT
    phase_common(k, l, (0, 1), nwbuf=2)
    NK = 17
    hT = P.sb([128, 8, NSTOK], BF16, 'hT')
    QT = P.sb([128, 4, NSTOK], BF16, 'QT')
    KTn = P.sb([128, 4, NSTOK], BF16, 'KTn')
    Vn = P.sb([128, 512], BF16, 'Vn')
    kc_nat = P.sb([128, 16, 512], BF16, 'kc_nat')
    Vc = P.sb([128, 16, 512], BF16, 'Vc')
    KTb = P.sb([128, 4, 2048], BF16, 'KTb')
    oaT = P.sb([128, 4, NSTOK], BF16, 'oaT')
    Gs = P.sb([128, NK, 64], BF16, 'Gs')
    Gsn = P.sb([128, NSS, 64], BF16, 'Gsn')
    kaug = P.sb([4, NK, 128], BF16, 'kaug')
    qaug = P.sb([4, 64], BF16, 'qaug')
    ones_b = P.sb([128, 128], BF16, 'ones_b')
    Pall = P.sb([128, NK, 64], BF16, 'Pall')
    kst = [P.sb([128, 512], F32, 'kst%d' % i) for i in range(2)]
    rb = P.sb([128, 64], F32, 'rb')
    junk = [P.sb([128, D], BF16, 'junk0')] * 2
    hn = [P.sb([128, D], BF16, 'hn0')] * 2
    mtmp = P.sb([128, 1, D], F32, 'mtmp')
    ssh = P.sb([128, 16], F32, 'ssh')
    tmp_t = [P.sb([128, D], F32, 'tmpt0')] * 2
    P.op('pool', lambda e: e.dma_start(out=Gs[:], in_=k.c_Gs[:, :, :]), w=['Gs'], dsem='G')
    P.op('pool', lambda e: e.dma_start(out=Gsn[0:NSTOK, :, :], in_=k.c_Gsn[:, :, :]), w=['Gsn'], dsem='G')
    P.op('pool', lambda e: e.dma_start(out=kaug[:], in_=k.c_kaug_s[:, :, :]), w=['kaug'], dsem='kaug')
    P.op('pool', lambda e: e.dma_start(out=qaug[:], in_=k.c_qaug_s[:, :]), w=['qaug'], dsem='qaug')
    P.op('dve', lambda e: e.memset(ones_b[:], 1.0), w=['ones_b'])
    wv_in = k.w_in_even[j].rearrange("(kc p) n -> p kc n", p=128)
    wov = k.w_out_even[j].rearrange("(kc p) n -> p kc n", p=128)
    tiles = [k.stile]
    T = NSTOK
    norm_transpose(k, tiles, 0, hT, 'hT', junk, hn)
    for which in range(3):
        wb, wkey = next_wbuf(k)
        wv = wb[:].rearrange("p (kc n) -> p kc n", kc=8)
        wload(k, P, ('evqkv', j, which), wv, wkey, wv_in[:, :, 512 * which:512 * which + 512])
        if which < 2:
            for m in range(4):
                pb = m % 2
                for kc in range(8):
                    P.op('pe', lambda e, pb=pb, kc=kc, m=m, wv=wv: e.matmul(k.ps[pb][:, 0:T], lhsT=wv[:, kc, m * 128:(m + 1) * 128], rhs=hT[:, kc, 0:T], start=(kc == 0), stop=(kc == 7)),
                         r=[wkey, 'hT'], w=['ps%d' % pb])
                if which == 0:
                    P.op('act', lambda e, pb=pb, m=m: e.activation(out=QT[:, m, :], in_=k.ps[pb][:, 0:T], func=AF.Copy, scale=0.125), r=['ps%d' % pb], w=['QT'])
                else:
                    P.op('dve', lambda e, pb=pb, m=m: e.tensor_copy(KTn[:, m, :], k.ps[pb][:, 0:T]), r=['ps%d' % pb], w=['KTn'])
        if which >= 1:
            pb = 2 + which % 2
            for kc in range(8):
                P.op('pe', lambda e, pb=pb, kc=kc, wv=wv: e.matmul(k.ps[pb][0:T, :], lhsT=hT[:, kc, 0:T], rhs=wv[:, kc, :], start=(kc == 0), stop=(kc == 7)),
                     r=[wkey, 'hT'], w=['ps%d' % pb])
            ks = kst[which % 2]
            P.op('dve', lambda e, pb=pb, ks=ks: e.tensor_copy(ks[0:T, :], k.ps[pb][0:T, :]), r=['ps%d' % pb], w=['kst%d' % (which % 2)])
            dst = k.k_s if which == 1 else k.v_s
            P.op('sp', lambda e, ks=ks, dst=dst: e.dma_start(out=dst[j, :, :], in_=ks[0:T, :]), r=['kst%d' % (which % 2)], dsem='kst%d' % (which % 2))
            if which == 2:
                P.op('act', lambda e, pb=pb: e.activation(out=Vn[0:T, :], in_=k.ps[pb][0:T, :], func=AF.Copy), r=['ps%d' % pb], w=['Vn'])
    for b in range(NSS):
        ckv = k.cache_k[j, b].rearrange("(t p) f -> p t f", p=128)
        cvv = k.cache_v[j, b].rearrange("(t p) f -> p t f", p=128)
        for q4 in range(4):
            P.op('pool', lambda e, ckv=ckv, q4=q4: e.dma_start(out=kc_nat[:, 4 * q4:4 * q4 + 4, :], in_=ckv[:, 4 * q4:4 * q4 + 4, :]), w=['kc_nat'], dsem='kc%d' % q4)
        for q4 in range(4):
            P.op('pool', lambda e, cvv=cvv, q4=q4: e.dma_start(out=Vc[:, 4 * q4:4 * q4 + 4, :], in_=cvv[:, 4 * q4:4 * q4 + 4, :]), w=['Vc'], dsem='vc%d' % q4)
        cnt = 0
        for m in range(4):
            for j8 in range(2):
                pb = cnt % 2
                cnt += 1
                for jj in range(8):
                    jt = 8 * j8 + jj
                    P.op('pe', lambda e, pb=pb, jj=jj, jt=jt, m=m: e.transpose(k.psb[pb][:, jj * 128:(jj + 1) * 128], kc_nat[:, jt, m * 128:(m + 1) * 128], k.ident[:, :]),
                         r=['kc_nat', 'ident'], w=['ps%d' % pb])
                if cnt % 2 == 0:
                    P.op('act', lambda e, pb=pb, m=m, j8=j8: e.activation(out=KTb[:, m, 1024 * j8:1024 * j8 + 1024], in_=k.psb[pb][:, :], func=AF.Copy), r=['ps%d' % pb], w=['KTb'])
                else:
                    P.op('dve', lambda e, pb=pb, m=m, j8=j8: e.tensor_copy(KTb[:, m, 1024 * j8:1024 * j8 + 1024], k.psb[pb][:, :]), r=['ps%d' % pb], w=['KTb'])
        for jt in range(NK):
            M = 128 if jt < 16 else NSTOK
            sb = 2 + jt % 2
            for h in range(8):
                m, hh = h // 2, h % 2
                rows = slice(64 * hh, 64 * hh + 64)
                lhs = KTb[rows, m, jt * 128:(jt + 1) * 128] if jt < 16 else KTn[rows, m, 0:NSTOK]
                lk = 'KTb' if jt < 16 else 'KTn'
                P.op('pe', lambda e, sb=sb, lhs=lhs, rows=rows, m=m, h=h, M=M, b=b: e.matmul(
                    k.ps[sb][0:M, h * 8:(h + 1) * 8], lhsT=lhs, rhs=QT[rows, m, 8 * b:8 * b + 8], start=True, stop=False), r=[lk, 'QT'], w=['ps%d' % sb])
                P.op('pe', lambda e, sb=sb, jt=jt, h=h, M=M: e.matmul(k.ps[sb][0:M, h * 8:(h + 1) * 8], lhsT=kaug[0:4, jt, 0:M], rhs=qaug[0:4, h * 8:(h + 1) * 8],
                                                                     start=False, stop=True), r=['kaug', 'qaug'], w=['ps%d' % sb])
            P.op('act', lambda e, sb=sb, jt=jt, M=M: e.activation(out=Pall[0:M, jt, :], in_=k.ps[sb][0:M, 0:64], func=AF.Exp), r=['ps%d' % sb], w=['Pall'])
            gmask = Gs[0:M, jt, :] if jt < 16 else Gsn[0:M, b, :]
            P.op('pool', lambda e, jt=jt, M=M, gmask=gmask: e.tensor_tensor(out=Pall[0:M, jt, :], in0=Pall[0:M, jt, :], in1=gmask, op=ALU.mult),
                 r=['Pall', 'Gs', 'Gsn'], w=['Pall'])
        for h in range(8):
            m = h // 2
            for jt in range(NK):
                M = 128 if jt < 16 else NSTOK
                lhs = Vc[:, jt, m * 128:(m + 1) * 128] if jt < 16 else Vn[0:NSTOK, m * 128:(m + 1) * 128]
                lk = 'Vc' if jt < 16 else 'Vn'
                f0, f1 = (jt == 0), (jt == NK - 1)
                P.op('pe', lambda e, lhs=lhs, h=h, jt=jt, M=M, f0=f0, f1=f1: e.matmul(k.ps[4][:, h * 8:(h + 1) * 8], lhsT=lhs, rhs=Pall[0:M, jt, h * 8:(h + 1) * 8], start=f0, stop=f1),
                     r=[lk, 'Pall'], w=['ps4'])
            for jt in range(NK):
                M = 128 if jt < 16 else NSTOK
                f0, f1 = (jt == 0), (jt == NK - 1)
                P.op('pe', lambda e, h=h, jt=jt, M=M, f0=f0, f1=f1: e.matmul(k.ps[5][:, h * 8:(h + 1) * 8], lhsT=ones_b[0:M, :], rhs=Pall[0:M, jt, h * 8:(h + 1) * 8], start=f0, stop=f1),
                     r=['ones_b', 'Pall'], w=['ps5'])
        P.op('dve', lambda e: e.reciprocal(out=rb[:, :], in_=k.ps[5][:, 0:64]), r=['ps5'], w=['rb'])
        for hh in range(2):
            rows = slice(64 * hh, 64 * hh + 64)
            P.op('dve', lambda e, rows=rows, hh=hh, b=b: e.tensor_tensor(
                out=oaT[rows, :, 8 * b:8 * b + 8], in0=k.ps[4][rows, 0:64].rearrange("p (m x t) -> p m x t", x=2, t=8)[:, :, hh, :],
                in1=rb[rows, :].rearrange("p (m x t) -> p m x t", x=2, t=8)[:, :, hh, :], op=ALU.mult), r=['ps4', 'rb'], w=['oaT'])

    def src(kc, col, np_):
        if kc < 4:
            return oaT[:, kc, col:col + np_], 'oaT'
        return o_bT[:, kc - 4, col:col + np_], 'o_bT'
    proj_out(k, tiles, src, None, wov, 8, mtmp, ssh, junk, wtag=('evo', j))
    post_norm_add(k, tiles, 1, mtmp, ssh, tmp_t)


_CACHE = {}


def host_constants():
    c = {}
    i128 = np.arange(128)
    c['c_maskp'] = (i128[None, :] >= i128[:, None]).astype(np.float32)
    se = np.zeros((128, 128), np.float32)
    se[127, :] = 1.0
    c['c_selendp'] = se
    t = np.arange(NSTOK)
    same = (t[:, None] // LS) == (t[None, :] // LS)
    c['c_masks'] = (same & (t[None, :] >= t[:, None])).astype(np.float32)
    c['c_selends'] = (t[:, None] == (LS * (t[None, :] // LS) + LS - 1)).astype(np.float32)
    sb = np.zeros((NSTOK, NSS, 128), np.float32)
    for b in range(NSS):
        sb[LS * b + LS - 1, b, :] = 1.0
    c['c_selendBs'] = sb
    sc = np.zeros((128, NSS, NSTOK), np.float32)
    sr = np.zeros((NSTOK, NSS), np.float32)
    for b in range(NSS):
        sc[:, b, LS * b:LS * (b + 1)] = 1.0
        sr[LS * b:LS * (b + 1), b] = 1.0
    c['c_seqcol'] = sc
    c['c_seqrow'] = sr
    c['c_negp'] = (c['c_maskp'] - 1.0) * 30000.0
    c['c_negs'] = (c['c_masks'] - 1.0) * 30000.0

    def cmult(d):
        d = np.asarray(d)
        return (((d >= 0) & (d <= 128)).astype(np.float32) + ((d >= 0) & (d <= 512) & (d % 4 == 0)).astype(np.float32)
                + ((d >= 0) & (d <= 2048) & (d % 16 == 0)).astype(np.float32))
    rl = np.arange(128)
    xx = np.arange(2304)
    c['c_G'] = cmult(xx[None, :] - 128 - rl[:, None]).astype(np.float32)
    ka = np.zeros((4, 128), np.float32)
    ka[0] = rl
    ka[1] = 1.0
    ka[2] = 1.0
    c['c_kaug'] = ka
    tl = np.arange(512)
    qa = np.zeros((4, 512), np.float32)
    qa[0] = 1.0
    qa[1] = -128.0 * (tl // 128)
    qa[2] = -(tl % 128)
    c['c_qaug'] = qa
    qs = np.zeros((128, 4), np.float32)
    for m in range(4):
        for p in range(128):
            qs[p, m] = 2.0 ** (2 * m + p // 64 - 2)
    c['c_qscale'] = qs
    sw = np.zeros((128, 128), np.float32)
    for p in range(64):
        sw[64 + p, p] = -1.0
        sw[p, 64 + p] = 1.0
    c['c_swp'] = sw
    c['c_rowmask'] = (rl[:, None] // 16 == np.arange(8)[None, :]).astype(np.float32)
    c['c_iota1'] = np.broadcast_to(np.arange(1, 513, dtype=np.float32)[None, :], (128, 512)).copy()
    jt = np.arange(17)
    tq = np.arange(LS)
    d = 2048 + tq[None, None, :] - (128 * jt[None, :, None] + rl[:, None, None])
    valid = (128 * jt[None, :, None] + rl[:, None, None]) < 2048 + LS
    gs = cmult(d) * valid
    c['c_Gs'] = np.repeat(gs[:, :, None, :], 8, axis=2).reshape(128, 17, 64).astype(np.float32)
    kas = np.zeros((4, 17, 128), np.float32)
    kas[0] = jt[:, None]
    kas[1] = rl[None, :]
    kas[2] = 1.0
    kas[3] = 1.0
    kas[1, 16, :] = rl % LS
    c['c_kaug_s'] = kas
    rk = np.arange(NSTOK)
    gn = np.zeros((NSTOK, NSS, 8, LS), np.float32)
    for b in range(NSS):
        dd = tq[None, :] - (rk[:, None] % LS)
        gn[:, b, :, :] = (cmult(dd) * ((rk[:, None] // LS) == b))[:, None, :]
    c['c_Gsn'] = gn.reshape(NSTOK, NSS, 64)
    qas = np.zeros((4, 8, LS), np.float32)
    for h in range(8):
        sl = 2.0 ** (-(h + 1))
        qas[0, h, :] = sl * 128.0
        qas[1, h, :] = sl
        qas[2, h, :] = -sl * 2048.0
        qas[3, h, :] = -sl * tq
    c['c_qaug_s'] = qas.reshape(4, 64)
    return c


def make_in_maps(inputs):
    f = lambda a: np.ascontiguousarray(np.asarray(a, dtype=np.float32))
    norms = f(np.stack([inputs['norm_mix_pre'], inputs['norm_mix_post'], inputs['norm_mlp_pre'], inputs['norm_mlp_post']]))
    shared = {
        'norms': norms,
        'w_mlp_up': f(inputs['w_mlp_up']),
        'w_mlp_down': f(inputs['w_mlp_down']),
        'identf': np.eye(128, dtype=np.float32),
        'w_in_odd': f(inputs['w_in_odd']), 'w_out_odd': f(inputs['w_out_odd']),
        'conv_w': f(inputs['conv_w']), 'conv_b': f(inputs['conv_b']), 'dt_bias': f(inputs['dt_bias']),
        'a_log': f(inputs['a_log']), 'd_skip': f(inputs['d_skip']), 'gnorm_w': f(inputs['gnorm_w']),
    }
    for nm in ('w_in_even', 'w_out_even', 's5_lambda_re', 's5_lambda_im', 's5_log_dt', 's5_b_re', 's5_b_im', 's5_c_re', 's5_c_im',
               's5_d', 's5_w_glu', 's5_b_glu'):
        shared[nm] = f(inputs[nm])
    shared.update(host_constants())
    maps = []
    for c in range(NCORES):
        m = dict(shared)
        m['x_p'] = f(inputs['x_prompt'][c])
        m['x_s'] = f(inputs['x_sample'][NSS * c:NSS * (c + 1)]).reshape(NSTOK, D)
        m['state_conv'] = f(inputs['state_conv'][:, NSS * c:NSS * (c + 1)])
        m['state_s5'] = f(inputs['state_s5'][:, NSS * c:NSS * (c + 1)])
        m['cache_k'] = f(inputs['cache_k'][:, NSS * c:NSS * (c + 1)]).reshape(2, NSS, 2048, 512)
        m['cache_v'] = f(inputs['cache_v'][:, NSS * c:NSS * (c + 1)]).reshape(2, NSS, 2048, 512)
        m['state_ssm'] = f(inputs['state_ssm'][:, NSS * c:NSS * (c + 1)])
        maps.append(m)
    return maps


def kernel(**inputs):
    cfg = {}
    if 'nc' not in _CACHE:
        _CACHE['nc'] = build(cfg)
    nc, k = _CACHE['nc']
    maps = make_in_maps(inputs)
    res = run_bass_kernel_spmd(nc, maps, core_ids=list(range(NCORES)))
    r = res.results
    C = range(NCORES)
    st = lambda name, shp: np.stack([r[c][name].reshape(shp) for c in C], axis=1)
    cat = lambda name, shp: np.concatenate([r[c][name].reshape(shp) for c in C], axis=1)
    y_p = np.stack([r[c]['y_p'] for c in C])
    y_s = np.concatenate([r[c]['y_s'].reshape(NSS, LS, D) for c in C])
    k_p = st('k_p', (2, SEQ, 8, 64))
    v_p = st('v_p', (2, SEQ, 8, 64))
    s5_p = st('s5_p', (2, 32, 64, 2))
    conv_p = st('conv_p', (2, 3, NXBC))
    ssm_p = st('ssm_p', (2, 32, 64, 128))
    k_s = cat('k_s', (2, NSS, LS, 8, 64))
    v_s = cat('v_s', (2, NSS, LS, 8, 64))
    s5_s = cat('s5_s', (2, NSS, 32, 64, 2))
    conv_s = cat('conv_s', (2, NSS, 3, NXBC))
    ssm_s = cat('ssm_s', (2, NSS, 32, 64, 128))
    return (y_p, y_s, k_p, v_p, s5_p, conv_p, ssm_p, k_s, v_s, s5_s, conv_s, ssm_s)
```

```python
import numpy as np
from contextlib import ExitStack
import concourse.bass as bass
import concourse.mybir as mybir
from concourse.bass_utils import run_bass_kernel_spmd

F32 = mybir.dt.float32
BF16 = mybir.dt.bfloat16
ALU = mybir.AluOpType
AF = mybir.ActivationFunctionType
AX = mybir.AxisListType

ENGS = ['pe', 'act', 'dve', 'pool', 'sp']


class Op:
    __slots__ = ('eng', 'fn', 'deps', 'marked', 'semval', 'dsem', 'dval', 'idx', 'ptail')

    def __init__(self, eng, fn):
        self.eng = eng
        self.fn = fn
        self.deps = []
        self.marked = False
        self.semval = 0
        self.dsem = None
        self.dval = 0


class Prog:
    def __init__(self, nc, stack, arena_bytes=210944):
        self.nc = nc
        self.stack = stack
        self.ops = {e: [] for e in ENGS}
        self.sems = {e: stack.enter_context(nc.semaphore('s_' + e)) for e in ENGS}
        self.dsems = {}
        self.last_w = {}
        self.readers = {}
        self.phase_op = None
        self.last_of = {}
        self.sync_same_engine = True
        self.hoist_floor = 0
        self.hoist_prev = None
        self.pool_tail = None
        self.unchained = set()
        arena = nc.alloc_sbuf_tensor('arena', [128, arena_bytes // 4], F32)
        self.base = nc.lookup_mloc(arena).addr
        self.limit = self.base + arena_bytes
        self.pers = self.base
        self.cur = None
        self.nid = 0
        self.peak = 0

    def _alloc(self, off, shape, dtype, name):
        self.nid += 1
        return self.nc.alloc_sbuf_tensor_at('%s_%d' % (name or 't', self.nid), list(shape), dtype, offset=off)

    @staticmethod
    def _bytes(shape, dtype):
        n = 1
        for s in shape[1:]:
            n *= s
        n *= mybir.dt.size(dtype)
        return (n + 63) // 64 * 64

    def sbp(self, shape, dtype, name=None):
        assert self.cur is None
        off = self.pers
        self.pers += self._bytes(shape, dtype)
        assert self.pers <= self.limit, 'sbuf overflow (persistent)'
        return self._alloc(off, shape, dtype, name)

    def phase_begin(self, keep=0):
        self.barrier()
        self.cur = self.pers + keep

    def sb(self, shape, dtype, name=None):
        off = self.cur
        self.cur += self._bytes(shape, dtype)
        assert self.cur <= self.limit, 'sbuf overflow (phase) need %d' % (self.cur - self.limit)
        self.peak = max(self.peak, self.cur - self.base)
        return self._alloc(off, shape, dtype, name)

    def _stream(self, o):
        return o.dsem if o.dsem is not None else o.eng

    def _val(self, d):
        return d.dval if d.dsem is not None else d.idx

    def op(self, eng, fn, r=(), w=(), dsem=None, hoist=False, chain=True):
        o = Op(eng, fn)
        deps = {}
        pr = [x for x in r if isinstance(x, str) and x.startswith('ps')]
        if pr:
            r = [x for x in r if x not in pr]
            w = list(w) + pr

        def add(d):
            if d is None:
                return
            s = self._stream(d)
            cur = deps.get(s)
            if cur is None or self._val(d) > self._val(cur):
                deps[s] = d

        for k in r:
            add(self.last_w.get(k))
        for k in w:
            add(self.last_w.get(k))
            for rd in self.readers.get(k, {}).values():
                add(rd)
        add(self.phase_op)
        if dsem is not None and not chain:
            self.unchained.add(dsem)
            deps.pop(dsem, None)
        for sname in list(deps.keys()):
            if sname in self.unchained and sname != dsem:
                deps[sname] = self.dsems[sname][2]
        if dsem is not None:
            ent = self.dsems.get(dsem)
            if ent is None:
                ent = [self.stack.enter_context(self.nc.semaphore('d_' + dsem)), 0, None]
                self.dsems[dsem] = ent
            if chain:
                add(ent[2])
            ent[1] += 16
            o.dsem = dsem
            o.dval = ent[1]
            ent[2] = o
        o.idx = len(self.ops[eng])
        final = []
        for s, d in deps.items():
            if d.dsem is None:
                if d.eng == eng and (eng == 'pe' or not self.sync_same_engine):
                    continue
                d.marked = True
            final.append(d)
        o.deps = final
        st = self._stream(o)
        for k in r:
            self.readers.setdefault(k, {})[st] = o
        for k in w:
            self.last_w[k] = o
            self.readers[k] = {}
        o.ptail = self.pool_tail
        if hoist and eng == 'pool':
            lst = self.ops[eng]
            pos = self.hoist_floor
            cands = [self.hoist_prev] + [d.ptail for d in final]
            for c in cands:
                if c is not None:
                    pos = max(pos, lst.index(c) + 1)
            lst.insert(pos, o)
            self.hoist_prev = o
        else:
            self.ops[eng].append(o)
            if eng == 'pool':
                self.pool_tail = o
        self.last_of[st] = o
        return o

    def barrier(self):
        self.hoist_floor = len(self.ops['pool'])
        self.hoist_prev = None
        o = Op('sp', lambda e: e.nop())
        o.ptail = self.pool_tail
        o.idx = len(self.ops['sp'])
        deps = []
        for s, d in self.last_of.items():
            if d.dsem is None:
                d.marked = True
            deps.append(d)
        o.deps = deps
        o.marked = True
        self.ops['sp'].append(o)
        self.last_of['sp'] = o
        self.phase_op = o
        self.last_w = {}
        self.readers = {}
        return o

    def emit(self):
        nc = self.nc
        for e in ENGS:
            c = 0
            for o in self.ops[e]:
                if o.dsem is None and o.marked:
                    c += 1
                    o.semval = c

        def run(ename, eng):
            waited = {}
            for o in self.ops[ename]:
                for d in o.deps:
                    if d.dsem is not None:
                        sem, val, key = self.dsems[d.dsem][0], d.dval, 'd_' + d.dsem
                    else:
                        sem, val, key = self.sems[d.eng], d.semval, d.eng
                    if waited.get(key, 0) >= val:
                        continue
                    waited[key] = val
                    eng.wait_ge(sem, val)
                ins = o.fn(eng)
                if o.dsem is not None:
                    ins.then_inc(self.dsems[o.dsem][0], 16)
                elif o.marked:
                    ins.then_inc(self.sems[ename], 1)

        with nc.Block() as block:
            @block.tensor
            def _(eng):
                run('pe', eng)

            @block.scalar
            def _(eng):
                run('act', eng)

            @block.vector
            def _(eng):
                run('dve', eng)

            @block.gpsimd
            def _(eng):
                run('pool', eng)

            @block.sync
            def _(eng):
                run('sp', eng)


D = 1024
SEQ = 2048
NT = SEQ // 128
DEPTH = 4
NSS = 4
LS = 8
NSTOK = NSS * LS
DFF = 4096
EPS = 1e-6
NCORES = 8


class K:
    pass


def build(cfg):
    nc = bass.Bass("TRN2", target_bir_lowering=False)
    depth = cfg.get('depth', DEPTH)
    do_mix = cfg.get('mix', 7)
    k = K()
    k.nc = nc
    k.cfg = cfg
    k.scratch = {}

    def din(name, shape):
        return nc.dram_tensor(name, list(shape), F32, kind="ExternalInput").ap()

    def dout(name, shape):
        return nc.dram_tensor(name, list(shape), F32, kind="ExternalOutput").ap()

    k.x_p = din('x_p', [SEQ, D])
    k.x_s = din('x_s', [NSTOK, D])
    k.norms = din('norms', [4, DEPTH, D])
    k.w_up = din('w_mlp_up', [DEPTH, D, DFF])
    k.w_down = din('w_mlp_down', [DEPTH, DFF, D])
    k.identf = din('identf', [128, 128])
    k.w_in_odd = din('w_in_odd', [2, D, 5152])
    k.w_out_odd = din('w_out_odd', [2, DIN, D])
    k.conv_w = din('conv_w', [2, 4, NXBC])
    k.conv_b = din('conv_b', [2, NXBC])
    k.dt_bias = din('dt_bias', [2, 32])
    k.a_log = din('a_log', [2, 32])
    k.d_skip = din('d_skip', [2, 32])
    k.gnorm_w = din('gnorm_w', [2, DIN])
    k.state_conv = din('state_conv', [2, NSS, 3, NXBC])
    k.state_ssm = din('state_ssm', [2, NSS, 32, 64, 128])
    k.c_maskp = din('c_maskp', [128, 128])
    k.c_masks = din('c_masks', [NSTOK, NSTOK])
    k.c_selendp = din('c_selendp', [128, 128])
    k.c_selends = din('c_selends', [NSTOK, NSTOK])
    k.c_selendBs = din('c_selendBs', [NSTOK, NSS, 128])
    k.c_seqcol = din('c_seqcol', [128, NSS, NSTOK])
    k.c_seqrow = din('c_seqrow', [NSTOK, NSS])
    k.c_negp = din('c_negp', [128, 128])
    k.c_negs = din('c_negs', [NSTOK, NSTOK])
    k.w_in_even = din('w_in_even', [2, D, 2048])
    k.w_out_even = din('w_out_even', [2, D, D])
    k.s5_lambda_re = din('s5_lambda_re', [2, 32, 64])
    k.s5_lambda_im = din('s5_lambda_im', [2, 32, 64])
    k.s5_log_dt = din('s5_log_dt', [2, 32])
    k.s5_b_re = din('s5_b_re', [2, 32, 64, 16])
    k.s5_b_im = din('s5_b_im', [2, 32, 64, 16])
    k.s5_c_re = din('s5_c_re', [2, 32, 16, 64])
    k.s5_c_im = din('s5_c_im', [2, 32, 16, 64])
    k.s5_d = din('s5_d', [2, 32, 16])
    k.s5_w_glu = din('s5_w_glu', [2, 512, 512])
    k.s5_b_glu = din('s5_b_glu', [2, 512])
    k.state_s5 = din('state_s5', [2, NSS, 32, 64, 2])
    k.cache_k = din('cache_k', [2, NSS, 2048, 512])
    k.cache_v = din('cache_v', [2, NSS, 2048, 512])
    k.c_G = din('c_G', [128, 2304])
    k.c_kaug = din('c_kaug', [4, 128])
    k.c_qaug = din('c_qaug', [4, 512])
    k.c_qscale = din('c_qscale', [128, 4])
    k.c_swp = din('c_swp', [128, 128])
    k.c_rowmask = din('c_rowmask', [128, 8])
    k.c_iota1 = din('c_iota1', [128, 512])
    k.c_Gs = din('c_Gs', [128, 17, 64])
    k.c_qaug_s = din('c_qaug_s', [4, 64])
    k.c_kaug_s = din('c_kaug_s', [4, 17, 128])
    k.c_Gsn = din('c_Gsn', [NSTOK, NSS, 64])
    k.y_p = dout('y_p', [SEQ, D])
    k.y_s = dout('y_s', [NSTOK, D])
    k.k_p = dout('k_p', [2, SEQ, 512])
    k.v_p = dout('v_p', [2, SEQ, 512])
    k.s5_p = dout('s5_p', [2, 32, 64, 2])
    k.k_s = dout('k_s', [2, NSTOK, 512])
    k.v_s = dout('v_s', [2, NSTOK, 512])
    k.s5_s = dout('s5_s', [2, NSS, 32, 64, 2])
    k.conv_p = dout('conv_p', [2, 3, NXBC])
    k.ssm_p = dout('ssm_p', [2, 32, 64, 128])
    k.conv_s = dout('conv_s', [2, NSS, 3, NXBC])
    k.ssm_s = dout('ssm_s', [2, NSS, 32, 64, 128])

    with ExitStack() as st:
        P = Prog(nc, st)
        k.P = P
        k.X = P.sbp([128, NT, D], F32, 'X')
        k.Xs = P.sbp([128, 1, D], F32, 'Xs')
        k.ident = P.sbp([128, 128], BF16, 'ident')
        k.identF = P.sbp([128, 128], F32, 'identF')
        k.wctr = 0
        k.small = P.sbp([128, 64], F32, 'small')
        k.smctr = 0
        k.ps = [nc.alloc_psum_tensor('ps%d' % i, [128, 512], F32) for i in range(8)]
        cp = {'mask': P.sbp([128, 128], F32, 'c_maskp'), 'selend': P.sbp([128, 128], F32, 'c_selendp')}
        cp['selendB'] = cp['selend'][:].rearrange("p (b n) -> p b n", b=1)
        cs = {'mask': P.sbp([128, NSTOK], F32, 'c_masks'), 'selend': P.sbp([128, NSTOK], F32, 'c_selends'),
              'selendB': P.sbp([128, NSS, 128], F32, 'c_selendBs'), 'seqcol': P.sbp([128, NSS, NSTOK], F32, 'c_seqcol'),
              'seqrow': P.sbp([128, NSS], F32, 'c_seqrow')}
        k.cst_p, k.cst_s = cp, cs
        k.qscale = P.sbp([128, 4], F32, 'qscale')
        k.swp = P.sbp([128, 128], BF16, 'swp')
        k.rowmask = P.sbp([128, 8], F32, 'rowmask')
        k.iota1 = P.sbp([128, 512], F32, 'iota1')
        P.op('sp', lambda e: e.dma_start(out=k.qscale[:], in_=k.c_qscale[:, :]), w=['qscale'], dsem='cst', chain=False)
        P.op('sp', lambda e: e.dma_start(out=k.rowmask[:], in_=k.c_rowmask[:, :]), w=['rowmask'], dsem='cst', chain=False)
        P.op('sp', lambda e: e.dma_start(out=k.iota1[:], in_=k.c_iota1[:, :]), w=['iota1'], dsem='cst', chain=False)
        P.op('pool', lambda e: e.dma_start(out=k.swp[:], in_=k.c_swp[:, :]), w=['swp'], dsem='swp')
        P.op('sp', lambda e: e.dma_start(out=cp['mask'][:], in_=k.c_maskp[:, :]), w=['cst'], dsem='cst', chain=False)
        P.op('sp', lambda e: e.dma_start(out=cp['selend'][:], in_=k.c_selendp[:, :]), w=['cst'], dsem='cst', chain=False)
        P.op('sp', lambda e: e.dma_start(out=cs['mask'][0:NSTOK, :], in_=k.c_masks[:, :]), w=['cst'], dsem='cst', chain=False)
        P.op('sp', lambda e: e.dma_start(out=cs['selend'][0:NSTOK, :], in_=k.c_selends[:, :]), w=['cst'], dsem='cst', chain=False)
        P.op('sp', lambda e: e.dma_start(out=cs['selendB'][0:NSTOK, :, :], in_=k.c_selendBs[:, :, :]), w=['cst'], dsem='cst', chain=False)
        P.op('sp', lambda e: e.dma_start(out=cs['seqcol'][:, :, :], in_=k.c_seqcol[:, :, :]), w=['cst'], dsem='cst', chain=False)
        P.op('sp', lambda e: e.dma_start(out=cs['seqrow'][0:NSTOK, :], in_=k.c_seqrow[:, :]), w=['cst'], dsem='cst', chain=False)
        k.psb = [p[:].bitcast(BF16) for p in k.ps]

        P.op('pool', lambda e: e.dma_start(out=k.ident[:], in_=k.identf[:, :]), w=['ident'], dsem='ident')
        P.op('sp', lambda e: e.dma_start(out=k.identF[:], in_=k.identf[:, :]), w=['identF'], dsem='identF')
        xv = k.x_p.rearrange("(i p) d -> p i d", p=128)
        for q in range(4):
            P.op('sp', lambda e, q=q: e.dma_start(out=k.X[:, 4 * q:4 * q + 4, :], in_=xv[:, 4 * q:4 * q + 4, :]),
                 w=['X%d' % i for i in range(4 * q, 4 * q + 4)], dsem='xload%d' % q)
        P.op('sp', lambda e: e.dma_start(out=k.Xs[0:NSTOK, 0, :], in_=k.x_s[:, :]), w=['Xs0'], dsem='xsload')

        k.ptiles = [(k.X, i, 128, 'X%d' % i) for i in range(NT)]
        k.stile = (k.Xs, 0, NSTOK, 'Xs0')

        for l in range(depth):
            if (do_mix & 2) and l % 2 == 0:
                s5_phase(k, l, l // 2, 'p')
                attn_phase_p(k, l, l // 2)
                if do_mix & 4:
                    s5_phase(k, l, l // 2, 's')
                    attn_phase_s(k, l, l // 2)
            if (do_mix & 1) and l % 2 == 1:
                mamba(k, l, l // 2, 'p')
                mamba(k, l, l // 2, 's')
            mlp(k, l)

        yv = k.y_p.rearrange("(i p) d -> p i d", p=128)
        for q in range(4):
            P.op('sp', lambda e, q=q: e.dma_start(out=yv[:, 4 * q:4 * q + 4, :], in_=k.X[:, 4 * q:4 * q + 4, :]),
                 r=['X%d' % i for i in range(4 * q, 4 * q + 4)], dsem='ystore%d' % q)
        P.op('sp', lambda e: e.dma_start(out=k.y_s[:, :], in_=k.Xs[0:NSTOK, 0, :]), r=['Xs0'], dsem='ysstore')
        P.barrier()
        P.emit()
    k.stats = {e: len(P.ops[e]) for e in ENGS}
    k.peak = P.peak
    return nc, k


def cfg_get(k, name, default):
    return k.cfg.get(name, default)


def phase_common(k, l, widxs, nwbuf=3):
    P = k.P
    wbc = P.sb([128, 2, D], F32, 'wbc')
    k.wbc = wbc
    for jj, j in enumerate(widxs):
        P.op('sp', lambda e, j=j, jj=jj: e.dma_start(out=wbc[:, jj, :], in_=k.norms[j, l:l + 1, :].to_broadcast([128, D])),
             w=['wbc%d' % jj], dsem='wbc%d' % jj)
    k.wbuf = [P.sb([128, 4096], BF16, 'wbuf%d' % i) for i in range(nwbuf)]


def small_slot(k, n=2):
    c = (k.smctr % 32) * 2
    k.smctr += 1
    return c


def wload(k, P, skey, wv, wkey, src):
    if not k.cfg.get('scratch', True):
        P.op('pool', lambda e: e.dma_start(out=wv, in_=src), w=[wkey], dsem=wkey, hoist=True)
        return
    sc = k.scratch.get(skey)
    if sc is None:
        name = 'wsc_%d' % len(k.scratch)
        sc = k.nc.dram_tensor(name, [128, 8, 512], BF16, kind="Internal").ap()
        k.scratch[skey] = sc
        P.op('pool', lambda e: e.dma_start(out=wv, in_=src), w=[wkey], dsem=wkey, hoist=True)
        P.op('sp', lambda e: e.dma_start(out=sc[:, :, :], in_=wv), r=[wkey], w=[('sc', skey)], dsem='st_' + wkey)
    else:
        P.op('sp', lambda e: e.dma_start(out=wv, in_=sc[:, :, :]), r=[('sc', skey)], w=[wkey], dsem='h_' + wkey)


def next_wbuf(k):
    i = k.wctr % len(k.wbuf)
    k.wctr += 1
    return k.wbuf[i], 'wbuf%d' % i


def rstd_from_ss(k, np_, ss_ap, out_ap, rkeys, wkeys):
    P = k.P
    P.op('act', lambda e: e.activation(out=out_ap, in_=ss_ap, func=AF.Ln, scale=1.0 / D, bias=EPS), r=rkeys, w=wkeys)
    P.op('act', lambda e: e.activation(out=out_ap, in_=out_ap, func=AF.Exp, scale=-0.5), r=wkeys, w=wkeys)


def norm_transpose(k, tiles, widx, hT, hTkey, junk, hn):
    P = k.P
    wbc = k.wbc
    col = 0
    for ti, (X, i, np_, xkey) in enumerate(tiles):
        s = small_slot(k, 2)
        sk = 'sm%d' % s
        ss = k.small[0:np_, s:s + 1]
        rs = k.small[0:np_, s + 1:s + 2]
        jslot = ti % 2
        P.op('act', lambda e, X=X, i=i, np_=np_, ss=ss, jslot=jslot: e.activation(
            out=junk[jslot][0:np_, :], in_=X[0:np_, i, :], func=AF.Square, accum_out=ss),
            r=[xkey], w=[('junk', id(junk[jslot])), sk])
        rstd_from_ss(k, np_, ss, rs, [sk], [sk + 'r'])
        P.op('dve', lambda e, X=X, i=i, np_=np_, rs=rs, jslot=jslot: e.scalar_tensor_tensor(
            out=hn[jslot][0:np_, :], in0=X[0:np_, i, :], scalar=rs, in1=wbc[0:np_, widx, :],
            op0=ALU.mult, op1=ALU.mult), r=[xkey, sk + 'r', 'wbc%d' % widx], w=['hn%d' % (jslot if hn[0] is not hn[1] else 0)])
        pb = 6 + (ti % 2)
        for kc in range(8):
            P.op('pe', lambda e, kc=kc, np_=np_, jslot=jslot, pb=pb: e.transpose(
                k.psb[pb][:, kc * 128:kc * 128 + np_], hn[jslot][0:np_, kc * 128:(kc + 1) * 128], k.ident[0:np_, 0:np_]),
                r=['hn%d' % (jslot if hn[0] is not hn[1] else 0), 'ident'], w=['ps%d' % pb])
        src = k.psb[pb].rearrange("p (c t) -> p c t", c=8)[:, :, 0:np_]
        eng = 'act' if ti % 2 == 0 else 'dve'
        if eng == 'act':
            P.op('act', lambda e, src=src, col=col, np_=np_: e.activation(out=hT[:, :, col:col + np_], in_=src, func=AF.Copy),
                 r=['ps%d' % pb], w=[hTkey])
        else:
            P.op('dve', lambda e, src=src, col=col, np_=np_: e.tensor_copy(hT[:, :, col:col + np_], src),
                 r=['ps%d' % pb], w=[hTkey])
        col += np_
    return col


def post_norm_add(k, tiles, widx, mtmp, ssh, tmp_t):
    P = k.P
    wbc = k.wbc
    for ti, (X, i, np_, xkey) in enumerate(tiles):
        s = small_slot(k, 2)
        sk = 'sm%d' % s
        ss = k.small[0:np_, s:s + 1]
        rs = k.small[0:np_, s + 1:s + 2]
        P.op('dve', lambda e, np_=np_, ss=ss, ti=ti: e.tensor_tensor(out=ss, in0=ssh[0:np_, 2 * ti:2 * ti + 1],
                                                                     in1=ssh[0:np_, 2 * ti + 1:2 * ti + 2], op=ALU.add),
             r=['ssh%d' % ti], w=[sk])
        rstd_from_ss(k, np_, ss, rs, [sk], [sk + 'r'])
        tt = tmp_t[ti % 2]
        tk = 'tmpt%d' % ((ti % 2) if tmp_t[0] is not tmp_t[1] else 0)
        P.op('dve', lambda e, np_=np_, rs=rs, ti=ti, tt=tt: e.scalar_tensor_tensor(
            out=tt[0:np_, :], in0=mtmp[0:np_, ti, :], scalar=rs, in1=wbc[0:np_, widx, :], op0=ALU.mult, op1=ALU.mult),
            r=['mtmp%d' % ti, sk + 'r', 'wbc%d' % widx], w=[tk])
        P.op('pool', lambda e, X=X, i=i, np_=np_, tt=tt: e.tensor_tensor(out=X[0:np_, i, :], in0=X[0:np_, i, :], in1=tt[0:np_, :], op=ALU.add),
             r=[tk, xkey], w=[xkey])


def chunks_of(n):
    out = []
    c = 0
    while c < n:
        m = min(512, n - c)
        out.append((c, m))
        c += m
    return out


def mlp(k, l):
    P = k.P
    P.phase_begin()
    phase_common(k, l, (2, 3), nwbuf=cfg_get(k, "mlp_nwbuf", 5))
    NC = 512 + NSTOK
    hT = P.sb([128, 8, NC], BF16, 'hT')
    aT = P.sb([128, 32, NC], BF16, 'aT')
    junk = [P.sb([128, D], BF16, 'junk%d' % i) for i in range(2)]
    hn = [P.sb([128, D], BF16, 'hn%d' % i) for i in range(2)]
    rt = [P.sb([128, 512], BF16, 'rt%d' % i) for i in range(2)]
    mtmp = P.sb([128, 5, D], F32, 'mtmp')
    ssh = P.sb([128, 16], F32, 'ssh')
    tmp_t = [P.sb([128, D], F32, 'tmpt%d' % i) for i in range(2)]
    wupv = k.w_up[l].rearrange("(kc p) n -> p kc n", p=128)
    wdnv = k.w_down[l].rearrange("(kc p) n -> p kc n", p=128)
    junkN = [P.sb([128, D], BF16, 'junkN')] * 2

    def tiles_of(b):
        t = k.ptiles[4 * b:4 * b + 4]
        return t + [k.stile] if b == 3 else t
    ncols = {0: norm_transpose(k, tiles_of(0), 0, hT, 'hT', junkN, hn)}
    for b in range(4):
        tiles = tiles_of(b)
        ncol = ncols[b]
        chs = chunks_of(ncol)
        cnt = 0
        for fb in range(8):
            wb, wkey = next_wbuf(k)
            wv = wb[:].rearrange("p (kc n) -> p kc n", kc=8)
            wload(k, P, ('up', l, fb), wv, wkey, wupv[:, :, fb * 512:(fb + 1) * 512])
            for m in range(4):
                for ci, (c0, n) in enumerate(chs):
                    if ci == 0:
                        pb = cnt % 2
                        pst = k.ps[pb][:, 0:n]
                        pkey = 'ps%d' % pb
                    else:
                        pst = k.ps[7][:, 256 * (cnt % 2):256 * (cnt % 2) + n]
                        pkey = 'ps7'
                    for kc in range(8):
                        P.op('pe', lambda e, pst=pst, wv=wv, kc=kc, m=m, c0=c0, n=n: e.matmul(
                            pst, lhsT=wv[:, kc, m * 128:(m + 1) * 128], rhs=hT[:, kc, c0:c0 + n], start=(kc == 0), stop=(kc == 7)),
                            r=[wkey, 'hT'], w=[pkey])
                    rslot = cnt % 2
                    P.op('act', lambda e, pst=pst, rslot=rslot, n=n: e.activation(out=rt[rslot][:, 0:n], in_=pst, func=AF.Relu),
                         r=[pkey], w=['rt%d' % rslot])
                    P.op('pool', lambda e, rslot=rslot, n=n, fb=fb, m=m, c0=c0: e.tensor_tensor(
                        out=aT[:, fb * 4 + m, c0:c0 + n], in0=rt[rslot][:, 0:n], in1=rt[rslot][:, 0:n], op=ALU.mult),
                        r=['rt%d' % rslot], w=['aT'])
                    cnt += 1
        if b < 3:
            ncols[b + 1] = norm_transpose(k, tiles_of(b + 1), 0, hT, 'hT', junkN, hn)
        proj_out(k, tiles, aT, 'aT', wdnv, 32, mtmp, ssh, junk, wtag=('dn', l))
        post_norm_add(k, tiles, 1, mtmp, ssh, tmp_t)


def proj_out(k, tiles, srcT, skey, wview, nK, mtmp, ssh, junk, wtag=None):
    P = k.P
    accb = [2, 3, 4, 5, 7]
    nkb = nK // 8
    for half in range(2):
        for kb in range(nkb):
            wb, wkey = next_wbuf(k)
            wv = wb[:].rearrange("p (kc n) -> p kc n", kc=8)
            wload(k, P, (wtag, half, kb), wv, wkey, wview[:, kb * 8:kb * 8 + 8, half * 512:(half + 1) * 512])
            col = 0
            for ti, (X, i, np_, xkey) in enumerate(tiles):
                for m in range(8):
                    if callable(srcT):
                        lhs, lk = srcT(kb * 8 + m, col, np_)
                    else:
                        lhs, lk = srcT[:, kb * 8 + m, col:col + np_], skey
                    P.op('pe', lambda e, ti=ti, np_=np_, kb=kb, m=m, wv=wv, lhs=lhs: e.matmul(
                        k.ps[accb[ti]][0:np_, :], lhsT=lhs, rhs=wv[:, m, :],
                        start=(kb == 0 and m == 0), stop=(kb == nkb - 1 and m == 7)),
                        r=[wkey, lk], w=['ps%d' % accb[ti]])
                col += np_
        for ti, (X, i, np_, xkey) in enumerate(tiles):
            pk = 'ps%d' % accb[ti]
            jslot = ti % 2
            P.op('act', lambda e, ti=ti, np_=np_, half=half, jslot=jslot: e.activation(
                out=junk[jslot][0:np_, 0:512], in_=k.ps[accb[ti]][0:np_, :], func=AF.Square,
                accum_out=ssh[0:np_, 2 * ti + half:2 * ti + half + 1]),
                r=[pk], w=['junk%d' % (jslot if junk[0] is not junk[1] else 0), 'ssh%d' % ti])
            P.op('dve', lambda e, ti=ti, np_=np_, half=half: e.tensor_copy(
                mtmp[0:np_, ti, half * 512:(half + 1) * 512], k.ps[accb[ti]][0:np_, :]),
                r=[pk], w=['mtmp%d' % ti])


DIN = 2048
NXBC = 3072
ZOFF, XOFF, DTOFF = 0, 2048, 5120


def mamba(k, l, j, grp):
    P = k.P
    nc = k.nc
    P.phase_begin()
    phase_common(k, l, (0, 1), nwbuf=2)
    prompt = grp == 'p'
    nseq = 1 if prompt else NSS
    SBT = 2
    Lb = 128 * SBT if prompt else LS
    NCOL = nseq * Lb
    T = 128 if prompt else NSTOK
    nchunk = NCOL // T
    nsb = NT // SBT if prompt else 1
    ntile = SBT if prompt else 1
    cst = k.cst_p if prompt else k.cst_s
    conv_out = k.conv_p if prompt else k.conv_s
    ssm_out = k.ssm_p if prompt else k.ssm_s

    hT = P.sb([128, 8, NCOL], BF16, 'hT')
    zs = P.sb([128, ntile, DIN], BF16, 'zs')
    xbcT = P.sb([128, 24, nseq, 4 + Lb], BF16, 'xbcT')
    xlast = P.sb([128, 24, nseq, 3], F32, 'xlast')
    xcT = [P.sb([128, NCOL], BF16, 'xcT%d' % i) for i in range(2)]
    accs = [P.sb([128, NCOL], F32, 'cacc%d' % i) for i in range(2)]
    BT = P.sb([128, 4, NCOL], BF16, 'BT')
    CT = P.sb([128, 4, NCOL], BF16, 'CT')
    x_tok = P.sb([128, ntile, DIN], BF16, 'x_tok')
    B_tok = P.sb([128, ntile, 512], BF16, 'B_tok')
    gnT = P.sb([128, 16, NCOL], BF16, 'gnT')
    ST = [P.sb([128, 4, 512], F32, 'ST%d' % b) for b in range(nseq)]
    STb = [P.sb([128, 512], BF16, 'STbs%d' % i) for i in range(2)]
    stctr = [0]
    gw_bc = P.sb([128, DIN], F32, 'gw_bc')
    D_bc = P.sb([128, 32], F32, 'D_bc')
    cw = P.sb([128, 24, 4], F32, 'cw')
    cbias = P.sb([128, 24], F32, 'cbias')
    vec = P.sb([64, 4], F32, 'vec')
    PK = P.sb([64, NCOL], F32, 'PK')
    dA = P.sb([64, NCOL], F32, 'dA')
    ones = P.sb([64, 128], F32, 'ones')
    tokpk = P.sb([128, nchunk, 64], F32, 'tokpk')
    Et = P.sb([128, 32], F32, 'Et')
    wend = P.sb([128, 32], F32, 'wend')
    Dtot = P.sb([128, nseq, 32], F32, 'Dtot')
    xdt = P.sb([128, 512], BF16, 'xdt')
    xw = P.sb([128, 512], BF16, 'xw')
    xD = P.sb([128, 512], BF16, 'xD')
    negm = P.sb([128, 128], F32, 'negm')
    dhb = P.sb([128, 512], F32, 'dhb')
    onesF = P.sb([128, 128], F32, 'onesF')
    e4 = [P.sb([128, 512], BF16, 'e4%d' % i) for i in range(2)]
    MT4 = [P.sb([128, 512], BF16, 'MT4%d' % i) for i in range(2)]
    nacs = P.sb([128, 32], F32, 'nacs')
    t1 = P.sb([128, 512], F32, 't1')
    gg = P.sb([128, 512], F32, 'gg')
    gn = P.sb([128, 512], BF16, 'gn')
    junk = [P.sb([128, D], BF16, 'junk0')] * 2
    hn = [P.sb([128, D], BF16, 'hn%d' % i) for i in range(2)]
    mtmp = P.sb([128, ntile, D], F32, 'mtmp')
    ssh = P.sb([128, 16], F32, 'ssh')
    tmp_t = [P.sb([128, D], F32, 'tmpt0')] * 2
    sout = P.sb([128, 4, 128], F32, 'sout')
    if not prompt:
        CTm = P.sb([128, NSS, NSTOK], BF16, 'CTm')
        Bm = P.sb([128, NSS, 128], BF16, 'Bm')
        sst = P.sb([128, 16, 128], F32, 'sst')
        cst_st = P.sb([128, 24, NSS, 3], F32, 'cst_st')

    P.op('sp', lambda e: e.dma_start(out=gw_bc[:], in_=k.gnorm_w[j:j + 1, :].to_broadcast([128, DIN])), w=['gw_bc'], dsem='gw_bc')
    P.op('sp', lambda e: e.dma_start(out=D_bc[:], in_=k.d_skip[j:j + 1, :].to_broadcast([128, 32])), w=['D_bc'], dsem='D_bc')
    with nc.allow_non_contiguous_dma(reason="small param relayout"):
        for kk in range(4):
            P.op('sp', lambda e, kk=kk: e.dma_start(out=cw[:, :, kk], in_=k.conv_w[j, kk].rearrange("(ft p) -> p ft", p=128), allow_slow_non_contiguous=True),
                 w=['cw'], dsem='cw', chain=False)
        P.op('sp', lambda e: e.dma_start(out=cbias[:], in_=k.conv_b[j].rearrange("(ft p) -> p ft", p=128), allow_slow_non_contiguous=True), w=['cbias'], dsem='cbias', chain=False)
        for half in range(2):
            P.op('sp', lambda e, half=half: e.dma_start(out=vec[32 * half:32 * half + 32, 0:1], in_=k.dt_bias[j].rearrange("(p o) -> p o", o=1), allow_slow_non_contiguous=True),
                 w=['vec'], dsem='vec', chain=False)
            P.op('sp', lambda e, half=half: e.dma_start(out=vec[32 * half:32 * half + 32, 1:2], in_=k.a_log[j].rearrange("(p o) -> p o", o=1), allow_slow_non_contiguous=True),
                 w=['vec'], dsem='vec', chain=False)
    P.op('act', lambda e: e.activation(out=vec[0:64, 2:3], in_=vec[0:64, 1:2], func=AF.Exp), r=['vec'], w=['vec2'])
    P.op('dve', lambda e: e.tensor_scalar(out=vec[0:64, 2:3], in0=vec[0:64, 2:3], scalar1=-1.0, scalar2=None, op0=ALU.mult), r=['vec2'], w=['vec2'])
    P.op('dve', lambda e: e.memset(ones[:], 1.0), w=['ones'])
    P.op('sp', lambda e: e.dma_start(out=negm[0:T, 0:T], in_=(k.c_negp if prompt else k.c_negs)[:, :]), w=['negm'], dsem='negm')
    P.op('dve', lambda e: e.memset(onesF[:], 1.0), w=['onesF'])
    if prompt:
        P.op('dve', lambda e: e.memset(xbcT[:, :, :, 0:4], 0.0), w=['xbcT'])
        P.op('dve', lambda e: e.memset(ST[0][:], 0.0), w=['ST0'])
    else:
        with nc.allow_non_contiguous_dma(reason="conv state relayout"):
            for b in range(NSS):
                for r_ in range(3):
                    P.op('sp', lambda e, b=b, r_=r_: e.dma_start(out=cst_st[:, :, b, r_], in_=k.state_conv[j, b, r_].rearrange("(ft p) -> p ft", p=128),
                                                                 allow_slow_non_contiguous=True), w=['cst_st'], dsem='cst_st', chain=False)
        P.op('dve', lambda e: e.tensor_copy(xbcT[:, :, :, 0:3], cst_st[:]), r=['cst_st'], w=['xbcT'])
        for b in range(NSS):
            sv = k.state_ssm[j, b].rearrange("h p n -> (h p) n").rearrange("(q r) n -> r q n", r=128)
            P.op('sp', lambda e, sv=sv: e.dma_start(out=sst[:], in_=sv), w=['sst'], dsem='sst')
            for gq in range(4):
                pb = 2 + gq % 2
                for qq in range(4):
                    P.op('pe', lambda e, gq=gq, qq=qq, pb=pb: e.transpose(k.ps[pb][:, qq * 128:(qq + 1) * 128], sst[:, gq * 4 + qq, :], k.identF[:]),
                         r=['sst', 'identF'], w=['ps%d' % pb])
                P.op('dve', lambda e, b=b, gq=gq, pb=pb: e.tensor_copy(ST[b][:, gq, :], k.ps[pb][:, :]), r=['ps%d' % pb], w=['ST%d' % b])

    wv_in = k.w_in_odd[j].rearrange("(kc p) n -> p kc n", p=128)
    wov = k.w_out_odd[j].rearrange("(kc p) n -> p kc n", p=128)
    maskc = cst['mask']
    for sb in range(nsb):
        tiles = k.ptiles[SBT * sb:SBT * sb + SBT] if prompt else [k.stile]
        last = sb == nsb - 1
        norm_transpose(k, tiles, 0, hT, 'hT', junk, hn)
        cnt = 0
        for cb in range(4):
            wb, wkey = next_wbuf(k)
            wv = wb[:].rearrange("p (kc n) -> p kc n", kc=8)
            wload(k, P, ('oddz', j, cb), wv, wkey, wv_in[:, :, ZOFF + cb * 512:ZOFF + (cb + 1) * 512])
            col = 0
            for ti, (X, i, np_, xkey) in enumerate(tiles):
                pb = cnt % 2
                cnt += 1
                for kc in range(8):
                    P.op('pe', lambda e, pb=pb, np_=np_, kc=kc, col=col, wv=wv: e.matmul(
                        k.ps[pb][0:np_, :], lhsT=hT[:, kc, col:col + np_], rhs=wv[:, kc, :], start=(kc == 0), stop=(kc == 7)),
                        r=[wkey, 'hT'], w=['ps%d' % pb])
                P.op('act', lambda e, pb=pb, np_=np_, ti=ti, cb=cb: e.activation(
                    out=zs[0:np_, ti, cb * 512:(cb + 1) * 512], in_=k.ps[pb][0:np_, :], func=AF.Silu), r=['ps%d' % pb], w=['zs'])
                col += np_
        for cb in range(6):
            wb, wkey = next_wbuf(k)
            wv = wb[:].rearrange("p (kc n) -> p kc n", kc=8)
            wload(k, P, ('oddx', j, cb), wv, wkey, wv_in[:, :, XOFF + cb * 512:XOFF + (cb + 1) * 512])
            for m in range(4):
                ft = cb * 4 + m
                pb = 2 + ft % 2
                for kc in range(8):
                    P.op('pe', lambda e, pb=pb, kc=kc, m=m, wv=wv: e.matmul(
                        k.ps[pb][:, 0:NCOL], lhsT=wv[:, kc, m * 128:(m + 1) * 128], rhs=hT[:, kc, 0:NCOL], start=(kc == 0), stop=(kc == 7)),
                        r=[wkey, 'hT'], w=['ps%d' % pb])
                src = k.ps[pb][:, 0:NCOL].rearrange("p (b t) -> p b t", b=nseq)
                if ft % 2 == 0:
                    P.op('dve', lambda e, ft=ft, src=src: e.tensor_copy(xbcT[:, ft, :, 3:3 + Lb], src), r=['ps%d' % pb], w=['xbcT'])
                else:
                    P.op('act', lambda e, ft=ft, src=src: e.activation(out=xbcT[:, ft, :, 3:3 + Lb], in_=src, func=AF.Copy), r=['ps%d' % pb], w=['xbcT'])
                if last:
                    P.op('dve', lambda e, ft=ft, src=src: e.tensor_copy(xlast[:, ft, :, :], src[:, :, Lb - 3:Lb]), r=['ps%d' % pb], w=['xlast'])
        wb, wkey = next_wbuf(k)
        wv = wb[:, 0:512].rearrange("p (kc n) -> p kc n", kc=8)
        for half in range(2):
            P.op('pool', lambda e, wv=wv, half=half: e.dma_start(out=wv[:, :, 32 * half:32 * half + 32], in_=wv_in[:, :, DTOFF:DTOFF + 32]),
                 w=[wkey], dsem=wkey, hoist=True)
        for kc in range(8):
            P.op('pe', lambda e, kc=kc, wv=wv: e.matmul(k.ps[4][0:64, 0:NCOL], lhsT=wv[:, kc, 0:64], rhs=hT[:, kc, 0:NCOL], start=(kc == 0), stop=(kc == 7)),
                 r=[wkey, 'hT'], w=['ps4'])
        P.op('act', lambda e: e.activation(out=PK[0:64, :], in_=k.ps[4][0:64, 0:NCOL], func=AF.Exp, bias=vec[0:64, 0:1]), r=['ps4', 'vec'], w=['PK'])
        P.op('act', lambda e: e.activation(out=PK[0:64, :], in_=PK[0:64, :], func=AF.Ln, bias=1.0), r=['PK'], w=['PK'])
        P.op('dve', lambda e: e.tensor_scalar(out=dA[32:64, :], in0=PK[32:64, :], scalar1=vec[32:64, 2:3], scalar2=None, op0=ALU.mult),
             r=['PK', 'vec2'], w=['dA'])
        slen = 128 if prompt else LS
        for s0 in range(0, NCOL, slen):
            P.op('dve', lambda e, s0=s0: e.tensor_tensor_scan(out=PK[32:64, s0:s0 + slen], data0=ones[32:64, 0:slen], data1=dA[32:64, s0:s0 + slen],
                                                             initial=0.0, op0=ALU.mult, op1=ALU.add), r=['dA', 'ones', 'PK'], w=['PK'])
        for c in range(nchunk):
            P.op('pe', lambda e, c=c: e.transpose(k.ps[5][0:T, 0:64], PK[0:64, c * T:(c + 1) * T], k.identF[0:64, 0:64]), r=['PK', 'identF'], w=['ps5'])
            P.op('dve', lambda e, c=c: e.tensor_copy(tokpk[0:T, c, :], k.ps[5][0:T, 0:64]), r=['ps5'], w=['tokpk'])
        for ft in range(24):
            a3 = accs[ft % 2][:, 0:NCOL].rearrange("p (b t) -> p b t", b=nseq)
            akey = 'cacc%d' % (ft % 2)
            P.op('pool', lambda e, a3=a3, ft=ft: e.tensor_scalar(out=a3, in0=xbcT[:, ft, :, 3:3 + Lb], scalar1=cw[:, ft, 3:4], scalar2=0.0,
                                                                 op0=ALU.mult, op1=ALU.add), r=['xbcT', 'cw'], w=[akey])
            for kk in (2, 1, 0):
                P.op('dve', lambda e, a3=a3, ft=ft, kk=kk: e.scalar_tensor_tensor(out=a3, in0=xbcT[:, ft, :, kk:kk + Lb], scalar=cw[:, ft, kk:kk + 1],
                                                                                   in1=a3, op0=ALU.mult, op1=ALU.add), r=['xbcT', 'cw', akey], w=[akey])
            if ft < 16:
                dst, dkey = xcT[ft % 2][:, 0:NCOL], 'xcT%d' % (ft % 2)
            elif ft < 20:
                dst, dkey = BT[:, ft - 16, 0:NCOL], 'BT'
            else:
                dst, dkey = CT[:, ft - 20, 0:NCOL], 'CT'
            P.op('act', lambda e, dst=dst, ft=ft: e.activation(out=dst, in_=accs[ft % 2][:, 0:NCOL], func=AF.Silu, bias=cbias[:, ft:ft + 1]),
                 r=[akey, 'cbias'], w=[dkey])
            if ft < 20:
                col = 0
                for ti, (X, i, np_, xkey) in enumerate(tiles):
                    P.op('pe', lambda e, ti=ti, np_=np_, col=col, dst=dst, ft=ft: e.transpose(
                        k.psb[ti][0:np_, (ft % 8) * 128:(ft % 8 + 1) * 128], dst[:, col:col + np_], k.ident[:, :]),
                        r=[dkey, 'ident'], w=['ps%d' % ti])
                    col += np_
                if ft % 8 == 7 or ft == 19:
                    for ti, (X, i, np_, xkey) in enumerate(tiles):
                        if ft < 16:
                            o_ap = x_tok[0:np_, ti, (ft // 8) * 1024:(ft // 8 + 1) * 1024]
                            i_ap = k.psb[ti][0:np_, :]
                            okey = 'x_tok'
                        else:
                            o_ap = B_tok[0:np_, ti, :]
                            i_ap = k.psb[ti][0:np_, 0:512]
                            okey = 'B_tok'
                        if ti % 2 == 0:
                            P.op('dve', lambda e, o_ap=o_ap, i_ap=i_ap: e.tensor_copy(o_ap, i_ap), r=['ps%d' % ti], w=[okey])
                        else:
                            P.op('act', lambda e, o_ap=o_ap, i_ap=i_ap: e.activation(out=o_ap, in_=i_ap, func=AF.Copy), r=['ps%d' % ti], w=[okey])
        if not last:
            P.op('dve', lambda e: e.tensor_copy(xbcT[:, :, :, 0:3], xbcT[:, :, :, Lb:Lb + 3]), r=['xbcT'], w=['xbcT'])
        for c in range(nchunk):
            ti = c if prompt else 0
            c0 = c * T
            dt_tok = tokpk[0:T, c, 0:32]
            acs = tokpk[0:T, c, 32:64]
            P.op('act', lambda e, acs=acs: e.activation(out=Et[0:T, :], in_=acs, func=AF.Exp), r=['tokpk'], w=['Et'])
            P.op('dve', lambda e, acs=acs: e.tensor_scalar(out=nacs[0:T, :], in0=acs, scalar1=-1.0, scalar2=None, op0=ALU.mult), r=['tokpk'], w=['nacs'])
            P.op('pe', lambda e, acs=acs: e.matmul(k.ps[5][0:T, 0:32], lhsT=cst['selend'][0:T, 0:T], rhs=acs, start=True, stop=True),
                 r=['tokpk', 'cst'], w=['ps5'])
            P.op('dve', lambda e, acs=acs: e.tensor_tensor(out=wend[0:T, :], in0=k.ps[5][0:T, 0:32], in1=acs, op=ALU.subtract), r=['ps5', 'tokpk'], w=['wend'])
            P.op('act', lambda e: e.activation(out=wend[0:T, :], in_=wend[0:T, :], func=AF.Exp), r=['wend'], w=['wend'])
            for b in range(nseq):
                P.op('pe', lambda e, acs=acs, b=b: e.matmul(k.ps[5][:, 64 + 32 * b:96 + 32 * b], lhsT=cst['selendB'][0:T, b, :], rhs=acs, start=True, stop=True),
                     r=['tokpk', 'cst'], w=['ps5'])
            P.op('act', lambda e: e.activation(out=Dtot[:, :, :], in_=k.ps[5][:, 64:64 + 32 * nseq].rearrange("p (b h) -> p b h", b=nseq), func=AF.Exp),
                 r=['ps5'], w=['Dtot'])
            psA = (3, 4)

            def stageA(gq, acs=acs):
                hs = gq * 8
                for q in range(2):
                    ab = psA[q]
                    dh3 = dhb[0:T, :].rearrange("p (j l) -> p j l", l=128)[:, :, 0:T]
                    P.op('dve', lambda e, dh3=dh3, q=q, hs=hs, acs=acs: e.tensor_tensor(
                        out=dh3, in0=k.identF[0:T, 0:T].unsqueeze(1).to_broadcast([T, 4, T]),
                        in1=acs[:, hs + 4 * q:hs + 4 * q + 4].unsqueeze(2).to_broadcast([T, 4, T]), op=ALU.mult), r=['identF', 'tokpk', 'dhb'], w=['dhb'])
                    if T == 128:
                        P.op('pe', lambda e, ab=ab: e.matmul(k.ps[ab][0:T, :], lhsT=onesF[0:T, 0:T], rhs=dhb[0:T, :], start=True, stop=False), r=['onesF', 'dhb'], w=['ps%d' % ab])
                    for jj in range(4):
                        o_ap = k.ps[ab][0:T, jj * 128:jj * 128 + T]
                        if T != 128:
                            P.op('pe', lambda e, o_ap=o_ap, jj=jj: e.matmul(o_ap, lhsT=onesF[0:T, 0:T], rhs=dhb[0:T, jj * 128:jj * 128 + T], start=True, stop=False),
                                 r=['onesF', 'dhb'], w=['ps%d' % ab])
                        P.op('pe', lambda e, o_ap=o_ap, jj=jj: e.matmul(o_ap, lhsT=k.identF[0:T, 0:T], rhs=negm[0:T, 0:T], start=False, stop=(T != 128 or jj == 3)),
                             r=['identF', 'negm'], w=['ps%d' % ab])

            for gq in range(4):
                hs = gq * 8
                xg = x_tok[0:T, ti, gq * 512:(gq + 1) * 512].rearrange("p (h d) -> p h d", h=8)
                P.op('dve', lambda e, xg=xg, dt_tok=dt_tok, hs=hs: e.tensor_tensor(
                    out=xdt[0:T, :].rearrange("p (h d) -> p h d", h=8), in0=xg, in1=dt_tok[:, hs:hs + 8].unsqueeze(2).to_broadcast([T, 8, 64]), op=ALU.mult),
                    r=['x_tok', 'tokpk'], w=['xdt'])
                P.op('pool', lambda e, hs=hs: e.tensor_tensor(
                    out=xw[0:T, :].rearrange("p (h d) -> p h d", h=8), in0=xdt[0:T, :].rearrange("p (h d) -> p h d", h=8),
                    in1=wend[0:T, hs:hs + 8].unsqueeze(2).to_broadcast([T, 8, 64]), op=ALU.mult), r=['xdt', 'wend'], w=['xw'])
                P.op('pool', lambda e, xg=xg, hs=hs: e.tensor_tensor(
                    out=xD[0:T, :].rearrange("p (h d) -> p h d", h=8), in0=xg, in1=D_bc[0:T, hs:hs + 8].unsqueeze(2).to_broadcast([T, 8, 64]), op=ALU.mult),
                    r=['x_tok', 'D_bc'], w=['xD'])
                P.op('pe', lambda e, gq=gq, c0=c0: e.matmul(k.ps[2][0:T, 0:T], lhsT=BT[:, gq, c0:c0 + T], rhs=CT[:, gq, c0:c0 + T], start=True, stop=True),
                     r=['BT', 'CT'], w=['ps2'])
                P.op('pe', lambda e: e.matmul(k.ps[0][0:T, :], lhsT=k.ident[0:T, 0:T], rhs=xD[0:T, :], start=True, stop=False), r=['ident', 'xD'], w=['ps0'])
                if gq == 0:
                    stageA(0)
                for jh in range(8):
                    h = hs + jh
                    ab = psA[jh // 4]
                    q = jh // 4
                    P.op('act', lambda e, h=h, ab=ab, q=q, jh=jh: e.activation(out=e4[q][0:T, (jh % 4) * 128:(jh % 4) * 128 + T],
                                                                             in_=k.ps[ab][0:T, (jh % 4) * 128:(jh % 4) * 128 + T], func=AF.Exp, bias=nacs[0:T, h:h + 1]),
                         r=['ps%d' % ab, 'nacs'], w=['e4%d' % q])
                if gq < 3:
                    stageA(gq + 1)
                for q in range(2):
                    P.op('dve', lambda e, q=q: e.tensor_tensor(
                        out=MT4[q][0:T, :].rearrange("p (j l) -> p j l", l=128)[:, :, 0:T], in0=e4[q][0:T, :].rearrange("p (j l) -> p j l", l=128)[:, :, 0:T],
                        in1=k.ps[2][0:T, 0:T].unsqueeze(1).to_broadcast([T, 4, T]), op=ALU.mult), r=['e4%d' % q, 'ps2'], w=['MT4%d' % q])
                for jh in range(8):
                    q = jh // 4
                    P.op('pe', lambda e, q=q, jh=jh: e.matmul(k.ps[0][0:T, jh * 64:(jh + 1) * 64], lhsT=MT4[q][0:T, (jh % 4) * 128:(jh % 4) * 128 + T],
                                                             rhs=xdt[0:T, jh * 64:(jh + 1) * 64], start=False, stop=(jh == 7)), r=['MT4%d' % q, 'xdt'], w=['ps0'])
                def st_bf16(b, gq):
                    sl_ = stctr[0] % 2
                    stctr[0] += 1
                    P.op('act', lambda e, b=b, gq=gq, sl_=sl_: e.activation(out=STb[sl_][:, :], in_=ST[b][:, gq, :], func=AF.Copy),
                         r=['ST%d' % b], w=['STbs%d' % sl_])
                    return STb[sl_], 'STbs%d' % sl_
                if prompt:
                    stb, stk = st_bf16(0, gq)
                    P.op('pe', lambda e, gq=gq, c0=c0, stb=stb: e.matmul(k.ps[1][0:T, :], lhsT=CT[:, gq, c0:c0 + T], rhs=stb[:, :], start=True, stop=True),
                         r=['CT', stk], w=['ps1'])
                else:
                    P.op('dve', lambda e, gq=gq: e.tensor_tensor(out=CTm[:, :, :], in0=CT[:, gq, 0:NSTOK].unsqueeze(1).to_broadcast([128, NSS, NSTOK]),
                                                                 in1=cst['seqcol'][:, :, :], op=ALU.mult), r=['CT', 'cst'], w=['CTm'])
                    for b in range(NSS):
                        stb, stk = st_bf16(b, gq)
                        P.op('pe', lambda e, gq=gq, b=b, stb=stb: e.matmul(k.ps[1][0:T, :], lhsT=CTm[:, b, :], rhs=stb[:, :], start=(b == 0), stop=(b == NSS - 1)),
                             r=['CTm', stk], w=['ps1'])
                P.op('dve', lambda e, hs=hs: e.tensor_tensor(out=t1[0:T, :].rearrange("p (h d) -> p h d", h=8), in0=k.ps[1][0:T, :].rearrange("p (h d) -> p h d", h=8),
                                                            in1=Et[0:T, hs:hs + 8].unsqueeze(2).to_broadcast([T, 8, 64]), op=ALU.mult), r=['ps1', 'Et'], w=['t1'])
                P.op('dve', lambda e: e.tensor_tensor(out=t1[0:T, :], in0=t1[0:T, :], in1=k.ps[0][0:T, :], op=ALU.add), r=['t1', 'ps0'], w=['t1'])
                P.op('pool', lambda e, ti=ti, gq=gq: e.tensor_tensor(out=gg[0:T, :], in0=t1[0:T, :], in1=zs[0:T, ti, gq * 512:(gq + 1) * 512], op=ALU.mult),
                     r=['t1', 'zs'], w=['gg'])
                s_ = small_slot(k)
                sk = 'sm%d' % s_
                ss = k.small[0:T, s_:s_ + 1]
                rs = k.small[0:T, s_ + 1:s_ + 2]
                P.op('act', lambda e, ss=ss: e.activation(out=junk[0][0:T, 0:512], in_=gg[0:T, :], func=AF.Square, accum_out=ss), r=['gg'], w=['junk0', sk])
                P.op('act', lambda e, ss=ss, rs=rs: e.activation(out=rs, in_=ss, func=AF.Ln, scale=1.0 / 512, bias=EPS), r=[sk], w=[sk + 'r'])
                P.op('act', lambda e, rs=rs: e.activation(out=rs, in_=rs, func=AF.Exp, scale=-0.5), r=[sk + 'r'], w=[sk + 'r'])
                P.op('dve', lambda e, rs=rs, gq=gq: e.scalar_tensor_tensor(out=gn[0:T, :], in0=gg[0:T, :], scalar=rs, in1=gw_bc[0:T, gq * 512:(gq + 1) * 512],
                                                                          op0=ALU.mult, op1=ALU.mult), r=['gg', sk + 'r', 'gw_bc'], w=['gn'])
                pb = 6 + gq % 2
                for q in range(4):
                    P.op('pe', lambda e, q=q, pb=pb: e.transpose(k.psb[pb][:, q * 128:q * 128 + T], gn[0:T, q * 128:(q + 1) * 128], k.ident[0:T, 0:T]),
                         r=['gn', 'ident'], w=['ps%d' % pb])
                P.op('act', lambda e, gq=gq, c0=c0, pb=pb: e.activation(out=gnT[:, gq * 4:gq * 4 + 4, c0:c0 + T],
                                                                      in_=k.psb[pb][:, 0:512].rearrange("p (q t) -> p q t", q=4)[:, :, 0:T], func=AF.Copy),
                     r=['ps%d' % pb], w=['gnT'])
                for b in range(nseq):
                    if prompt:
                        lhs = B_tok[0:T, ti, gq * 128:(gq + 1) * 128]
                        lkey = 'B_tok'
                    else:
                        P.op('dve', lambda e, b=b, gq=gq: e.tensor_scalar(out=Bm[0:T, b, :], in0=B_tok[0:T, 0, gq * 128:(gq + 1) * 128],
                                                                         scalar1=cst['seqrow'][0:T, b:b + 1], scalar2=None, op0=ALU.mult),
                             r=['B_tok', 'cst'], w=['Bm%d' % b])
                        lhs = Bm[0:T, b, :]
                        lkey = 'Bm%d' % b
                    P.op('pe', lambda e, lhs=lhs: e.matmul(k.ps[5][:, :], lhsT=lhs, rhs=xw[0:T, :], start=True, stop=True), r=[lkey, 'xw'], w=['ps5'])
                    stv = ST[b][:, gq, :].rearrange("p (h d) -> p h d", h=8)
                    P.op('pool', lambda e, stv=stv, b=b, hs=hs: e.tensor_tensor(out=stv, in0=stv, in1=Dtot[:, b, hs:hs + 8].unsqueeze(2).to_broadcast([128, 8, 64]), op=ALU.mult),
                         r=['ST%d' % b, 'Dtot'], w=['ST%d' % b])
                    P.op('dve', lambda e, b=b, gq=gq: e.tensor_tensor(out=ST[b][:, gq, :], in0=ST[b][:, gq, :], in1=k.ps[5][:, :], op=ALU.add),
                         r=['ST%d' % b, 'ps5'], w=['ST%d' % b])
        proj_out(k, tiles, gnT, 'gnT', wov, 16, mtmp, ssh, junk, wtag=('oddo', j))
        post_norm_add(k, tiles, 1, mtmp, ssh, tmp_t)
    with nc.allow_non_contiguous_dma(reason="conv state relayout"):
        for b in range(nseq):
            for r_ in range(3):
                dst = (conv_out[j, r_] if prompt else conv_out[j, b, r_]).rearrange("(ft p) -> p ft", p=128)
                P.op('sp', lambda e, dst=dst, b=b, r_=r_: e.dma_start(out=dst, in_=xlast[:, :, b, r_], allow_slow_non_contiguous=True), r=['xlast'], dsem='xlast', chain=False)
    for b in range(nseq):
        dv = (ssm_out[j] if prompt else ssm_out[j, b]).rearrange("h p n -> (h p) n")
        for gq in range(4):
            pb = 2 + gq % 2
            for qq in range(4):
                P.op('pe', lambda e, b=b, gq=gq, qq=qq, pb=pb: e.transpose(k.ps[pb][:, qq * 128:(qq + 1) * 128], ST[b][:, gq, qq * 128:(qq + 1) * 128], k.identF[:]),
                     r=['ST%d' % b, 'identF'], w=['ps%d' % pb])
            P.op('dve', lambda e, pb=pb: e.tensor_copy(sout[:].rearrange("p q n -> p (q n)"), k.ps[pb][:, :]), r=['ps%d' % pb], w=['sout'])
            P.op('sp', lambda e, dv=dv, gq=gq: e.dma_start(out=dv[gq * 512:(gq + 1) * 512, :].rearrange("(q r) n -> r q n", r=128), in_=sout[:]),
                 r=['sout'], dsem='sout')


TWO_PI = 6.283185307179586
GELU_C = 0.044715
GELU_S = 1.5957691216057308
I32 = mybir.dt.int32


def s5_phase(k, l, j, grp):
    P = k.P
    P.phase_begin()
    prompt = grp == 'p'
    nseq = 1 if prompt else NSS
    Lseq = SEQ if prompt else LS
    NTOK = nseq * Lseq
    Lh = 512 if prompt else LS
    nsegs = Lseq // Lh
    SC = nseq * Lh if not prompt else Lh
    nstep = NTOK // SC
    o_bT = P.sb([128, 4, NTOK], BF16, 'o_bT')
    k.o_bT = o_bT
    k.o_bT_bytes = P.cur - P.pers
    phase_common(k, l, (0,), nwbuf=1)
    ncolb = 512 if prompt else NSTOK
    hT = P.sb([128, 8, ncolb], BF16, 'hT')
    uT = P.sb([128, 4, NTOK], BF16, 'uT')
    junk = [P.sb([128, D], BF16, 'junk0')] * 2
    hn = [P.sb([128, D], BF16, 'hn0')] * 2
    LR = P.sb([128, 32], F32, 'LR')
    LI = P.sb([128, 32], F32, 'LI')
    DT = P.sb([128, 32], F32, 'DT')
    RHO = P.sb([128, 32], F32, 'RHO')
    TH = P.sb([128, 32], F32, 'TH')
    K1 = P.sb([128, 32], F32, 'K1')
    K2 = P.sb([128, 32], F32, 'K2')
    K1s = P.sb([128, 32], F32, 'K1s')
    K2s = P.sb([128, 32], F32, 'K2s')
    pt = [P.sb([128, 32], F32, 'pt%d' % i) for i in range(8)]
    pti = P.sb([128, 32], I32, 'pti')
    X1 = P.sb([128, 32, 16], F32, 'X1')
    X2 = P.sb([128, 32, 16], F32, 'X2')
    BB = P.sb([128, 32, 16], F32, 'BB')
    BBs = P.sb([128, 32, 16], F32, 'BBs')
    Bblk = P.sb([128, 32, 128], BF16, 'Bblk')
    Bblks = P.sb([128, 32, 128], BF16, 'Bblks')
    Cnat = P.sb([128, 4, 2, 64], F32, 'Cnat')
    CC = P.sb([128, 4, 128], F32, 'CC')
    Cblk = P.sb([128, 32, 128], BF16, 'Cblk')
    Dv = P.sb([128, 4], F32, 'Dv')
    bglu = P.sb([128, 4], F32, 'bglu')
    H0 = P.sb([128, nseq, 32], F32, 'H0')
    Hlast = P.sb([128, nseq, 32], F32, 'Hlast')
    carry = P.sb([128, 2], F32, 'carry')
    COS = P.sb([128, Lh], F32, 'COS')
    SIN = P.sb([128, Lh], F32, 'SIN')
    RH = P.sb([128, Lh], F32, 'RH')
    phi = P.sb([128, Lh], F32, 'phi')
    qi = P.sb([128, Lh], I32, 'qi')
    qf = P.sb([128, Lh], F32, 'qf')
    s2 = P.sb([128, Lh], F32, 's2')
    s4 = P.sb([128, Lh], F32, 's4')
    a1 = P.sb([128, SC], F32, 'a1')
    a2 = P.sb([128, SC], F32, 'a2')
    Sp = P.sb([128, SC], F32, 'Sp')
    gb = P.sb([128, SC], BF16, 'gb')
    Hb = P.sb([128, SC], BF16, 'Hb')
    gl = [P.sb([128, ncolb], F32, 'gl%d' % i) for i in range(2)]
    swp = k.swp

    for half in range(2):
        rows = slice(64 * half, 64 * half + 64)
        P.op('sp', lambda e, rows=rows: e.dma_start(out=LR[rows, :], in_=k.s5_lambda_re[j].rearrange("g p -> p g"), allow_slow_non_contiguous=True), w=['LR'], dsem='s5p', chain=False)
        P.op('sp', lambda e, rows=rows: e.dma_start(out=LI[rows, :], in_=k.s5_lambda_im[j].rearrange("g p -> p g"), allow_slow_non_contiguous=True), w=['LI'], dsem='s5p', chain=False)
        bre = k.s5_b_re[j].rearrange("g p c -> p g c")
        bim = k.s5_b_im[j].rearrange("g p c -> p g c")
        P.op('sp', lambda e, rows=rows, half=half, bre=bre, bim=bim: e.dma_start(out=X1[rows, :, :], in_=(bre if half == 0 else bim)), w=['X1'], dsem='s5p', chain=False)
        P.op('sp', lambda e, rows=rows, half=half, bre=bre, bim=bim: e.dma_start(out=X2[rows, :, :], in_=(bim if half == 0 else bre)), w=['X2'], dsem='s5p', chain=False)
    P.op('sp', lambda e: e.dma_start(out=DT[:], in_=k.s5_log_dt[j:j + 1, :].to_broadcast([128, 32])), w=['DT'], dsem='s5p', chain=False)
    P.op('sp', lambda e: e.dma_start(out=Cnat[:, :, 0, :], in_=k.s5_c_re[j].rearrange("g c p -> (g c) p").rearrange("(m r) p -> r m p", r=128)), w=['Cnat'], dsem='s5p', chain=False)
    P.op('sp', lambda e: e.dma_start(out=Cnat[:, :, 1, :], in_=k.s5_c_im[j].rearrange("g c p -> (g c) p").rearrange("(m r) p -> r m p", r=128)), w=['Cnat'], dsem='s5p', chain=False)
    P.op('sp', lambda e: e.dma_start(out=Dv[:], in_=k.s5_d[j].rearrange("g c -> (g c)").rearrange("(m r) -> r m", r=128), allow_slow_non_contiguous=True), w=['Dv'], dsem='s5p', chain=False)
    P.op('sp', lambda e: e.dma_start(out=bglu[:], in_=k.s5_b_glu[j].rearrange("(m r) -> r m", r=128), allow_slow_non_contiguous=True), w=['bglu'], dsem='s5p', chain=False)
    if prompt:
        P.op('dve', lambda e: e.memset(H0[:], 0.0), w=['H0'])
    else:
        for b in range(NSS):
            for ri in range(2):
                P.op('sp', lambda e, b=b, ri=ri: e.dma_start(out=H0[64 * ri:64 * ri + 64, b, :], in_=k.state_s5[j, b].rearrange("g p r -> p g r")[:, :, ri],
                                                             allow_slow_non_contiguous=True), w=['H0'], dsem='s5p', chain=False)

    def reduce_sincos(eng_note, ang, n, qi_, qf_, r_, s2_, s4_, cos_out, sin_out, keys):
        P.op('dve', lambda e: e.tensor_scalar(out=qi_, in0=ang, scalar1=1.0 / TWO_PI, scalar2=None, op0=ALU.mult), r=keys['ang'], w=keys['qi'])
        P.op('dve', lambda e: e.tensor_copy(qf_, qi_), r=keys['qi'], w=keys['qf'])
        P.op('dve', lambda e: e.scalar_tensor_tensor(out=r_, in0=qf_, scalar=-TWO_PI, in1=ang, op0=ALU.mult, op1=ALU.add), r=keys['qf'] + keys['ang'], w=keys['r'])
        P.op('act', lambda e: e.activation(out=s4_, in_=r_, func=AF.Sin, scale=0.25), r=keys['r'], w=keys['s4'])
        P.op('act', lambda e: e.activation(out=s2_, in_=r_, func=AF.Sin, scale=0.5), r=keys['r'], w=keys['s2'])
        P.op('pool', lambda e: e.tensor_tensor(out=s4_, in0=s4_, in1=s4_, op=ALU.mult), r=keys['s4'], w=keys['s4'])
        P.op('pool', lambda e: e.tensor_scalar(out=s4_, in0=s4_, scalar1=-2.0, scalar2=1.0, op0=ALU.mult, op1=ALU.add), r=keys['s4'], w=keys['s4'])
        P.op('dve', lambda e: e.scalar_tensor_tensor(out=sin_out, in0=s2_, scalar=2.0, in1=s4_, op0=ALU.mult, op1=ALU.mult), r=keys['s2'] + keys['s4'], w=keys['sin'])
        P.op('pool', lambda e: e.tensor_tensor(out=s2_, in0=s2_, in1=s2_, op=ALU.mult), r=keys['s2'] + keys['sin'], w=keys['s2'])
        P.op('pool', lambda e: e.tensor_scalar(out=cos_out, in0=s2_, scalar1=-2.0, scalar2=1.0, op0=ALU.mult, op1=ALU.add), r=keys['s2'], w=keys['cos'])

    P.op('act', lambda e: e.activation(out=DT[:], in_=DT[:], func=AF.Exp), r=['DT'], w=['DT'])
    P.op('dve', lambda e: e.tensor_tensor(out=pt[0][:], in0=LR[:], in1=DT[:], op=ALU.mult), r=['LR', 'DT'], w=['pt0'])
    P.op('dve', lambda e: e.tensor_tensor(out=TH[:], in0=LI[:], in1=DT[:], op=ALU.mult), r=['LI', 'DT'], w=['TH'])
    P.op('act', lambda e: e.activation(out=RHO[:], in_=pt[0][:], func=AF.Exp), r=['pt0'], w=['RHO'])
    kk = {'ang': ['TH'], 'qi': ['pti'], 'qf': ['pt1'], 'r': ['pt2'], 's4': ['pt3'], 's2': ['pt4'], 'sin': ['pt5'], 'cos': ['pt6']}
    reduce_sincos('p', TH[:], 32, pti[:], pt[1][:], pt[2][:], pt[4][:], pt[3][:], pt[6][:], pt[5][:], kk)
    P.op('dve', lambda e: e.tensor_tensor(out=pt[0][:], in0=RHO[:], in1=pt[6][:], op=ALU.mult), r=['RHO', 'pt6'], w=['pt0'])
    P.op('dve', lambda e: e.tensor_scalar(out=pt[0][:], in0=pt[0][:], scalar1=-1.0, scalar2=None, op0=ALU.add), r=['pt0'], w=['pt0'])
    P.op('dve', lambda e: e.tensor_tensor(out=pt[1][:], in0=RHO[:], in1=pt[5][:], op=ALU.mult), r=['RHO', 'pt5', 'pt1'], w=['pt1'])
    P.op('dve', lambda e: e.tensor_tensor(out=pt[2][:], in0=LR[:], in1=LR[:], op=ALU.mult), r=['LR', 'pt2'], w=['pt2'])
    P.op('dve', lambda e: e.tensor_tensor(out=pt[3][:], in0=LI[:], in1=LI[:], op=ALU.mult), r=['LI', 'pt3'], w=['pt3'])
    P.op('dve', lambda e: e.tensor_tensor(out=pt[2][:], in0=pt[2][:], in1=pt[3][:], op=ALU.add), r=['pt2', 'pt3'], w=['pt2'])
    P.op('dve', lambda e: e.reciprocal(out=pt[2][:], in_=pt[2][:]), r=['pt2'], w=['pt2'])
    P.op('dve', lambda e: e.tensor_tensor(out=pt[3][:], in0=pt[0][:], in1=LR[:], op=ALU.mult), r=['pt0', 'LR', 'pt3'], w=['pt3'])
    P.op('dve', lambda e: e.tensor_tensor(out=pt[4][:], in0=pt[1][:], in1=LI[:], op=ALU.mult), r=['pt1', 'LI', 'pt4'], w=['pt4'])
    P.op('dve', lambda e: e.tensor_tensor(out=pt[3][:], in0=pt[3][:], in1=pt[4][:], op=ALU.add), r=['pt3', 'pt4'], w=['pt3'])
    P.op('dve', lambda e: e.tensor_tensor(out=K1[:], in0=pt[3][:], in1=pt[2][:], op=ALU.mult), r=['pt3', 'pt2'], w=['K1'])
    P.op('dve', lambda e: e.tensor_tensor(out=pt[3][:], in0=pt[1][:], in1=LR[:], op=ALU.mult), r=['pt1', 'LR', 'pt3'], w=['pt3'])
    P.op('dve', lambda e: e.tensor_tensor(out=pt[4][:], in0=pt[0][:], in1=LI[:], op=ALU.mult), r=['pt0', 'LI', 'pt4'], w=['pt4'])
    P.op('dve', lambda e: e.tensor_tensor(out=pt[3][:], in0=pt[3][:], in1=pt[4][:], op=ALU.subtract), r=['pt3', 'pt4'], w=['pt3'])
    P.op('dve', lambda e: e.tensor_tensor(out=pt[7][:], in0=pt[3][:], in1=pt[2][:], op=ALU.mult), r=['pt3', 'pt2'], w=['pt7'])
    P.op('dve', lambda e: e.tensor_scalar(out=K2[0:64, :], in0=pt[7][0:64, :], scalar1=-1.0, scalar2=None, op0=ALU.mult), r=['pt7'], w=['K2'])
    P.op('dve', lambda e: e.tensor_copy(K2[64:128, :], pt[7][64:128, :]), r=['pt7'], w=['K2'])
    P.op('dve', lambda e: e.tensor_scalar(out=K1s[0:64, :], in0=K1[0:64, :], scalar1=-1.0, scalar2=None, op0=ALU.mult), r=['K1'], w=['K1s'])
    P.op('dve', lambda e: e.tensor_copy(K1s[64:128, :], K1[64:128, :]), r=['K1'], w=['K1s'])
    P.op('dve', lambda e: e.tensor_scalar(out=K2s[:], in0=pt[7][:], scalar1=-1.0, scalar2=None, op0=ALU.mult), r=['pt7'], w=['K2s'])

    def bc(t):
        return t[:].unsqueeze(2).to_broadcast([128, 32, 16])
    P.op('dve', lambda e: e.tensor_tensor(out=BB[:], in0=X1[:], in1=bc(K1), op=ALU.mult), r=['X1', 'K1'], w=['BB'])
    P.op('dve', lambda e: e.tensor_tensor(out=BBs[:], in0=X2[:], in1=bc(K2), op=ALU.mult), r=['X2', 'K2'], w=['BBs'])
    P.op('dve', lambda e: e.tensor_tensor(out=BB[:], in0=BB[:], in1=BBs[:], op=ALU.add), r=['BB', 'BBs'], w=['BB'])
    P.op('dve', lambda e: e.tensor_tensor(out=BBs[:], in0=X2[:], in1=bc(K1s), op=ALU.mult), r=['X2', 'K1s', 'BBs'], w=['BBs'])
    P.op('dve', lambda e: e.tensor_tensor(out=X2[:], in0=X1[:], in1=bc(K2s), op=ALU.mult), r=['X1', 'K2s', 'X2'], w=['X2'])
    P.op('dve', lambda e: e.tensor_tensor(out=BBs[:], in0=BBs[:], in1=X2[:], op=ALU.add), r=['BBs', 'X2'], w=['BBs'])
    P.op('pool', lambda e: e.memset(Cblk[:], 0.0), w=['Cblk'])
    for src, dstt, dkey in ((BB, Bblk, 'Bblk'), (BBs, Bblks, 'Bblks')):
        for m in range(4):
            pb = m % 2
            P.op('pe', lambda e, src=src, m=m, pb=pb: e.transpose(k.ps[pb][:, 0:128], src[:, 8 * m:8 * m + 8, :].rearrange("p g c -> p (g c)"), k.identF[:]),
                 r=[('BB' if src is BB else 'BBs'), 'identF'], w=['ps%d' % pb])
            for gl_ in range(8):
                P.op('act', lambda e, dstt=dstt, m=m, gl_=gl_, pb=pb: e.activation(out=dstt[:, 8 * m + gl_, :], in_=k.ps[pb][:, 0:128], func=AF.Copy,
                                                                                  scale=k.rowmask[:, gl_:gl_ + 1]), r=['ps%d' % pb, 'rowmask'], w=[dkey])
    for m in range(4):
        pb = 2 + m % 2
        P.op('pe', lambda e, m=m, pb=pb: e.transpose(k.ps[pb][:, 0:128], Cnat[:, m, :, :].rearrange("p r q -> p (r q)"), k.identF[:]), r=['Cnat', 'identF'], w=['ps%d' % pb])
        P.op('act', lambda e, m=m, pb=pb: e.activation(out=CC[0:64, m, :], in_=k.ps[pb][0:64, 0:128], func=AF.Copy), r=['ps%d' % pb], w=['CC'])
        P.op('act', lambda e, m=m, pb=pb: e.activation(out=CC[64:128, m, :], in_=k.ps[pb][64:128, 0:128], func=AF.Copy, scale=-1.0), r=['ps%d' % pb], w=['CC'])
        for gl_ in range(8):
            P.op('dve', lambda e, m=m, gl_=gl_: e.tensor_copy(Cblk[:, 8 * m + gl_, 16 * gl_:16 * gl_ + 16], CC[:, m, 16 * gl_:16 * gl_ + 16]), r=['CC'], w=['Cblk'])
    wv_in = k.w_in_even[j].rearrange("(kc p) n -> p kc n", p=128)
    nblk = NTOK // ncolb
    for bi in range(nblk):
        tiles = k.ptiles[4 * bi:4 * bi + 4] if prompt else [k.stile]
        norm_transpose(k, tiles, 0, hT, 'hT', junk, hn)
        wb, wkey = next_wbuf(k)
        wv = wb[:].rearrange("p (kc n) -> p kc n", kc=8)
        wload(k, P, ('evu', j), wv, wkey, wv_in[:, :, 1536:2048])
        for m in range(4):
            pb = m % 2
            for kc in range(8):
                P.op('pe', lambda e, pb=pb, kc=kc, m=m, wv=wv: e.matmul(k.ps[pb][:, 0:ncolb], lhsT=wv[:, kc, m * 128:(m + 1) * 128], rhs=hT[:, kc, 0:ncolb],
                                                                       start=(kc == 0), stop=(kc == 7)), r=[wkey, 'hT'], w=['ps%d' % pb])
            if m % 2 == 0:
                P.op('act', lambda e, pb=pb, m=m, bi=bi: e.activation(out=uT[:, m, bi * ncolb:(bi + 1) * ncolb], in_=k.ps[pb][:, 0:ncolb], func=AF.Copy), r=['ps%d' % pb], w=['uT'])
            else:
                P.op('dve', lambda e, pb=pb, m=m, bi=bi: e.tensor_copy(uT[:, m, bi * ncolb:(bi + 1) * ncolb], k.ps[pb][:, 0:ncolb]), r=['ps%d' % pb], w=['uT'])
    psy = [4, 5, 6, 7]
    for g in range(32):
        m = g // 8
        P.op('dve', lambda e, g=g: e.tensor_scalar(out=phi[:], in0=k.iota1[:, 0:Lh], scalar1=TH[:, g:g + 1], scalar2=None, op0=ALU.mult), r=['iota1', 'TH'], w=['phi'])
        kk = {'ang': ['phi'], 'qi': ['qi'], 'qf': ['qf'], 'r': ['qf'], 's4': ['s4'], 's2': ['s2'], 'sin': ['SIN'], 'cos': ['COS']}
        reduce_sincos('g', phi[:], Lh, qi[:], qf[:], qf[:], s2[:], s4[:], COS[:], SIN[:], kk)
        P.op('pool', lambda e, g=g: e.tensor_scalar(out=RH[:], in0=k.iota1[:, 0:Lh], scalar1=0.0, scalar2=RHO[:, g:g + 1], op0=ALU.mult, op1=ALU.add), r=['iota1', 'RHO'], w=['RH'])
        cosb = COS[:, 0:Lh].unsqueeze(1).to_broadcast([128, SC // Lh, Lh])
        sinb = SIN[:, 0:Lh].unsqueeze(1).to_broadcast([128, SC // Lh, Lh])

        def v3(ap):
            return ap.rearrange("p (b t) -> p b t", t=Lh)

        def s_mm(st, g=g, m=m):
            pa, pb_ = (0, 1) if st % 2 == 0 else (2, 3)
            c0 = st * SC
            P.op('pe', lambda e, c0=c0, pa=pa: e.matmul(k.ps[pa][:, 0:SC], lhsT=Bblk[:, g, :], rhs=uT[:, m, c0:c0 + SC], start=True, stop=True), r=['Bblk', 'uT'], w=['ps%d' % pa])
            P.op('pe', lambda e, c0=c0, pb_=pb_: e.matmul(k.ps[pb_][:, 0:SC], lhsT=Bblks[:, g, :], rhs=uT[:, m, c0:c0 + SC], start=True, stop=True), r=['Bblks', 'uT'], w=['ps%d' % pb_])
        s_mm(0)
        for st in range(nstep):
            pa, pb_ = (0, 1) if st % 2 == 0 else (2, 3)
            P.op('dve', lambda e, pa=pa: e.tensor_tensor(out=v3(a1[:, 0:SC]), in0=v3(k.ps[pa][:, 0:SC]), in1=cosb, op=ALU.mult), r=['ps%d' % pa, 'COS'], w=['a1'])
            P.op('dve', lambda e, pb_=pb_: e.tensor_tensor(out=v3(a2[:, 0:SC]), in0=v3(k.ps[pb_][:, 0:SC]), in1=sinb, op=ALU.mult), r=['ps%d' % pb_, 'SIN'], w=['a2'])
            P.op('dve', lambda e: e.tensor_tensor(out=Sp[:, 0:SC], in0=a1[:, 0:SC], in1=a2[:, 0:SC], op=ALU.subtract), r=['a1', 'a2'], w=['Sp'])
            if st + 1 < nstep:
                s_mm(st + 1)
            for b in range(SC // Lh):
                if prompt:
                    init = H0[:, 0, g:g + 1] if st == 0 else carry[:, 0:1]
                    ikey = 'H0' if st == 0 else 'carry'
                else:
                    init = H0[:, b, g:g + 1]
                    ikey = 'H0'
                P.op('dve', lambda e, b=b, init=init: e.tensor_tensor_scan(out=gb[:, b * Lh:(b + 1) * Lh], data0=RH[:, 0:Lh], data1=Sp[:, b * Lh:(b + 1) * Lh],
                                                                           initial=init, op0=ALU.mult, op1=ALU.add), r=['RH', 'Sp', ikey], w=['gb'])
            P.op('pe', lambda e, pa=pa: e.matmul(k.ps[pa][:, 0:SC], lhsT=swp[:, :], rhs=gb[:, 0:SC], start=True, stop=True), r=['gb', 'swp'], w=['ps%d' % pa])
            P.op('pool', lambda e: e.tensor_tensor(out=v3(a2[:, 0:SC]), in0=v3(gb[:, 0:SC]), in1=cosb, op=ALU.mult), r=['gb', 'COS'], w=['a2'])
            P.op('dve', lambda e, pa=pa: e.tensor_tensor(out=v3(a1[:, 0:SC]), in0=v3(k.ps[pa][:, 0:SC]), in1=sinb, op=ALU.mult), r=['ps%d' % pa, 'SIN'], w=['a1'])
            P.op('dve', lambda e: e.tensor_tensor(out=Hb[:, 0:SC], in0=a1[:, 0:SC], in1=a2[:, 0:SC], op=ALU.add), r=['a1', 'a2'], w=['Hb'])
            if prompt:
                if st < nstep - 1:
                    P.op('dve', lambda e: e.tensor_tensor(out=carry[:, 0:1], in0=a1[:, SC - 1:SC], in1=a2[:, SC - 1:SC], op=ALU.add), r=['a1', 'a2'], w=['carry'])
                else:
                    P.op('dve', lambda e, g=g: e.tensor_tensor(out=Hlast[:, 0, g:g + 1], in0=a1[:, SC - 1:SC], in1=a2[:, SC - 1:SC], op=ALU.add), r=['a1', 'a2'], w=['Hlast'])
            else:
                P.op('dve', lambda e, g=g: e.tensor_tensor(out=Hlast[:, :, g:g + 1], in0=v3(a1[:, 0:SC])[:, :, Lh - 1:Lh], in1=v3(a2[:, 0:SC])[:, :, Lh - 1:Lh], op=ALU.add),
                     r=['a1', 'a2'], w=['Hlast'])
            P.op('pe', lambda e, g=g, st=st: e.matmul(k.ps[psy[st]][:, 0:SC], lhsT=Cblk[:, g, :], rhs=Hb[:, 0:SC], start=(g % 8 == 0), stop=(g % 8 == 7)),
                 r=['Cblk', 'Hb'], w=['ps%d' % psy[st]])
        if g % 8 == 7:
            for st in range(nstep):
                c0 = st * SC
                ga, gbk = gl[0][:, 0:SC], gl[1][:, 0:SC]
                P.op('dve', lambda e, m=m, st=st, c0=c0, ga=ga: e.scalar_tensor_tensor(out=ga, in0=uT[:, m, c0:c0 + SC], scalar=Dv[:, m:m + 1], in1=k.ps[psy[st]][:, 0:SC],
                                                                                      op0=ALU.mult, op1=ALU.add), r=['uT', 'Dv', 'ps%d' % psy[st]], w=['gl0'])
                P.op('pool', lambda e, ga=ga, gbk=gbk: e.tensor_tensor(out=gbk, in0=ga, in1=ga, op=ALU.mult), r=['gl0'], w=['gl1'])
                P.op('pool', lambda e, gbk=gbk: e.tensor_scalar(out=gbk, in0=gbk, scalar1=GELU_C, scalar2=1.0, op0=ALU.mult, op1=ALU.add), r=['gl1'], w=['gl1'])
                P.op('pool', lambda e, ga=ga, gbk=gbk: e.tensor_tensor(out=gbk, in0=gbk, in1=ga, op=ALU.mult), r=['gl1', 'gl0'], w=['gl1'])
                P.op('act', lambda e, gbk=gbk: e.activation(out=gbk, in_=gbk, func=AF.Sigmoid, scale=GELU_S), r=['gl1'], w=['gl1'])
                P.op('dve', lambda e, m=m, c0=c0, ga=ga, gbk=gbk: e.tensor_tensor(out=uT[:, m, c0:c0 + SC], in0=ga, in1=gbk, op=ALU.mult), r=['gl0', 'gl1'], w=['uT'])
    wb, wkey = next_wbuf(k)
    wg = wb[:, 0:2048].rearrange("p (kc n) -> p kc n", kc=4)
    P.op('pool', lambda e: e.dma_start(out=wg, in_=k.s5_w_glu[j].rearrange("(kc p) n -> p kc n", p=128)), w=[wkey], dsem=wkey, hoist=True)
    cnt = 0
    for c0 in range(0, NTOK, ncolb):
        for mo in range(4):
            pb = cnt % 2
            cnt += 1
            for kc in range(4):
                P.op('pe', lambda e, pb=pb, kc=kc, mo=mo, c0=c0: e.matmul(k.ps[pb][:, 0:ncolb], lhsT=wg[:, kc, mo * 128:(mo + 1) * 128], rhs=uT[:, kc, c0:c0 + ncolb],
                                                                        start=(kc == 0), stop=(kc == 3)), r=[wkey, 'uT'], w=['ps%d' % pb])
            gs = gl[pb][:, 0:ncolb]
            P.op('act', lambda e, pb=pb, mo=mo, gs=gs: e.activation(out=gs, in_=k.ps[pb][:, 0:ncolb], func=AF.Sigmoid, bias=bglu[:, mo:mo + 1]), r=['ps%d' % pb, 'bglu'], w=['gl%d' % pb])
            P.op('dve', lambda e, mo=mo, c0=c0, gs=gs: e.tensor_tensor(out=o_bT[:, mo, c0:c0 + ncolb], in0=gs, in1=uT[:, mo, c0:c0 + ncolb], op=ALU.mult), r=['gl%d' % pb, 'uT'], w=['o_bT'])
    s5o = k.s5_p if prompt else k.s5_s
    for b in range(nseq):
        for ri in range(2):
            dst = (s5o[j] if prompt else s5o[j, b]).rearrange("g p r -> p g r")[:, :, ri]
            P.op('sp', lambda e, dst=dst, b=b, ri=ri: e.dma_start(out=dst, in_=Hlast[64 * ri:64 * ri + 64, b, :], allow_slow_non_contiguous=True), r=['Hlast'], dsem='s5o', chain=False)
    return o_bT


def attn_phase_p(k, l, j):
    P = k.P
    P.phase_begin(keep=k.o_bT_bytes)
    o_bT = k.o_bT
    phase_common(k, l, (0, 1), nwbuf=2)
    KT = P.sb([128, 4, SEQ], BF16, 'KT')
    Vb = P.sb([128, NT, 512], BF16, 'Vb')
    QT = P.sb([128, 4, 512], BF16, 'QT')
    oaT = P.sb([128, 4, 512], BF16, 'oaT')
    hT = P.sb([128, 8, 512], BF16, 'hT')
    G = P.sb([128, 2304], BF16, 'G')
    kaug = P.sb([128, 128], BF16, 'kaug')
    qaug = P.sb([128, 512], BF16, 'qaug')
    ones_b = P.sb([128, 128], BF16, 'ones_b')
    Pt = [P.sb([128, 512], BF16, 'Pt%d' % i) for i in range(3)]
    kst = [P.sb([128, 512], F32, 'kst%d' % i) for i in range(2)]
    rb = P.sb([128, 512], F32, 'rb')
    junk = [P.sb([128, D], BF16, 'junk0')] * 2
    hn = [P.sb([128, D], BF16, 'hn0')] * 2
    mtmp = P.sb([128, 4, D], F32, 'mtmp')
    ssh = P.sb([128, 16], F32, 'ssh')
    tmp_t = [P.sb([128, D], F32, 'tmpt0')] * 2
    P.op('pool', lambda e: e.dma_start(out=G[:], in_=k.c_G[:, :]), w=['G'], dsem='G')
    P.op('dve', lambda e: e.memset(kaug[:], 0.0), w=['kaug'])
    P.op('dve', lambda e: e.memset(qaug[:], 0.0), w=['qaug'])
    P.op('pool', lambda e: e.dma_start(out=kaug[0:4, :], in_=k.c_kaug[:, :]), w=['kaug'], dsem='kaug')
    P.op('pool', lambda e: e.dma_start(out=qaug[0:4, :], in_=k.c_qaug[:, :]), w=['qaug'], dsem='qaug')
    P.op('dve', lambda e: e.memset(ones_b[:], 1.0), w=['ones_b'])
    wv_in = k.w_in_even[j].rearrange("(kc p) n -> p kc n", p=128)
    wov = k.w_out_even[j].rearrange("(kc p) n -> p kc n", p=128)
    sctr = 0
    for qc in range(4):
        tiles = k.ptiles[4 * qc:4 * qc + 4]
        norm_transpose(k, tiles, 0, hT, 'hT', junk, hn)
        wb, wkey = next_wbuf(k)
        wv = wb[:].rearrange("p (kc n) -> p kc n", kc=8)
        wload(k, P, ('evqkv', j, 0), wv, wkey, wv_in[:, :, 0:512])
        for m in range(4):
            pb = m % 2
            for kc in range(8):
                P.op('pe', lambda e, pb=pb, kc=kc, m=m, wv=wv: e.matmul(k.ps[pb][:, :], lhsT=wv[:, kc, m * 128:(m + 1) * 128], rhs=hT[:, kc, :], start=(kc == 0), stop=(kc == 7)),
                     r=[wkey, 'hT'], w=['ps%d' % pb])
            P.op('act', lambda e, pb=pb, m=m: e.activation(out=QT[:, m, :], in_=k.ps[pb][:, :], func=AF.Copy, scale=k.qscale[:, m:m + 1]), r=['ps%d' % pb, 'qscale'], w=['QT'])
        wb, wkey = next_wbuf(k)
        wv = wb[:].rearrange("p (kc n) -> p kc n", kc=8)
        wload(k, P, ('evqkv', j, 1), wv, wkey, wv_in[:, :, 512:1024])
        for m in range(4):
            pb = m % 2
            for kc in range(8):
                P.op('pe', lambda e, pb=pb, kc=kc, m=m, wv=wv: e.matmul(k.ps[pb][:, :], lhsT=wv[:, kc, m * 128:(m + 1) * 128], rhs=hT[:, kc, :], start=(kc == 0), stop=(kc == 7)),
                     r=[wkey, 'hT'], w=['ps%d' % pb])
            P.op('dve', lambda e, pb=pb, m=m, qc=qc: e.tensor_copy(KT[:, m, qc * 512:(qc + 1) * 512], k.ps[pb][:, :]), r=['ps%d' % pb], w=['KT'])
        for ti in range(4):
            pb = 2 + ti % 2
            for kc in range(8):
                P.op('pe', lambda e, pb=pb, kc=kc, ti=ti, wv=wv: e.matmul(k.ps[pb][:, :], lhsT=hT[:, kc, ti * 128:(ti + 1) * 128], rhs=wv[:, kc, :], start=(kc == 0), stop=(kc == 7)),
                     r=[wkey, 'hT'], w=['ps%d' % pb])
            ks = kst[ti % 2]
            P.op('dve', lambda e, pb=pb, ks=ks: e.tensor_copy(ks[:, :], k.ps[pb][:, :]), r=['ps%d' % pb], w=['kst%d' % (ti % 2)])
            row0 = (4 * qc + ti) * 128
            P.op('sp', lambda e, ks=ks, row0=row0: e.dma_start(out=k.k_p[j, row0:row0 + 128, :], in_=ks[:, :]), r=['kst%d' % (ti % 2)], dsem='kst%d' % (ti % 2))
        wb, wkey = next_wbuf(k)
        wv = wb[:].rearrange("p (kc n) -> p kc n", kc=8)
        wload(k, P, ('evqkv', j, 2), wv, wkey, wv_in[:, :, 1024:1536])
        for ti in range(4):
            pb = 2 + ti % 2
            for kc in range(8):
                P.op('pe', lambda e, pb=pb, kc=kc, ti=ti, wv=wv: e.matmul(k.ps[pb][:, :], lhsT=hT[:, kc, ti * 128:(ti + 1) * 128], rhs=wv[:, kc, :], start=(kc == 0), stop=(kc == 7)),
                     r=[wkey, 'hT'], w=['ps%d' % pb])
            ks = kst[ti % 2]
            P.op('dve', lambda e, pb=pb, ks=ks: e.tensor_copy(ks[:, :], k.ps[pb][:, :]), r=['ps%d' % pb], w=['kst%d' % (ti % 2)])
            P.op('act', lambda e, pb=pb, ti=ti, qc=qc: e.activation(out=Vb[:, 4 * qc + ti, :], in_=k.ps[pb][:, :], func=AF.Copy), r=['ps%d' % pb], w=['Vb'])
            row0 = (4 * qc + ti) * 128
            P.op('sp', lambda e, ks=ks, row0=row0: e.dma_start(out=k.v_p[j, row0:row0 + 128, :], in_=ks[:, :]), r=['kst%d' % (ti % 2)], dsem='kst%d' % (ti % 2))
        nj = 4 * qc + 4
        items = [(h, jt) for h in range(8) for jt in range(nj)]

        def geom(h, jt):
            t0 = max(512 * qc, 128 * jt)
            off = t0 - 512 * qc
            return off, 512 - off, 128 * jt - 512 * qc, t0 - 128 * jt + 128

        def scores(idx):
            h, jt = items[idx]
            m, hh = h // 2, h % 2
            rows = slice(64 * hh, 64 * hh + 64)
            off, N, cj, x0 = geom(h, jt)
            sb = idx % 3
            P.op('pe', lambda e, sb=sb, m=m, rows=rows, jt=jt, off=off, N=N: e.matmul(
                k.ps[sb][:, 0:N], lhsT=KT[rows, m, jt * 128:(jt + 1) * 128], rhs=QT[rows, m, off:512], start=True, stop=False),
                r=['KT', 'QT'], w=['ps%d' % sb])
            P.op('pe', lambda e, sb=sb, off=off, N=N: e.matmul(k.ps[sb][:, 0:N], lhsT=kaug[:, :], rhs=qaug[:, off:512], start=False, stop=True),
                 r=['kaug', 'qaug'], w=['ps%d' % sb])
        scores(0)
        for idx, (h, jt) in enumerate(items):
            if idx + 1 < len(items):
                scores(idx + 1)
            m, hh = h // 2, h % 2
            rows = slice(64 * hh, 64 * hh + 64)
            slope = 2.0 ** (-(h + 1))
            pv = 3 + hh
            pd = 5 + hh
            off, N, cj, x0 = geom(h, jt)
            sb = idx % 3
            pt = Pt[sb]
            P.op('act', lambda e, sb=sb, pt=pt, N=N, slope=slope, cj=cj: e.activation(out=pt[:, 0:N], in_=k.ps[sb][:, 0:N], func=AF.Exp, scale=slope, bias=float(slope * cj)),
                 r=['ps%d' % sb], w=['Pt%d' % sb])
            P.op('pool', lambda e, pt=pt, N=N, x0=x0: e.tensor_tensor(out=pt[:, 0:N], in0=pt[:, 0:N], in1=G[:, x0:x0 + N], op=ALU.mult), r=['Pt%d' % sb, 'G'], w=['Pt%d' % sb])
            f0, f1 = (jt == 0), (jt == nj - 1)
            P.op('pe', lambda e, pv=pv, pt=pt, jt=jt, m=m, off=off, N=N, f0=f0, f1=f1: e.matmul(k.ps[pv][:, off:512], lhsT=Vb[:, jt, m * 128:(m + 1) * 128], rhs=pt[:, 0:N],
                                                                                               start=f0, stop=f1), r=['Vb', 'Pt%d' % sb], w=['ps%d' % pv])
            P.op('pe', lambda e, pd=pd, pt=pt, off=off, N=N, f0=f0, f1=f1: e.matmul(k.ps[pd][:, off:512], lhsT=ones_b[:, :], rhs=pt[:, 0:N], start=f0, stop=f1),
                 r=['ones_b', 'Pt%d' % sb], w=['ps%d' % pd])
            if jt == nj - 1:
                P.op('dve', lambda e, pd=pd, rows=rows: e.reciprocal(out=rb[rows, :], in_=k.ps[pd][rows, :]), r=['ps%d' % pd], w=['rb%d' % hh])
                P.op('dve', lambda e, pv=pv, rows=rows, m=m: e.tensor_tensor(out=oaT[rows, m, :], in0=k.ps[pv][rows, :], in1=rb[rows, :], op=ALU.mult),
                     r=['ps%d' % pv, 'rb%d' % hh], w=['oaT'])

        def src(kc, col, np_, qc=qc):
            if kc < 4:
                return oaT[:, kc, col:col + np_], 'oaT'
            return o_bT[:, kc - 4, qc * 512 + col:qc * 512 + col + np_], 'o_bT'
        proj_out(k, tiles, src, None, wov, 8, mtmp, ssh, junk, wtag=('evo', j))
        post_norm_add(k, tiles, 1, mtmp, ssh, tmp_t)


def attn_phase_s(k, l, j):
    P = k.P
    P.phase_begin(keep=k.o_bT_bytes)
    o_bT = k.o_bT
    phase_common(k, l, (0, 1), nwbuf=2)
    NK = 17
    hT = P.sb([128, 8, NSTOK], BF16, 'hT')
    QT = P.sb([128, 4, NSTOK], BF16, 'QT')
    KTn = P.sb([128, 4, NSTOK], BF16, 'KTn')
    Vn = P.sb([128, 512], BF16, 'Vn')
    kc_nat = P.sb([128, 16, 512], BF16, 'kc_nat')
    Vc = P.sb([128, 16, 512], BF16, 'Vc')
    KTb = P.sb([128, 4, 2048], BF16, 'KTb')
    oaT = P.sb([128, 4, NSTOK], BF16, 'oaT')
    Gs = P.sb([128, NK, 64], BF16, 'Gs')
    Gsn = P.sb([128, NSS, 64], BF16, 'Gsn')
    kaug = P.sb([4, NK, 128], BF16, 'kaug')
    qaug = P.sb([4, 64], BF16, 'qaug')
    ones_b = P.sb([128, 128], BF16, 'ones_b')
    Pall = P.sb([128, NK, 64], BF16, 'Pall')
    kst = [P.sb([128, 512], F32, 'kst%d' % i) for i in range(2)]
    rb = P.sb([128, 64], F32, 'rb')
    junk = [P.sb([128, D], BF16, 'junk0')] * 2
    hn = [P.sb([128, D], BF16, 'hn0')] * 2
    mtmp = P.sb([128, 1, D], F32, 'mtmp')
    ssh = P.sb([128, 16], F32, 'ssh')
    tmp_t = [P.sb([128, D], F32, 'tmpt0')] * 2
    P.op('pool', lambda e: e.dma_start(out=Gs[:], in_=k.c_Gs[:, :, :]), w=['Gs'], dsem='G')
    P.op('pool', lambda e: e.dma_start(out=Gsn[0:NSTOK, :, :], in_=k.c_Gsn[:, :, :]), w=['Gsn'], dsem='G')
    P.op('pool', lambda e: e.dma_start(out=kaug[:], in_=k.c_kaug_s[:, :, :]), w=['kaug'], dsem='kaug')
    P.op('pool', lambda e: e.dma_start(out=qaug[:], in_=k.c_qaug_s[:, :]), w=['qaug'], dsem='qaug')
    P.op('dve', lambda e: e.memset(ones_b[:], 1.0), w=['ones_b'])
    wv_in = k.w_in_even[j].rearrange("(kc p) n -> p kc n", p=128)
    wov = k.w_out_even[j].rearrange("(kc p) n -> p kc n", p=128)
    tiles = [k.stile]
    T = NSTOK
    norm_transpose(k, tiles, 0, hT, 'hT', junk, hn)
    for which in range(3):
        wb, wkey = next_wbuf(k)
        wv = wb[:].rearrange("p (kc n) -> p kc n", kc=8)
        wload(k, P, ('evqkv', j, which), wv, wkey, wv_in[:, :, 512 * which:512 * which + 512])
        if which < 2:
            for m in range(4):
                pb = m % 2
                for kc in range(8):
                    P.op('pe', lambda e, pb=pb, kc=kc, m=m, wv=wv: e.matmul(k.ps[pb][:, 0:T], lhsT=wv[:, kc, m * 128:(m + 1) * 128], rhs=hT[:, kc, 0:T], start=(kc == 0), stop=(kc == 7)),
                         r=[wkey, 'hT'], w=['ps%d' % pb])
                if which == 0:
                    P.op('act', lambda e, pb=pb, m=m: e.activation(out=QT[:, m, :], in_=k.ps[pb][:, 0:T], func=AF.Copy, scale=0.125), r=['ps%d' % pb], w=['QT'])
                else:
                    P.op('dve', lambda e, pb=pb, m=m: e.tensor_copy(KTn[:, m, :], k.ps[pb][:, 0:T]), r=['ps%d' % pb], w=['KTn'])
        if which >= 1:
            pb = 2 + which % 2
            for kc in range(8):
                P.op('pe', lambda e, pb=pb, kc=kc, wv=wv: e.matmul(k.ps[pb][0:T, :], lhsT=hT[:, kc, 0:T], rhs=wv[:, kc, :], start=(kc == 0), stop=(kc == 7)),
                     r=[wkey, 'hT'], w=['ps%d' % pb])
            ks = kst[which % 2]
            P.op('dve', lambda e, pb=pb, ks=ks: e.tensor_copy(ks[0:T, :], k.ps[pb][0:T, :]), r=['ps%d' % pb], w=['kst%d' % (which % 2)])
            dst = k.k_s if which == 1 else k.v_s
            P.op('sp', lambda e, ks=ks, dst=dst: e.dma_start(out=dst[j, :, :], in_=ks[0:T, :]), r=['kst%d' % (which % 2)], dsem='kst%d' % (which % 2))
            if which == 2:
                P.op('act', lambda e, pb=pb: e.activation(out=Vn[0:T, :], in_=k.ps[pb][0:T, :], func=AF.Copy), r=['ps%d' % pb], w=['Vn'])
    for b in range(NSS):
        ckv = k.cache_k[j, b].rearrange("(t p) f -> p t f", p=128)
        cvv = k.cache_v[j, b].rearrange("(t p) f -> p t f", p=128)
        for q4 in range(4):
            P.op('pool', lambda e, ckv=ckv, q4=q4: e.dma_start(out=kc_nat[:, 4 * q4:4 * q4 + 4, :], in_=ckv[:, 4 * q4:4 * q4 + 4, :]), w=['kc_nat'], dsem='kc%d' % q4)
        for q4 in range(4):
            P.op('pool', lambda e, cvv=cvv, q4=q4: e.dma_start(out=Vc[:, 4 * q4:4 * q4 + 4, :], in_=cvv[:, 4 * q4:4 * q4 + 4, :]), w=['Vc'], dsem='vc%d' % q4)
        cnt = 0
        for m in range(4):
            for j8 in range(2):
                pb = cnt % 2
                cnt += 1
                for jj in range(8):
                    jt = 8 * j8 + jj
                    P.op('pe', lambda e, pb=pb, jj=jj, jt=jt, m=m: e.transpose(k.psb[pb][:, jj * 128:(jj + 1) * 128], kc_nat[:, jt, m * 128:(m + 1) * 128], k.ident[:, :]),
                         r=['kc_nat', 'ident'], w=['ps%d' % pb])
                if cnt % 2 == 0:
                    P.op('act', lambda e, pb=pb, m=m, j8=j8: e.activation(out=KTb[:, m, 1024 * j8:1024 * j8 + 1024], in_=k.psb[pb][:, :], func=AF.Copy), r=['ps%d' % pb], w=['KTb'])
                else:
                    P.op('dve', lambda e, pb=pb, m=m, j8=j8: e.tensor_copy(KTb[:, m, 1024 * j8:1024 * j8 + 1024], k.psb[pb][:, :]), r=['ps%d' % pb], w=['KTb'])
        for jt in range(NK):
            M = 128 if jt < 16 else NSTOK
            sb = 2 + jt % 2
            for h in range(8):
                m, hh = h // 2, h % 2
                rows = slice(64 * hh, 64 * hh + 64)
                lhs = KTb[rows, m, jt * 128:(jt + 1) * 128] if jt < 16 else KTn[rows, m, 0:NSTOK]
                lk = 'KTb' if jt < 16 else 'KTn'
                P.op('pe', lambda e, sb=sb, lhs=lhs, rows=rows, m=m, h=h, M=M, b=b: e.matmul(
                    k.ps[sb][0:M, h * 8:(h + 1) * 8], lhsT=lhs, rhs=QT[rows, m, 8 * b:8 * b + 8], start=True, stop=False), r=[lk, 'QT'], w=['ps%d' % sb])
                P.op('pe', lambda e, sb=sb, jt=jt, h=h, M=M: e.matmul(k.ps[sb][0:M, h * 8:(h + 1) * 8], lhsT=kaug[0:4, jt, 0:M], rhs=qaug[0:4, h * 8:(h + 1) * 8],
                                                                     start=False, stop=True), r=['kaug', 'qaug'], w=['ps%d' % sb])
            P.op('act', lambda e, sb=sb, jt=jt, M=M: e.activation(out=Pall[0:M, jt, :], in_=k.ps[sb][0:M, 0:64], func=AF.Exp), r=['ps%d' % sb], w=['Pall'])
            gmask = Gs[0:M, jt, :] if jt < 16 else Gsn[0:M, b, :]
            P.op('pool', lambda e, jt=jt, M=M, gmask=gmask: e.tensor_tensor(out=Pall[0:M, jt, :], in0=Pall[0:M, jt, :], in1=gmask, op=ALU.mult),
                 r=['Pall', 'Gs', 'Gsn'], w=['Pall'])
        for h in range(8):
            m = h // 2
            for jt in range(NK):
                M = 128 if jt < 16 else NSTOK
                lhs = Vc[:, jt, m * 128:(m + 1) * 128] if jt < 16 else Vn[0:NSTOK, m * 128:(m + 1) * 128]
                lk = 'Vc' if jt < 16 else 'Vn'
                f0, f1 = (jt == 0), (jt == NK - 1)
                P.op('pe', lambda e, lhs=lhs, h=h, jt=jt, M=M, f0=f0, f1=f1: e.matmul(k.ps[4][:, h * 8:(h + 1) * 8], lhsT=lhs, rhs=Pall[0:M, jt, h * 8:(h + 1) * 8], start=f0, stop=f1),
                     r=[lk, 'Pall'], w=['ps4'])
            for jt in range(NK):
                M = 128 if jt < 16 else NSTOK
                f0, f1 = (jt == 0), (jt == NK - 1)
                P.op('pe', lambda e, h=h, jt=jt, M=M, f0=f0, f1=f1: e.matmul(k.ps[5][:, h * 8:(h + 1) * 8], lhsT=ones_b[0:M, :], rhs=Pall[0:M, jt, h * 8:(h + 1) * 8], start=f0, stop=f1),
                     r=['ones_b', 'Pall'], w=['ps5'])
        P.op('dve', lambda e: e.reciprocal(out=rb[:, :], in_=k.ps[5][:, 0:64]), r=['ps5'], w=['rb'])
        for hh in range(2):
            rows = slice(64 * hh, 64 * hh + 64)
            P.op('dve', lambda e, rows=rows, hh=hh, b=b: e.tensor_tensor(
                out=oaT[rows, :, 8 * b:8 * b + 8], in0=k.ps[4][rows, 0:64].rearrange("p (m x t) -> p m x t", x=2, t=8)[:, :, hh, :],
                in1=rb[rows, :].rearrange("p (m x t) -> p m x t", x=2, t=8)[:, :, hh, :], op=ALU.mult), r=['ps4', 'rb'], w=['oaT'])

    def src(kc, col, np_):
        if kc < 4:
            return oaT[:, kc, col:col + np_], 'oaT'
        return o_bT[:, kc - 4, col:col + np_], 'o_bT'
    proj_out(k, tiles, src, None, wov, 8, mtmp, ssh, junk, wtag=('evo', j))
    post_norm_add(k, tiles, 1, mtmp, ssh, tmp_t)


_CACHE = {}


def host_constants():
    c = {}
    i128 = np.arange(128)
    c['c_maskp'] = (i128[None, :] >= i128[:, None]).astype(np.float32)
    se = np.zeros((128, 128), np.float32)
    se[127, :] = 1.0
    c['c_selendp'] = se
    t = np.arange(NSTOK)
    same = (t[:, None] // LS) == (t[None, :] // LS)
    c['c_masks'] = (same & (t[None, :] >= t[:, None])).astype(np.float32)
    c['c_selends'] = (t[:, None] == (LS * (t[None, :] // LS) + LS - 1)).astype(np.float32)
    sb = np.zeros((NSTOK, NSS, 128), np.float32)
    for b in range(NSS):
        sb[LS * b + LS - 1, b, :] = 1.0
    c['c_selendBs'] = sb
    sc = np.zeros((128, NSS, NSTOK), np.float32)
    sr = np.zeros((NSTOK, NSS), np.float32)
    for b in range(NSS):
        sc[:, b, LS * b:LS * (b + 1)] = 1.0
        sr[LS * b:LS * (b + 1), b] = 1.0
    c['c_seqcol'] = sc
    c['c_seqrow'] = sr
    c['c_negp'] = (c['c_maskp'] - 1.0) * 30000.0
    c['c_negs'] = (c['c_masks'] - 1.0) * 30000.0

    def cmult(d):
        d = np.asarray(d)
        return (((d >= 0) & (d <= 128)).astype(np.float32) + ((d >= 0) & (d <= 512) & (d % 4 == 0)).astype(np.float32)
                + ((d >= 0) & (d <= 2048) & (d % 16 == 0)).astype(np.float32))
    rl = np.arange(128)
    xx = np.arange(2304)
    c['c_G'] = cmult(xx[None, :] - 128 - rl[:, None]).astype(np.float32)
    ka = np.zeros((4, 128), np.float32)
    ka[0] = rl
    ka[1] = 1.0
    ka[2] = 1.0
    c['c_kaug'] = ka
    tl = np.arange(512)
    qa = np.zeros((4, 512), np.float32)
    qa[0] = 1.0
    qa[1] = -128.0 * (tl // 128)
    qa[2] = -(tl % 128)
    c['c_qaug'] = qa
    qs = np.zeros((128, 4), np.float32)
    for m in range(4):
        for p in range(128):
            qs[p, m] = 2.0 ** (2 * m + p // 64 - 2)
    c['c_qscale'] = qs
    sw = np.zeros((128, 128), np.float32)
    for p in range(64):
        sw[64 + p, p] = -1.0
        sw[p, 64 + p] = 1.0
    c['c_swp'] = sw
    c['c_rowmask'] = (rl[:, None] // 16 == np.arange(8)[None, :]).astype(np.float32)
    c['c_iota1'] = np.broadcast_to(np.arange(1, 513, dtype=np.float32)[None, :], (128, 512)).copy()
    jt = np.arange(17)
    tq = np.arange(LS)
    d = 2048 + tq[None, None, :] - (128 * jt[None, :, None] + rl[:, None, None])
    valid = (128 * jt[None, :, None] + rl[:, None, None]) < 2048 + LS
    gs = cmult(d) * valid
    c['c_Gs'] = np.repeat(gs[:, :, None, :], 8, axis=2).reshape(128, 17, 64).astype(np.float32)
    kas = np.zeros((4, 17, 128), np.float32)
    kas[0] = jt[:, None]
    kas[1] = rl[None, :]
    kas[2] = 1.0
    kas[3] = 1.0
    kas[1, 16, :] = rl % LS
    c['c_kaug_s'] = kas
    rk = np.arange(NSTOK)
    gn = np.zeros((NSTOK, NSS, 8, LS), np.float32)
    for b in range(NSS):
        dd = tq[None, :] - (rk[:, None] % LS)
        gn[:, b, :, :] = (cmult(dd) * ((rk[:, None] // LS) == b))[:, None, :]
    c['c_Gsn'] = gn.reshape(NSTOK, NSS, 64)
    qas = np.zeros((4, 8, LS), np.float32)
    for h in range(8):
        sl = 2.0 ** (-(h + 1))
        qas[0, h, :] = sl * 128.0
        qas[1, h, :] = sl
        qas[2, h, :] = -sl * 2048.0
        qas[3, h, :] = -sl * tq
    c['c_qaug_s'] = qas.reshape(4, 64)
    return c


def make_in_maps(inputs):
    f = lambda a: np.ascontiguousarray(np.asarray(a, dtype=np.float32))
    norms = f(np.stack([inputs['norm_mix_pre'], inputs['norm_mix_post'], inputs['norm_mlp_pre'], inputs['norm_mlp_post']]))
    shared = {
        'norms': norms,
        'w_mlp_up': f(inputs['w_mlp_up']),
        'w_mlp_down': f(inputs['w_mlp_down']),
        'identf': np.eye(128, dtype=np.float32),
        'w_in_odd': f(inputs['w_in_odd']), 'w_out_odd': f(inputs['w_out_odd']),
        'conv_w': f(inputs['conv_w']), 'conv_b': f(inputs['conv_b']), 'dt_bias': f(inputs['dt_bias']),
        'a_log': f(inputs['a_log']), 'd_skip': f(inputs['d_skip']), 'gnorm_w': f(inputs['gnorm_w']),
    }
    for nm in ('w_in_even', 'w_out_even', 's5_lambda_re', 's5_lambda_im', 's5_log_dt', 's5_b_re', 's5_b_im', 's5_c_re', 's5_c_im',
               's5_d', 's5_w_glu', 's5_b_glu'):
        shared[nm] = f(inputs[nm])
    shared.update(host_constants())
    maps = []
    for c in range(NCORES):
        m = dict(shared)
        m['x_p'] = f(inputs['x_prompt'][c])
        m['x_s'] = f(inputs['x_sample'][NSS * c:NSS * (c + 1)]).reshape(NSTOK, D)
        m['state_conv'] = f(inputs['state_conv'][:, NSS * c:NSS * (c + 1)])
        m['state_s5'] = f(inputs['state_s5'][:, NSS * c:NSS * (c + 1)])
        m['cache_k'] = f(inputs['cache_k'][:, NSS * c:NSS * (c + 1)]).reshape(2, NSS, 2048, 512)
        m['cache_v'] = f(inputs['cache_v'][:, NSS * c:NSS * (c + 1)]).reshape(2, NSS, 2048, 512)
        m['state_ssm'] = f(inputs['state_ssm'][:, NSS * c:NSS * (c + 1)])
        maps.append(m)
    return maps


def kernel(**inputs):
    cfg = {}
    if 'nc' not in _CACHE:
        _CACHE['nc'] = build(cfg)
    nc, k = _CACHE['nc']
    maps = make_in_maps(inputs)
    res = run_bass_kernel_spmd(nc, maps, core_ids=list(range(NCORES)))
    r = res.results
    C = range(NCORES)
    st = lambda name, shp: np.stack([r[c][name].reshape(shp) for c in C], axis=1)
    cat = lambda name, shp: np.concatenate([r[c][name].reshape(shp) for c in C], axis=1)
    y_p = np.stack([r[c]['y_p'] for c in C])
    y_s = np.concatenate([r[c]['y_s'].reshape(NSS, LS, D) for c in C])
    k_p = st('k_p', (2, SEQ, 8, 64))
    v_p = st('v_p', (2, SEQ, 8, 64))
    s5_p = st('s5_p', (2, 32, 64, 2))
    conv_p = st('conv_p', (2, 3, NXBC))
    ssm_p = st('ssm_p', (2, 32, 64, 128))
    k_s = cat('k_s', (2, NSS, LS, 8, 64))
    v_s = cat('v_s', (2, NSS, LS, 8, 64))
    s5_s = cat('s5_s', (2, NSS, 32, 64, 2))
    conv_s = cat('conv_s', (2, NSS, 3, NXBC))
    ssm_s = cat('ssm_s', (2, NSS, 32, 64, 128))
    return (y_p, y_s, k_p, v_p, s5_p, conv_p, ssm_p, k_s, v_s, s5_s, conv_s, ssm_s)
```

```python
import numpy as np
from contextlib import ExitStack
import concourse.bass as bass
import concourse.mybir as mybir
from concourse.bass_utils import run_bass_kernel_spmd

F32 = mybir.dt.float32
BF16 = mybir.dt.bfloat16
ALU = mybir.AluOpType
AF = mybir.ActivationFunctionType
AX = mybir.AxisListType

ENGS = ['pe', 'act', 'dve', 'pool', 'sp']


class Op:
    __slots__ = ('eng', 'fn', 'deps', 'marked', 'semval', 'dsem', 'dval', 'idx', 'ptail')

    def __init__(self, eng, fn):
        self.eng = eng
        self.fn = fn
        self.deps = []
        self.marked = False
        self.semval = 0
        self.dsem = None
        self.dval = 0


class Prog:
    def __init__(self, nc, stack, arena_bytes=210944):
        self.nc = nc
        self.stack = stack
        self.ops = {e: [] for e in ENGS}
        self.sems = {e: stack.enter_context(nc.semaphore('s_' + e)) for e in ENGS}
        self.dsems = {}
        self.last_w = {}
        self.readers = {}
        self.phase_op = None
        self.last_of = {}
        self.sync_same_engine = True
        self.hoist_floor = 0
        self.hoist_prev = None
        self.pool_tail = None
        self.unchained = set()
        arena = nc.alloc_sbuf_tensor('arena', [128, arena_bytes // 4], F32)
        self.base = nc.lookup_mloc(arena).addr
        self.limit = self.base + arena_bytes
        self.pers = self.base
        self.cur = None
        self.nid = 0
        self.peak = 0

    def _alloc(self, off, shape, dtype, name):
        self.nid += 1
        return self.nc.alloc_sbuf_tensor_at('%s_%d' % (name or 't', self.nid), list(shape), dtype, offset=off)

    @staticmethod
    def _bytes(shape, dtype):
        n = 1
        for s in shape[1:]:
            n *= s
        n *= mybir.dt.size(dtype)
        return (n + 63) // 64 * 64

    def sbp(self, shape, dtype, name=None):
        assert self.cur is None
        off = self.pers
        self.pers += self._bytes(shape, dtype)
        assert self.pers <= self.limit, 'sbuf overflow (persistent)'
        return self._alloc(off, shape, dtype, name)

    def phase_begin(self, keep=0):
        self.barrier()
        self.cur = self.pers + keep

    def sb(self, shape, dtype, name=None):
        off = self.cur
        self.cur += self._bytes(shape, dtype)
        assert self.cur <= self.limit, 'sbuf overflow (phase) need %d' % (self.cur - self.limit)
        self.peak = max(self.peak, self.cur - self.base)
        return self._alloc(off, shape, dtype, name)

    def _stream(self, o):
        return o.dsem if o.dsem is not None else o.eng

    def _val(self, d):
        return d.dval if d.dsem is not None else d.idx

    def op(self, eng, fn, r=(), w=(), dsem=None, hoist=False, chain=True):
        o = Op(eng, fn)
        deps = {}
        pr = [x for x in r if isinstance(x, str) and x.startswith('ps')]
        if pr:
            r = [x for x in r if x not in pr]
            w = list(w) + pr

        def add(d):
            if d is None:
                return
            s = self._stream(d)
            cur = deps.get(s)
            if cur is None or self._val(d) > self._val(cur):
                deps[s] = d

        for k in r:
            add(self.last_w.get(k))
        for k in w:
            add(self.last_w.get(k))
            for rd in self.readers.get(k, {}).values():
                add(rd)
        add(self.phase_op)
        if dsem is not None and not chain:
            self.unchained.add(dsem)
            deps.pop(dsem, None)
        for sname in list(deps.keys()):
            if sname in self.unchained and sname != dsem:
                deps[sname] = self.dsems[sname][2]
        if dsem is not None:
            ent = self.dsems.get(dsem)
            if ent is None:
                ent = [self.stack.enter_context(self.nc.semaphore('d_' + dsem)), 0, None]
                self.dsems[dsem] = ent
            if chain:
                add(ent[2])
            ent[1] += 16
            o.dsem = dsem
            o.dval = ent[1]
            ent[2] = o
        o.idx = len(self.ops[eng])
        final = []
        for s, d in deps.items():
            if d.dsem is None:
                if d.eng == eng and (eng == 'pe' or not self.sync_same_engine):
                    continue
                d.marked = True
            final.append(d)
        o.deps = final
        st = self._stream(o)
        for k in r:
            self.readers.setdefault(k, {})[st] = o
        for k in w:
            self.last_w[k] = o
            self.readers[k] = {}
        o.ptail = self.pool_tail
        if hoist and eng == 'pool':
            lst = self.ops[eng]
            pos = self.hoist_floor
            cands = [self.hoist_prev] + [d.ptail for d in final]
            for c in cands:
                if c is not None:
                    pos = max(pos, lst.index(c) + 1)
            lst.insert(pos, o)
            self.hoist_prev = o
        else:
            self.ops[eng].append(o)
            if eng == 'pool':
                self.pool_tail = o
        self.last_of[st] = o
        return o

    def barrier(self):
        self.hoist_floor = len(self.ops['pool'])
        self.hoist_prev = None
        o = Op('sp', lambda e: e.nop())
        o.ptail = self.pool_tail
        o.idx = len(self.ops['sp'])
        deps = []
        for s, d in self.last_of.items():
            if d.dsem is None:
                d.marked = True
            deps.append(d)
        o.deps = deps
        o.marked = True
        self.ops['sp'].append(o)
        self.last_of['sp'] = o
        self.phase_op = o
        self.last_w = {}
        self.readers = {}
        return o

    def emit(self):
        nc = self.nc
        for e in ENGS:
            c = 0
            for o in self.ops[e]:
                if o.dsem is None and o.marked:
                    c += 1
                    o.semval = c

        def run(ename, eng):
            waited = {}
            for o in self.ops[ename]:
                for d in o.deps:
                    if d.dsem is not None:
                        sem, val, key = self.dsems[d.dsem][0], d.dval, 'd_' + d.dsem
                    else:
                        sem, val, key = self.sems[d.eng], d.semval, d.eng
                    if waited.get(key, 0) >= val:
                        continue
                    waited[key] = val
                    eng.wait_ge(sem, val)
                ins = o.fn(eng)
                if o.dsem is not None:
                    ins.then_inc(self.dsems[o.dsem][0], 16)
                elif o.marked:
                    ins.then_inc(self.sems[ename], 1)

        with nc.Block() as block:
            @block.tensor
            def _(eng):
                run('pe', eng)

            @block.scalar
            def _(eng):
                run('act', eng)

            @block.vector
            def _(eng):
                run('dve', eng)

            @block.gpsimd
            def _(eng):
                run('pool', eng)

            @block.sync
            def _(eng):
                run('sp', eng)


D = 1024
SEQ = 2048
NT = SEQ // 128
DEPTH = 4
NSS = 4
LS = 8
NSTOK = NSS * LS
DFF = 4096
EPS = 1e-6
NCORES = 8


class K:
    pass


def build(cfg):
    nc = bass.Bass("TRN2", target_bir_lowering=False)
    depth = cfg.get('depth', DEPTH)
    do_mix = cfg.get('mix', 7)
    k = K()
    k.nc = nc
    k.cfg = cfg
    k.scratch = {}

    def din(name, shape):
        return nc.dram_tensor(name, list(shape), F32, kind="ExternalInput").ap()

    def dout(name, shape):
        return nc.dram_tensor(name, list(shape), F32, kind="ExternalOutput").ap()

    k.x_p = din('x_p', [SEQ, D])
    k.x_s = din('x_s', [NSTOK, D])
    k.norms = din('norms', [4, DEPTH, D])
    k.w_up = din('w_mlp_up', [DEPTH, D, DFF])
    k.w_down = din('w_mlp_down', [DEPTH, DFF, D])
    k.identf = din('identf', [128, 128])
    k.w_in_odd = din('w_in_odd', [2, D, 5152])
    k.w_out_odd = din('w_out_odd', [2, DIN, D])
    k.conv_w = din('conv_w', [2, 4, NXBC])
    k.conv_b = din('conv_b', [2, NXBC])
    k.dt_bias = din('dt_bias', [2, 32])
    k.a_log = din('a_log', [2, 32])
    k.d_skip = din('d_skip', [2, 32])
    k.gnorm_w = din('gnorm_w', [2, DIN])
    k.state_conv = din('state_conv', [2, NSS, 3, NXBC])
    k.state_ssm = din('state_ssm', [2, NSS, 32, 64, 128])
    k.c_maskp = din('c_maskp', [128, 128])
    k.c_masks = din('c_masks', [NSTOK, NSTOK])
    k.c_selendp = din('c_selendp', [128, 128])
    k.c_selends = din('c_selends', [NSTOK, NSTOK])
    k.c_selendBs = din('c_selendBs', [NSTOK, NSS, 128])
    k.c_seqcol = din('c_seqcol', [128, NSS, NSTOK])
    k.c_seqrow = din('c_seqrow', [NSTOK, NSS])
    k.c_negp = din('c_negp', [128, 128])
    k.c_negs = din('c_negs', [NSTOK, NSTOK])
    k.w_in_even = din('w_in_even', [2, D, 2048])
    k.w_out_even = din('w_out_even', [2, D, D])
    k.s5_lambda_re = din('s5_lambda_re', [2, 32, 64])
    k.s5_lambda_im = din('s5_lambda_im', [2, 32, 64])
    k.s5_log_dt = din('s5_log_dt', [2, 32])
    k.s5_b_re = din('s5_b_re', [2, 32, 64, 16])
    k.s5_b_im = din('s5_b_im', [2, 32, 64, 16])
    k.s5_c_re = din('s5_c_re', [2, 32, 16, 64])
    k.s5_c_im = din('s5_c_im', [2, 32, 16, 64])
    k.s5_d = din('s5_d', [2, 32, 16])
    k.s5_w_glu = din('s5_w_glu', [2, 512, 512])
    k.s5_b_glu = din('s5_b_glu', [2, 512])
    k.state_s5 = din('state_s5', [2, NSS, 32, 64, 2])
    k.cache_k = din('cache_k', [2, NSS, 2048, 512])
    k.cache_v = din('cache_v', [2, NSS, 2048, 512])
    k.c_G = din('c_G', [128, 2304])
    k.c_kaug = din('c_kaug', [4, 128])
    k.c_qaug = din('c_qaug', [4, 512])
    k.c_qscale = din('c_qscale', [128, 4])
    k.c_swp = din('c_swp', [128, 128])
    k.c_rowmask = din('c_rowmask', [128, 8])
    k.c_iota1 = din('c_iota1', [128, 512])
    k.c_Gs = din('c_Gs', [128, 17, 64])
    k.c_qaug_s = din('c_qaug_s', [4, 64])
    k.c_kaug_s = din('c_kaug_s', [4, 17, 128])
    k.c_Gsn = din('c_Gsn', [NSTOK, NSS, 64])
    k.y_p = dout('y_p', [SEQ, D])
    k.y_s = dout('y_s', [NSTOK, D])
    k.k_p = dout('k_p', [2, SEQ, 512])
    k.v_p = dout('v_p', [2, SEQ, 512])
    k.s5_p = dout('s5_p', [2, 32, 64, 2])
    k.k_s = dout('k_s', [2, NSTOK, 512])
    k.v_s = dout('v_s', [2, NSTOK, 512])
    k.s5_s = dout('s5_s', [2, NSS, 32, 64, 2])
    k.conv_p = dout('conv_p', [2, 3, NXBC])
    k.ssm_p = dout('ssm_p', [2, 32, 64, 128])
    k.conv_s = dout('conv_s', [2, NSS, 3, NXBC])
    k.ssm_s = dout('ssm_s', [2, NSS, 32, 64, 128])

    with ExitStack() as st:
        P = Prog(nc, st)
        k.P = P
        k.X = P.sbp([128, NT, D], F32, 'X')
        k.Xs = P.sbp([128, 1, D], F32, 'Xs')
        k.ident = P.sbp([128, 128], BF16, 'ident')
        k.identF = P.sbp([128, 128], F32, 'identF')
        k.wctr = 0
        k.small = P.sbp([128, 64], F32, 'small')
        k.smctr = 0
        k.ps = [nc.alloc_psum_tensor('ps%d' % i, [128, 512], F32) for i in range(8)]
        cp = {'mask': P.sbp([128, 128], F32, 'c_maskp'), 'selend': P.sbp([128, 128], F32, 'c_selendp')}
        cp['selendB'] = cp['selend'][:].rearrange("p (b n) -> p b n", b=1)
        cs = {'mask': P.sbp([128, NSTOK], F32, 'c_masks'), 'selend': P.sbp([128, NSTOK], F32, 'c_selends'),
              'selendB': P.sbp([128, NSS, 128], F32, 'c_selendBs'), 'seqcol': P.sbp([128, NSS, NSTOK], F32, 'c_seqcol'),
              'seqrow': P.sbp([128, NSS], F32, 'c_seqrow')}
        k.cst_p, k.cst_s = cp, cs
        k.qscale = P.sbp([128, 4], F32, 'qscale')
        k.swp = P.sbp([128, 128], BF16, 'swp')
        k.rowmask = P.sbp([128, 8], F32, 'rowmask')
        k.iota1 = P.sbp([128, 512], F32, 'iota1')
        P.op('sp', lambda e: e.dma_start(out=k.qscale[:], in_=k.c_qscale[:, :]), w=['qscale'], dsem='cst', chain=False)
        P.op('sp', lambda e: e.dma_start(out=k.rowmask[:], in_=k.c_rowmask[:, :]), w=['rowmask'], dsem='cst', chain=False)
        P.op('sp', lambda e: e.dma_start(out=k.iota1[:], in_=k.c_iota1[:, :]), w=['iota1'], dsem='cst', chain=False)
        P.op('pool', lambda e: e.dma_start(out=k.swp[:], in_=k.c_swp[:, :]), w=['swp'], dsem='swp')
        P.op('sp', lambda e: e.dma_start(out=cp['mask'][:], in_=k.c_maskp[:, :]), w=['cst'], dsem='cst', chain=False)
        P.op('sp', lambda e: e.dma_start(out=cp['selend'][:], in_=k.c_selendp[:, :]), w=['cst'], dsem='cst', chain=False)
        P.op('sp', lambda e: e.dma_start(out=cs['mask'][0:NSTOK, :], in_=k.c_masks[:, :]), w=['cst'], dsem='cst', chain=False)
        P.op('sp', lambda e: e.dma_start(out=cs['selend'][0:NSTOK, :], in_=k.c_selends[:, :]), w=['cst'], dsem='cst', chain=False)
        P.op('sp', lambda e: e.dma_start(out=cs['selendB'][0:NSTOK, :, :], in_=k.c_selendBs[:, :, :]), w=['cst'], dsem='cst', chain=False)
        P.op('sp', lambda e: e.dma_start(out=cs['seqcol'][:, :, :], in_=k.c_seqcol[:, :, :]), w=['cst'], dsem='cst', chain=False)
        P.op('sp', lambda e: e.dma_start(out=cs['seqrow'][0:NSTOK, :], in_=k.c_seqrow[:, :]), w=['cst'], dsem='cst', chain=False)
        k.psb = [p[:].bitcast(BF16) for p in k.ps]

        P.op('pool', lambda e: e.dma_start(out=k.ident[:], in_=k.identf[:, :]), w=['ident'], dsem='ident')
        P.op('sp', lambda e: e.dma_start(out=k.identF[:], in_=k.identf[:, :]), w=['identF'], dsem='identF')
        xv = k.x_p.rearrange("(i p) d -> p i d", p=128)
        for q in range(4):
            P.op('sp', lambda e, q=q: e.dma_start(out=k.X[:, 4 * q:4 * q + 4, :], in_=xv[:, 4 * q:4 * q + 4, :]),
                 w=['X%d' % i for i in range(4 * q, 4 * q + 4)], dsem='xload%d' % q)
        P.op('sp', lambda e: e.dma_start(out=k.Xs[0:NSTOK, 0, :], in_=k.x_s[:, :]), w=['Xs0'], dsem='xsload')

        k.ptiles = [(k.X, i, 128, 'X%d' % i) for i in range(NT)]
        k.stile = (k.Xs, 0, NSTOK, 'Xs0')

        for l in range(depth):
            if (do_mix & 2) and l % 2 == 0:
                s5_phase(k, l, l // 2, 'p')
                attn_phase_p(k, l, l // 2)
                if do_mix & 4:
                    s5_phase(k, l, l // 2, 's')
                    attn_phase_s(k, l, l // 2)
            if (do_mix & 1) and l % 2 == 1:
                mamba(k, l, l // 2, 'p')
                mamba(k, l, l // 2, 's')
            mlp(k, l)

        yv = k.y_p.rearrange("(i p) d -> p i d", p=128)
        for q in range(4):
            P.op('sp', lambda e, q=q: e.dma_start(out=yv[:, 4 * q:4 * q + 4, :], in_=k.X[:, 4 * q:4 * q + 4, :]),
                 r=['X%d' % i for i in range(4 * q, 4 * q + 4)], dsem='ystore%d' % q)
        P.op('sp', lambda e: e.dma_start(out=k.y_s[:, :], in_=k.Xs[0:NSTOK, 0, :]), r=['Xs0'], dsem='ysstore')
        P.barrier()
        P.emit()
    k.stats = {e: len(P.ops[e]) for e in ENGS}
    k.peak = P.peak
    return nc, k


def cfg_get(k, name, default):
    return k.cfg.get(name, default)


def phase_common(k, l, widxs, nwbuf=3):
    P = k.P
    wbc = P.sb([128, 2, D], F32, 'wbc')
    k.wbc = wbc
    for jj, j in enumerate(widxs):
        P.op('sp', lambda e, j=j, jj=jj: e.dma_start(out=wbc[:, jj, :], in_=k.norms[j, l:l + 1, :].to_broadcast([128, D])),
             w=['wbc%d' % jj], dsem='wbc%d' % jj)
    k.wbuf = [P.sb([128, 4096], BF16, 'wbuf%d' % i) for i in range(nwbuf)]


def small_slot(k, n=2):
    c = (k.smctr % 32) * 2
    k.smctr += 1
    return c


def wload(k, P, skey, wv, wkey, src):
    if not k.cfg.get('scratch', True):
        P.op('pool', lambda e: e.dma_start(out=wv, in_=src), w=[wkey], dsem=wkey, hoist=True)
        return
    sc = k.scratch.get(skey)
    if sc is None:
        name = 'wsc_%d' % len(k.scratch)
        sc = k.nc.dram_tensor(name, [128, 8, 512], BF16, kind="Internal").ap()
        k.scratch[skey] = sc
        P.op('pool', lambda e: e.dma_start(out=wv, in_=src), w=[wkey], dsem=wkey, hoist=True)
        P.op('sp', lambda e: e.dma_start(out=sc[:, :, :], in_=wv), r=[wkey], w=[('sc', skey)], dsem='st_' + wkey)
    else:
        P.op('sp', lambda e: e.dma_start(out=wv, in_=sc[:, :, :]), r=[('sc', skey)], w=[wkey], dsem='h_' + wkey)


def next_wbuf(k):
    i = k.wctr % len(k.wbuf)
    k.wctr += 1
    return k.wbuf[i], 'wbuf%d' % i


def rstd_from_ss(k, np_, ss_ap, out_ap, rkeys, wkeys):
    P = k.P
    P.op('act', lambda e: e.activation(out=out_ap, in_=ss_ap, func=AF.Ln, scale=1.0 / D, bias=EPS), r=rkeys, w=wkeys)
    P.op('act', lambda e: e.activation(out=out_ap, in_=out_ap, func=AF.Exp, scale=-0.5), r=wkeys, w=wkeys)


def norm_transpose(k, tiles, widx, hT, hTkey, junk, hn):
    P = k.P
    wbc = k.wbc
    col = 0
    for ti, (X, i, np_, xkey) in enumerate(tiles):
        s = small_slot(k, 2)
        sk = 'sm%d' % s
        ss = k.small[0:np_, s:s + 1]
        rs = k.small[0:np_, s + 1:s + 2]
        jslot = ti % 2
        P.op('act', lambda e, X=X, i=i, np_=np_, ss=ss, jslot=jslot: e.activation(
            out=junk[jslot][0:np_, :], in_=X[0:np_, i, :], func=AF.Square, accum_out=ss),
            r=[xkey], w=[('junk', id(junk[jslot])), sk])
        rstd_from_ss(k, np_, ss, rs, [sk], [sk + 'r'])
        P.op('dve', lambda e, X=X, i=i, np_=np_, rs=rs, jslot=jslot: e.scalar_tensor_tensor(
            out=hn[jslot][0:np_, :], in0=X[0:np_, i, :], scalar=rs, in1=wbc[0:np_, widx, :],
            op0=ALU.mult, op1=ALU.mult), r=[xkey, sk + 'r', 'wbc%d' % widx], w=['hn%d' % (jslot if hn[0] is not hn[1] else 0)])
        pb = 6 + (ti % 2)
        for kc in range(8):
            P.op('pe', lambda e, kc=kc, np_=np_, jslot=jslot, pb=pb: e.transpose(
                k.psb[pb][:, kc * 128:kc * 128 + np_], hn[jslot][0:np_, kc * 128:(kc + 1) * 128], k.ident[0:np_, 0:np_]),
                r=['hn%d' % (jslot if hn[0] is not hn[1] else 0), 'ident'], w=['ps%d' % pb])
        src = k.psb[pb].rearrange("p (c t) -> p c t", c=8)[:, :, 0:np_]
        eng = 'act' if ti % 2 == 0 else 'dve'
        if eng == 'act':
            P.op('act', lambda e, src=src, col=col, np_=np_: e.activation(out=hT[:, :, col:col + np_], in_=src, func=AF.Copy),
                 r=['ps%d' % pb], w=[hTkey])
        else:
            P.op('dve', lambda e, src=src, col=col, np_=np_: e.tensor_copy(hT[:, :, col:col + np_], src),
                 r=['ps%d' % pb], w=[hTkey])
        col += np_
    return col


def post_norm_add(k, tiles, widx, mtmp, ssh, tmp_t):
    P = k.P
    wbc = k.wbc
    for ti, (X, i, np_, xkey) in enumerate(tiles):
        s = small_slot(k, 2)
        sk = 'sm%d' % s
        ss = k.small[0:np_, s:s + 1]
        rs = k.small[0:np_, s + 1:s + 2]
        P.op('dve', lambda e, np_=np_, ss=ss, ti=ti: e.tensor_tensor(out=ss, in0=ssh[0:np_, 2 * ti:2 * ti + 1],
                                                                     in1=ssh[0:np_, 2 * ti + 1:2 * ti + 2], op=ALU.add),
             r=['ssh%d' % ti], w=[sk])
        rstd_from_ss(k, np_, ss, rs, [sk], [sk + 'r'])
        tt = tmp_t[ti % 2]
        tk = 'tmpt%d' % ((ti % 2) if tmp_t[0] is not tmp_t[1] else 0)
        P.op('dve', lambda e, np_=np_, rs=rs, ti=ti, tt=tt: e.scalar_tensor_tensor(
            out=tt[0:np_, :], in0=mtmp[0:np_, ti, :], scalar=rs, in1=wbc[0:np_, widx, :], op0=ALU.mult, op1=ALU.mult),
            r=['mtmp%d' % ti, sk + 'r', 'wbc%d' % widx], w=[tk])
        P.op('pool', lambda e, X=X, i=i, np_=np_, tt=tt: e.tensor_tensor(out=X[0:np_, i, :], in0=X[0:np_, i, :], in1=tt[0:np_, :], op=ALU.add),
             r=[tk, xkey], w=[xkey])


def chunks_of(n):
    out = []
    c = 0
    while c < n:
        m = min(512, n - c)
        out.append((c, m))
        c += m
    return out


def mlp(k, l):
    P = k.P
    P.phase_begin()
    phase_common(k, l, (2, 3), nwbuf=cfg_get(k, "mlp_nwbuf", 5))
    NC = 512 + NSTOK
    hT = P.sb([128, 8, NC], BF16, 'hT')
    aT = P.sb([128, 32, NC], BF16, 'aT')
    junk = [P.sb([128, D], BF16, 'junk%d' % i) for i in range(2)]
    hn = [P.sb([128, D], BF16, 'hn%d' % i) for i in range(2)]
    rt = [P.sb([128, 512], BF16, 'rt%d' % i) for i in range(2)]
    mtmp = P.sb([128, 5, D], F32, 'mtmp')
    ssh = P.sb([128, 16], F32, 'ssh')
    tmp_t = [P.sb([128, D], F32, 'tmpt%d' % i) for i in range(2)]
    wupv = k.w_up[l].rearrange("(kc p) n -> p kc n", p=128)
    wdnv = k.w_down[l].rearrange("(kc p) n -> p kc n", p=128)
    junkN = [P.sb([128, D], BF16, 'junkN')] * 2

    def tiles_of(b):
        t = k.ptiles[4 * b:4 * b + 4]
        return t + [k.stile] if b == 3 else t
    ncols = {0: norm_transpose(k, tiles_of(0), 0, hT, 'hT', junkN, hn)}
    for b in range(4):
        tiles = tiles_of(b)
        ncol = ncols[b]
        chs = chunks_of(ncol)
        cnt = 0
        for fb in range(8):
            wb, wkey = next_wbuf(k)
            wv = wb[:].rearrange("p (kc n) -> p kc n", kc=8)
            wload(k, P, ('up', l, fb), wv, wkey, wupv[:, :, fb * 512:(fb + 1) * 512])
            for m in range(4):
                for ci, (c0, n) in enumerate(chs):
                    if ci == 0:
                        pb = cnt % 2
                        pst = k.ps[pb][:, 0:n]
                        pkey = 'ps%d' % pb
                    else:
                        pst = k.ps[7][:, 256 * (cnt % 2):256 * (cnt % 2) + n]
                        pkey = 'ps7'
                    for kc in range(8):
                        P.op('pe', lambda e, pst=pst, wv=wv, kc=kc, m=m, c0=c0, n=n: e.matmul(
                            pst, lhsT=wv[:, kc, m * 128:(m + 1) * 128], rhs=hT[:, kc, c0:c0 + n], start=(kc == 0), stop=(kc == 7)),
                            r=[wkey, 'hT'], w=[pkey])
                    rslot = cnt % 2
                    P.op('act', lambda e, pst=pst, rslot=rslot, n=n: e.activation(out=rt[rslot][:, 0:n], in_=pst, func=AF.Relu),
                         r=[pkey], w=['rt%d' % rslot])
                    P.op('pool', lambda e, rslot=rslot, n=n, fb=fb, m=m, c0=c0: e.tensor_tensor(
                        out=aT[:, fb * 4 + m, c0:c0 + n], in0=rt[rslot][:, 0:n], in1=rt[rslot][:, 0:n], op=ALU.mult),
                        r=['rt%d' % rslot], w=['aT'])
                    cnt += 1
        if b < 3:
            ncols[b + 1] = norm_transpose(k, tiles_of(b + 1), 0, hT, 'hT', junkN, hn)
        proj_out(k, tiles, aT, 'aT', wdnv, 32, mtmp, ssh, junk, wtag=('dn', l))
        post_norm_add(k, tiles, 1, mtmp, ssh, tmp_t)


def proj_out(k, tiles, srcT, skey, wview, nK, mtmp, ssh, junk, wtag=None):
    P = k.P
    accb = [2, 3, 4, 5, 7]
    nkb = nK // 8
    for half in range(2):
        for kb in range(nkb):
            wb, wkey = next_wbuf(k)
            wv = wb[:].rearrange("p (kc n) -> p kc n", kc=8)
            wload(k, P, (wtag, half, kb), wv, wkey, wview[:, kb * 8:kb * 8 + 8, half * 512:(half + 1) * 512])
            col = 0
            for ti, (X, i, np_, xkey) in enumerate(tiles):
                for m in range(8):
                    if callable(srcT):
                        lhs, lk = srcT(kb * 8 + m, col, np_)
                    else:
                        lhs, lk = srcT[:, kb * 8 + m, col:col + np_], skey
                    P.op('pe', lambda e, ti=ti, np_=np_, kb=kb, m=m, wv=wv, lhs=lhs: e.matmul(
                        k.ps[accb[ti]][0:np_, :], lhsT=lhs, rhs=wv[:, m, :],
                        start=(kb == 0 and m == 0), stop=(kb == nkb - 1 and m == 7)),
                        r=[wkey, lk], w=['ps%d' % accb[ti]])
                col += np_
        for ti, (X, i, np_, xkey) in enumerate(tiles):
            pk = 'ps%d' % accb[ti]
            jslot = ti % 2
            P.op('act', lambda e, ti=ti, np_=np_, half=half, jslot=jslot: e.activation(
                out=junk[jslot][0:np_, 0:512], in_=k.ps[accb[ti]][0:np_, :], func=AF.Square,
                accum_out=ssh[0:np_, 2 * ti + half:2 * ti + half + 1]),
                r=[pk], w=['junk%d' % (jslot if junk[0] is not junk[1] else 0), 'ssh%d' % ti])
            P.op('dve', lambda e, ti=ti, np_=np_, half=half: e.tensor_copy(
                mtmp[0:np_, ti, half * 512:(half + 1) * 512], k.ps[accb[ti]][0:np_, :]),
                r=[pk], w=['mtmp%d' % ti])


DIN = 2048
NXBC = 3072
ZOFF, XOFF, DTOFF = 0, 2048, 5120


def mamba(k, l, j, grp):
    P = k.P
    nc = k.nc
    P.phase_begin()
    phase_common(k, l, (0, 1), nwbuf=2)
    prompt = grp == 'p'
    nseq = 1 if prompt else NSS
    SBT = 2
    Lb = 128 * SBT if prompt else LS
    NCOL = nseq * Lb
    T = 128 if prompt else NSTOK
    nchunk = NCOL // T
    nsb = NT // SBT if prompt else 1
    ntile = SBT if prompt else 1
    cst = k.cst_p if prompt else k.cst_s
    conv_out = k.conv_p if prompt else k.conv_s
    ssm_out = k.ssm_p if prompt else k.ssm_s

    hT = P.sb([128, 8, NCOL], BF16, 'hT')
    zs = P.sb([128, ntile, DIN], BF16, 'zs')
    xbcT = P.sb([128, 24, nseq, 4 + Lb], BF16, 'xbcT')
    xlast = P.sb([128, 24, nseq, 3], F32, 'xlast')
    xcT = [P.sb([128, NCOL], BF16, 'xcT%d' % i) for i in range(2)]
    accs = [P.sb([128, NCOL], F32, 'cacc%d' % i) for i in range(2)]
    BT = P.sb([128, 4, NCOL], BF16, 'BT')
    CT = P.sb([128, 4, NCOL], BF16, 'CT')
    x_tok = P.sb([128, ntile, DIN], BF16, 'x_tok')
    B_tok = P.sb([128, ntile, 512], BF16, 'B_tok')
    gnT = P.sb([128, 16, NCOL], BF16, 'gnT')
    ST = [P.sb([128, 4, 512], F32, 'ST%d' % b) for b in range(nseq)]
    STb = [P.sb([128, 512], BF16, 'STbs%d' % i) for i in range(2)]
    stctr = [0]
    gw_bc = P.sb([128, DIN], F32, 'gw_bc')
    D_bc = P.sb([128, 32], F32, 'D_bc')
    cw = P.sb([128, 24, 4], F32, 'cw')
    cbias = P.sb([128, 24], F32, 'cbias')
    vec = P.sb([64, 4], F32, 'vec')
    PK = P.sb([64, NCOL], F32, 'PK')
    dA = P.sb([64, NCOL], F32, 'dA')
    ones = P.sb([64, 128], F32, 'ones')
    tokpk = P.sb([128, nchunk, 64], F32, 'tokpk')
    Et = P.sb([128, 32], F32, 'Et')
    wend = P.sb([128, 32], F32, 'wend')
    Dtot = P.sb([128, nseq, 32], F32, 'Dtot')
    xdt = P.sb([128, 512], BF16, 'xdt')
    xw = P.sb([128, 512], BF16, 'xw')
    xD = P.sb([128, 512], BF16, 'xD')
    negm = P.sb([128, 128], F32, 'negm')
    dhb = P.sb([128, 512], F32, 'dhb')
    onesF = P.sb([128, 128], F32, 'onesF')
    e4 = [P.sb([128, 512], BF16, 'e4%d' % i) for i in range(2)]
    MT4 = [P.sb([128, 512], BF16, 'MT4%d' % i) for i in range(2)]
    nacs = P.sb([128, 32], F32, 'nacs')
    t1 = P.sb([128, 512], F32, 't1')
    gg = P.sb([128, 512], F32, 'gg')
    gn = P.sb([128, 512], BF16, 'gn')
    junk = [P.sb([128, D], BF16, 'junk0')] * 2
    hn = [P.sb([128, D], BF16, 'hn%d' % i) for i in range(2)]
    mtmp = P.sb([128, ntile, D], F32, 'mtmp')
    ssh = P.sb([128, 16], F32, 'ssh')
    tmp_t = [P.sb([128, D], F32, 'tmpt0')] * 2
    sout = P.sb([128, 4, 128], F32, 'sout')
    if not prompt:
        CTm = P.sb([128, NSS, NSTOK], BF16, 'CTm')
        Bm = P.sb([128, NSS, 128], BF16, 'Bm')
        sst = P.sb([128, 16, 128], F32, 'sst')
        cst_st = P.sb([128, 24, NSS, 3], F32, 'cst_st')

    P.op('sp', lambda e: e.dma_start(out=gw_bc[:], in_=k.gnorm_w[j:j + 1, :].to_broadcast([128, DIN])), w=['gw_bc'], dsem='gw_bc')
    P.op('sp', lambda e: e.dma_start(out=D_bc[:], in_=k.d_skip[j:j + 1, :].to_broadcast([128, 32])), w=['D_bc'], dsem='D_bc')
    with nc.allow_non_contiguous_dma(reason="small param relayout"):
        for kk in range(4):
            P.op('sp', lambda e, kk=kk: e.dma_start(out=cw[:, :, kk], in_=k.conv_w[j, kk].rearrange("(ft p) -> p ft", p=128), allow_slow_non_contiguous=True),
                 w=['cw'], dsem='cw', chain=False)
        P.op('sp', lambda e: e.dma_start(out=cbias[:], in_=k.conv_b[j].rearrange("(ft p) -> p ft", p=128), allow_slow_non_contiguous=True), w=['cbias'], dsem='cbias', chain=False)
        for half in range(2):
            P.op('sp', lambda e, half=half: e.dma_start(out=vec[32 * half:32 * half + 32, 0:1], in_=k.dt_bias[j].rearrange("(p o) -> p o", o=1), allow_slow_non_contiguous=True),
                 w=['vec'], dsem='vec', chain=False)
            P.op('sp', lambda e, half=half: e.dma_start(out=vec[32 * half:32 * half + 32, 1:2], in_=k.a_log[j].rearrange("(p o) -> p o", o=1), allow_slow_non_contiguous=True),
                 w=['vec'], dsem='vec', chain=False)
    P.op('act', lambda e: e.activation(out=vec[0:64, 2:3], in_=vec[0:64, 1:2], func=AF.Exp), r=['vec'], w=['vec2'])
    P.op('dve', lambda e: e.tensor_scalar(out=vec[0:64, 2:3], in0=vec[0:64, 2:3], scalar1=-1.0, scalar2=None, op0=ALU.mult), r=['vec2'], w=['vec2'])
    P.op('dve', lambda e: e.memset(ones[:], 1.0), w=['ones'])
    P.op('sp', lambda e: e.dma_start(out=negm[0:T, 0:T], in_=(k.c_negp if prompt else k.c_negs)[:, :]), w=['negm'], dsem='negm')
    P.op('dve', lambda e: e.memset(onesF[:], 1.0), w=['onesF'])
    if prompt:
        P.op('dve', lambda e: e.memset(xbcT[:, :, :, 0:4], 0.0), w=['xbcT'])
        P.op('dve', lambda e: e.memset(ST[0][:], 0.0), w=['ST0'])
    else:
        with nc.allow_non_contiguous_dma(reason="conv state relayout"):
            for b in range(NSS):
                for r_ in range(3):
                    P.op('sp', lambda e, b=b, r_=r_: e.dma_start(out=cst_st[:, :, b, r_], in_=k.state_conv[j, b, r_].rearrange("(ft p) -> p ft", p=128),
                                                                 allow_slow_non_contiguous=True), w=['cst_st'], dsem='cst_st', chain=False)
        P.op('dve', lambda e: e.tensor_copy(xbcT[:, :, :, 0:3], cst_st[:]), r=['cst_st'], w=['xbcT'])
        for b in range(NSS):
            sv = k.state_ssm[j, b].rearrange("h p n -> (h p) n").rearrange("(q r) n -> r q n", r=128)
            P.op('sp', lambda e, sv=sv: e.dma_start(out=sst[:], in_=sv), w=['sst'], dsem='sst')
            for gq in range(4):
                pb = 2 + gq % 2
                for qq in range(4):
                    P.op('pe', lambda e, gq=gq, qq=qq, pb=pb: e.transpose(k.ps[pb][:, qq * 128:(qq + 1) * 128], sst[:, gq * 4 + qq, :], k.identF[:]),
                         r=['sst', 'identF'], w=['ps%d' % pb])
                P.op('dve', lambda e, b=b, gq=gq, pb=pb: e.tensor_copy(ST[b][:, gq, :], k.ps[pb][:, :]), r=['ps%d' % pb], w=['ST%d' % b])

    wv_in = k.w_in_odd[j].rearrange("(kc p) n -> p kc n", p=128)
    wov = k.w_out_odd[j].rearrange("(kc p) n -> p kc n", p=128)
    maskc = cst['mask']
    for sb in range(nsb):
        tiles = k.ptiles[SBT * sb:SBT * sb + SBT] if prompt else [k.stile]
        last = sb == nsb - 1
        norm_transpose(k, tiles, 0, hT, 'hT', junk, hn)
        cnt = 0
        for cb in range(4):
            wb, wkey = next_wbuf(k)
            wv = wb[:].rearrange("p (kc n) -> p kc n", kc=8)
            wload(k, P, ('oddz', j, cb), wv, wkey, wv_in[:, :, ZOFF + cb * 512:ZOFF + (cb + 1) * 512])
            col = 0
            for ti, (X, i, np_, xkey) in enumerate(tiles):
                pb = cnt % 2
                cnt += 1
                for kc in range(8):
                    P.op('pe', lambda e, pb=pb, np_=np_, kc=kc, col=col, wv=wv: e.matmul(
                        k.ps[pb][0:np_, :], lhsT=hT[:, kc, col:col + np_], rhs=wv[:, kc, :], start=(kc == 0), stop=(kc == 7)),
                        r=[wkey, 'hT'], w=['ps%d' % pb])
                P.op('act', lambda e, pb=pb, np_=np_, ti=ti, cb=cb: e.activation(
                    out=zs[0:np_, ti, cb * 512:(cb + 1) * 512], in_=k.ps[pb][0:np_, :], func=AF.Silu), r=['ps%d' % pb], w=['zs'])
                col += np_
        for cb in range(6):
            wb, wkey = next_wbuf(k)
            wv = wb[:].rearrange("p (kc n) -> p kc n", kc=8)
            wload(k, P, ('oddx', j, cb), wv, wkey, wv_in[:, :, XOFF + cb * 512:XOFF + (cb + 1) * 512])
            for m in range(4):
                ft = cb * 4 + m
                pb = 2 + ft % 2
                for kc in range(8):
                    P.op('pe', lambda e, pb=pb, kc=kc, m=m, wv=wv: e.matmul(
                        k.ps[pb][:, 0:NCOL], lhsT=wv[:, kc, m * 128:(m + 1) * 128], rhs=hT[:, kc, 0:NCOL], start=(kc == 0), stop=(kc == 7)),
                        r=[wkey, 'hT'], w=['ps%d' % pb])
                src = k.ps[pb][:, 0:NCOL].rearrange("p (b t) -> p b t", b=nseq)
                if ft % 2 == 0:
                    P.op('dve', lambda e, ft=ft, src=src: e.tensor_copy(xbcT[:, ft, :, 3:3 + Lb], src), r=['ps%d' % pb], w=['xbcT'])
                else:
                    P.op('act', lambda e, ft=ft, src=src: e.activation(out=xbcT[:, ft, :, 3:3 + Lb], in_=src, func=AF.Copy), r=['ps%d' % pb], w=['xbcT'])
                if last:
                    P.op('dve', lambda e, ft=ft, src=src: e.tensor_copy(xlast[:, ft, :, :], src[:, :, Lb - 3:Lb]), r=['ps%d' % pb], w=['xlast'])
        wb, wkey = next_wbuf(k)
        wv = wb[:, 0:512].rearrange("p (kc n) -> p kc n", kc=8)
        for half in range(2):
            P.op('pool', lambda e, wv=wv, half=half: e.dma_start(out=wv[:, :, 32 * half:32 * half + 32], in_=wv_in[:, :, DTOFF:DTOFF + 32]),
                 w=[wkey], dsem=wkey, hoist=True)
        for kc in range(8):
            P.op('pe', lambda e, kc=kc, wv=wv: e.matmul(k.ps[4][0:64, 0:NCOL], lhsT=wv[:, kc, 0:64], rhs=hT[:, kc, 0:NCOL], start=(kc == 0), stop=(kc == 7)),
                 r=[wkey, 'hT'], w=['ps4'])
        P.op('act', lambda e: e.activation(out=PK[0:64, :], in_=k.ps[4][0:64, 0:NCOL], func=AF.Exp, bias=vec[0:64, 0:1]), r=['ps4', 'vec'], w=['PK'])
        P.op('act', lambda e: e.activation(out=PK[0:64, :], in_=PK[0:64, :], func=AF.Ln, bias=1.0), r=['PK'], w=['PK'])
        P.op('dve', lambda e: e.tensor_scalar(out=dA[32:64, :], in0=PK[32:64, :], scalar1=vec[32:64, 2:3], scalar2=None, op0=ALU.mult),
             r=['PK', 'vec2'], w=['dA'])
        slen = 128 if prompt else LS
        for s0 in range(0, NCOL, slen):
            P.op('dve', lambda e, s0=s0: e.tensor_tensor_scan(out=PK[32:64, s0:s0 + slen], data0=ones[32:64, 0:slen], data1=dA[32:64, s0:s0 + slen],
                                                             initial=0.0, op0=ALU.mult, op1=ALU.add), r=['dA', 'ones', 'PK'], w=['PK'])
        for c in range(nchunk):
            P.op('pe', lambda e, c=c: e.transpose(k.ps[5][0:T, 0:64], PK[0:64, c * T:(c + 1) * T], k.identF[0:64, 0:64]), r=['PK', 'identF'], w=['ps5'])
            P.op('dve', lambda e, c=c: e.tensor_copy(tokpk[0:T, c, :], k.ps[5][0:T, 0:64]), r=['ps5'], w=['tokpk'])
        for ft in range(24):
            a3 = accs[ft % 2][:, 0:NCOL].rearrange("p (b t) -> p b t", b=nseq)
            akey = 'cacc%d' % (ft % 2)
            P.op('pool', lambda e, a3=a3, ft=ft: e.tensor_scalar(out=a3, in0=xbcT[:, ft, :, 3:3 + Lb], scalar1=cw[:, ft, 3:4], scalar2=0.0,
                                                                 op0=ALU.mult, op1=ALU.add), r=['xbcT', 'cw'], w=[akey])
            for kk in (2, 1, 0):
                P.op('dve', lambda e, a3=a3, ft=ft, kk=kk: e.scalar_tensor_tensor(out=a3, in0=xbcT[:, ft, :, kk:kk + Lb], scalar=cw[:, ft, kk:kk + 1],
                                                                                   in1=a3, op0=ALU.mult, op1=ALU.add), r=['xbcT', 'cw', akey], w=[akey])
            if ft < 16:
                dst, dkey = xcT[ft % 2][:, 0:NCOL], 'xcT%d' % (ft % 2)
            elif ft < 20:
                dst, dkey = BT[:, ft - 16, 0:NCOL], 'BT'
            else:
                dst, dkey = CT[:, ft - 20, 0:NCOL], 'CT'
            P.op('act', lambda e, dst=dst, ft=ft: e.activation(out=dst, in_=accs[ft % 2][:, 0:NCOL], func=AF.Silu, bias=cbias[:, ft:ft + 1]),
                 r=[akey, 'cbias'], w=[dkey])
            if ft < 20:
                col = 0
                for ti, (X, i, np_, xkey) in enumerate(tiles):
                    P.op('pe', lambda e, ti=ti, np_=np_, col=col, dst=dst, ft=ft: e.transpose(
                        k.psb[ti][0:np_, (ft % 8) * 128:(ft % 8 + 1) * 128], dst[:, col:col + np_], k.ident[:, :]),
                        r=[dkey, 'ident'], w=['ps%d' % ti])
                    col += np_
                if ft % 8 == 7 or ft == 19:
                    for ti, (X, i, np_, xkey) in enumerate(tiles):
                        if ft < 16:
                            o_ap = x_tok[0:np_, ti, (ft // 8) * 1024:(ft // 8 + 1) * 1024]
                            i_ap = k.psb[ti][0:np_, :]
                            okey = 'x_tok'
                        else:
                            o_ap = B_tok[0:np_, ti, :]
                            i_ap = k.psb[ti][0:np_, 0:512]
                            okey = 'B_tok'
                        if ti % 2 == 0:
                            P.op('dve', lambda e, o_ap=o_ap, i_ap=i_ap: e.tensor_copy(o_ap, i_ap), r=['ps%d' % ti], w=[okey])
                        else:
                            P.op('act', lambda e, o_ap=o_ap, i_ap=i_ap: e.activation(out=o_ap, in_=i_ap, func=AF.Copy), r=['ps%d' % ti], w=[okey])
        if not last:
            P.op('dve', lambda e: e.tensor_copy(xbcT[:, :, :, 0:3], xbcT[:, :, :, Lb:Lb + 3]), r=['xbcT'], w=['xbcT'])
        for c in range(nchunk):
            ti = c if prompt else 0
            c0 = c * T
            dt_tok = tokpk[0:T, c, 0:32]
            acs = tokpk[0:T, c, 32:64]
            P.op('act', lambda e, acs=acs: e.activation(out=Et[0:T, :], in_=acs, func=AF.Exp), r=['tokpk'], w=['Et'])
            P.op('dve', lambda e, acs=acs: e.tensor_scalar(out=nacs[0:T, :], in0=acs, scalar1=-1.0, scalar2=None, op0=ALU.mult), r=['tokpk'], w=['nacs'])
            P.op('pe', lambda e, acs=acs: e.matmul(k.ps[5][0:T, 0:32], lhsT=cst['selend'][0:T, 0:T], rhs=acs, start=True, stop=True),
                 r=['tokpk', 'cst'], w=['ps5'])
            P.op('dve', lambda e, acs=acs: e.tensor_tensor(out=wend[0:T, :], in0=k.ps[5][0:T, 0:32], in1=acs, op=ALU.subtract), r=['ps5', 'tokpk'], w=['wend'])
            P.op('act', lambda e: e.activation(out=wend[0:T, :], in_=wend[0:T, :], func=AF.Exp), r=['wend'], w=['wend'])
            for b in range(nseq):
                P.op('pe', lambda e, acs=acs, b=b: e.matmul(k.ps[5][:, 64 + 32 * b:96 + 32 * b], lhsT=cst['selendB'][0:T, b, :], rhs=acs, start=True, stop=True),
                     r=['tokpk', 'cst'], w=['ps5'])
            P.op('act', lambda e: e.activation(out=Dtot[:, :, :], in_=k.ps[5][:, 64:64 + 32 * nseq].rearrange("p (b h) -> p b h", b=nseq), func=AF.Exp),
                 r=['ps5'], w=['Dtot'])
            psA = (3, 4)

            def stageA(gq, acs=acs):
                hs = gq * 8
                for q in range(2):
                    ab = psA[q]
                    dh3 = dhb[0:T, :].rearrange("p (j l) -> p j l", l=128)[:, :, 0:T]
                    P.op('dve', lambda e, dh3=dh3, q=q, hs=hs, acs=acs: e.tensor_tensor(
                        out=dh3, in0=k.identF[0:T, 0:T].unsqueeze(1).to_broadcast([T, 4, T]),
                        in1=acs[:, hs + 4 * q:hs + 4 * q + 4].unsqueeze(2).to_broadcast([T, 4, T]), op=ALU.mult), r=['identF', 'tokpk', 'dhb'], w=['dhb'])
                    if T == 128:
                        P.op('pe', lambda e, ab=ab: e.matmul(k.ps[ab][0:T, :], lhsT=onesF[0:T, 0:T], rhs=dhb[0:T, :], start=True, stop=False), r=['onesF', 'dhb'], w=['ps%d' % ab])
                    for jj in range(4):
                        o_ap = k.ps[ab][0:T, jj * 128:jj * 128 + T]
                        if T != 128:
                            P.op('pe', lambda e, o_ap=o_ap, jj=jj: e.matmul(o_ap, lhsT=onesF[0:T, 0:T], rhs=dhb[0:T, jj * 128:jj * 128 + T], start=True, stop=False),
                                 r=['onesF', 'dhb'], w=['ps%d' % ab])
                        P.op('pe', lambda e, o_ap=o_ap, jj=jj: e.matmul(o_ap, lhsT=k.identF[0:T, 0:T], rhs=negm[0:T, 0:T], start=False, stop=(T != 128 or jj == 3)),
                             r=['identF', 'negm'], w=['ps%d' % ab])

            for gq in range(4):
                hs = gq * 8
                xg = x_tok[0:T, ti, gq * 512:(gq + 1) * 512].rearrange("p (h d) -> p h d", h=8)
                P.op('dve', lambda e, xg=xg, dt_tok=dt_tok, hs=hs: e.tensor_tensor(
                    out=xdt[0:T, :].rearrange("p (h d) -> p h d", h=8), in0=xg, in1=dt_tok[:, hs:hs + 8].unsqueeze(2).to_broadcast([T, 8, 64]), op=ALU.mult),
                    r=['x_tok', 'tokpk'], w=['xdt'])
                P.op('pool', lambda e, hs=hs: e.tensor_tensor(
                    out=xw[0:T, :].rearrange("p (h d) -> p h d", h=8), in0=xdt[0:T, :].rearrange("p (h d) -> p h d", h=8),
                    in1=wend[0:T, hs:hs + 8].unsqueeze(2).to_broadcast([T, 8, 64]), op=ALU.mult), r=['xdt', 'wend'], w=['xw'])
                P.op('pool', lambda e, xg=xg, hs=hs: e.tensor_tensor(
                    out=xD[0:T, :].rearrange("p (h d) -> p h d", h=8), in0=xg, in1=D_bc[0:T, hs:hs + 8].unsqueeze(2).to_broadcast([T, 8, 64]), op=ALU.mult),
                    r=['x_tok', 'D_bc'], w=['xD'])
                P.op('pe', lambda e, gq=gq, c0=c0: e.matmul(k.ps[2][0:T, 0:T], lhsT=BT[:, gq, c0:c0 + T], rhs=CT[:, gq, c0:c0 + T], start=True, stop=True),
                     r=['BT', 'CT'], w=['ps2'])
                P.op('pe', lambda e: e.matmul(k.ps[0][0:T, :], lhsT=k.ident[0:T, 0:T], rhs=xD[0:T, :], start=True, stop=False), r=['ident', 'xD'], w=['ps0'])
                if gq == 0:
                    stageA(0)
                for jh in range(8):
                    h = hs + jh
                    ab = psA[jh // 4]
                    q = jh // 4
                    P.op('act', lambda e, h=h, ab=ab, q=q, jh=jh: e.activation(out=e4[q][0:T, (jh % 4) * 128:(jh % 4) * 128 + T],
                                                                             in_=k.ps[ab][0:T, (jh % 4) * 128:(jh % 4) * 128 + T], func=AF.Exp, bias=nacs[0:T, h:h + 1]),
                         r=['ps%d' % ab, 'nacs'], w=['e4%d' % q])
                if gq < 3:
                    stageA(gq + 1)
                for q in range(2):
                    P.op('dve', lambda e, q=q: e.tensor_tensor(
                        out=MT4[q][0:T, :].rearrange("p (j l) -> p j l", l=128)[:, :, 0:T], in0=e4[q][0:T, :].rearrange("p (j l) -> p j l", l=128)[:, :, 0:T],
                        in1=k.ps[2][0:T, 0:T].unsqueeze(1).to_broadcast([T, 4, T]), op=ALU.mult), r=['e4%d' % q, 'ps2'], w=['MT4%d' % q])
                for jh in range(8):
                    q = jh // 4
                    P.op('pe', lambda e, q=q, jh=jh: e.matmul(k.ps[0][0:T, jh * 64:(jh + 1) * 64], lhsT=MT4[q][0:T, (jh % 4) * 128:(jh % 4) * 128 + T],
                                                             rhs=xdt[0:T, jh * 64:(jh + 1) * 64], start=False, stop=(jh == 7)), r=['MT4%d' % q, 'xdt'], w=['ps0'])
                def st_bf16(b, gq):
                    sl_ = stctr[0] % 2
                    stctr[0] += 1
                    P.op('act', lambda e, b=b, gq=gq, sl_=sl_: e.activation(out=STb[sl_][:, :], in_=ST[b][:, gq, :], func=AF.Copy),
                         r=['ST%d' % b], w=['STbs%d' % sl_])
                    return STb[sl_], 'STbs%d' % sl_
                if prompt:
                    stb, stk = st_bf16(0, gq)
                    P.op('pe', lambda e, gq=gq, c0=c0, stb=stb: e.matmul(k.ps[1][0:T, :], lhsT=CT[:, gq, c0:c0 + T], rhs=stb[:, :], start=True, stop=True),
                         r=['CT', stk], w=['ps1'])
                else:
                    P.op('dve', lambda e, gq=gq: e.tensor_tensor(out=CTm[:, :, :], in0=CT[:, gq, 0:NSTOK].unsqueeze(1).to_broadcast([128, NSS, NSTOK]),
                                                                 in1=cst['seqcol'][:, :, :], op=ALU.mult), r=['CT', 'cst'], w=['CTm'])
                    for b in range(NSS):
                        stb, stk = st_bf16(b, gq)
                        P.op('pe', lambda e, gq=gq, b=b, stb=stb: e.matmul(k.ps[1][0:T, :], lhsT=CTm[:, b, :], rhs=stb[:, :], start=(b == 0), stop=(b == NSS - 1)),
                             r=['CTm', stk], w=['ps1'])
                P.op('dve', lambda e, hs=hs: e.tensor_tensor(out=t1[0:T, :].rearrange("p (h d) -> p h d", h=8), in0=k.ps[1][0:T, :].rearrange("p (h d) -> p h d", h=8),
                                                            in1=Et[0:T, hs:hs + 8].unsqueeze(2).to_broadcast([T, 8, 64]), op=ALU.mult), r=['ps1', 'Et'], w=['t1'])
                P.op('dve', lambda e: e.tensor_tensor(out=t1[0:T, :], in0=t1[0:T, :], in1=k.ps[0][0:T, :], op=ALU.add), r=['t1', 'ps0'], w=['t1'])
                P.op('pool', lambda e, ti=ti, gq=gq: e.tensor_tensor(out=gg[0:T, :], in0=t1[0:T, :], in1=zs[0:T, ti, gq * 512:(gq + 1) * 512], op=ALU.mult),
                     r=['t1', 'zs'], w=['gg'])
                s_ = small_slot(k)
                sk = 'sm%d' % s_
                ss = k.small[0:T, s_:s_ + 1]
                rs = k.small[0:T, s_ + 1:s_ + 2]
                P.op('act', lambda e, ss=ss: e.activation(out=junk[0][0:T, 0:512], in_=gg[0:T, :], func=AF.Square, accum_out=ss), r=['gg'], w=['junk0', sk])
                P.op('act', lambda e, ss=ss, rs=rs: e.activation(out=rs, in_=ss, func=AF.Ln, scale=1.0 / 512, bias=EPS), r=[sk], w=[sk + 'r'])
                P.op('act', lambda e, rs=rs: e.activation(out=rs, in_=rs, func=AF.Exp, scale=-0.5), r=[sk + 'r'], w=[sk + 'r'])
                P.op('dve', lambda e, rs=rs, gq=gq: e.scalar_tensor_tensor(out=gn[0:T, :], in0=gg[0:T, :], scalar=rs, in1=gw_bc[0:T, gq * 512:(gq + 1) * 512],
                                                                          op0=ALU.mult, op1=ALU.mult), r=['gg', sk + 'r', 'gw_bc'], w=['gn'])
                pb = 6 + gq % 2
                for q in range(4):
                    P.op('pe', lambda e, q=q, pb=pb: e.transpose(k.psb[pb][:, q * 128:q * 128 + T], gn[0:T, q * 128:(q + 1) * 128], k.ident[0:T, 0:T]),
                         r=['gn', 'ident'], w=['ps%d' % pb])
                P.op('act', lambda e, gq=gq, c0=c0, pb=pb: e.activation(out=gnT[:, gq * 4:gq * 4 + 4, c0:c0 + T],
                                                                      in_=k.psb[pb][:, 0:512].rearrange("p (q t) -> p q t", q=4)[:, :, 0:T], func=AF.Copy),
                     r=['ps%d' % pb], w=['gnT'])
                for b in range(nseq):
                    if prompt:
                        lhs = B_tok[0:T, ti, gq * 128:(gq + 1) * 128]
                        lkey = 'B_tok'
                    else:
                        P.op('dve', lambda e, b=b, gq=gq: e.tensor_scalar(out=Bm[0:T, b, :], in0=B_tok[0:T, 0, gq * 128:(gq + 1) * 128],
                                                                         scalar1=cst['seqrow'][0:T, b:b + 1], scalar2=None, op0=ALU.mult),
                             r=['B_tok', 'cst'], w=['Bm%d' % b])
                        lhs = Bm[0:T, b, :]
                        lkey = 'Bm%d' % b
                    P.op('pe', lambda e, lhs=lhs: e.matmul(k.ps[5][:, :], lhsT=lhs, rhs=xw[0:T, :], start=True, stop=True), r=[lkey, 'xw'], w=['ps5'])
                    stv = ST[b][:, gq, :].rearrange("p (h d) -> p h d", h=8)
                    P.op('pool', lambda e, stv=stv, b=b, hs=hs: e.tensor_tensor(out=stv, in0=stv, in1=Dtot[:, b, hs:hs + 8].unsqueeze(2).to_broadcast([128, 8, 64]), op=ALU.mult),
                         r=['ST%d' % b, 'Dtot'], w=['ST%d' % b])
                    P.op('dve', lambda e, b=b, gq=gq: e.tensor_tensor(out=ST[b][:, gq, :], in0=ST[b][:, gq, :], in1=k.ps[5][:, :], op=ALU.add),
                         r=['ST%d' % b, 'ps5'], w=['ST%d' % b])
        proj_out(k, tiles, gnT, 'gnT', wov, 16, mtmp, ssh, junk, wtag=('oddo', j))
        post_norm_add(k, tiles, 1, mtmp, ssh, tmp_t)
    with nc.allow_non_contiguous_dma(reason="conv state relayout"):
        for b in range(nseq):
            for r_ in range(3):
                dst = (conv_out[j, r_] if prompt else conv_out[j, b, r_]).rearrange("(ft p) -> p ft", p=128)
                P.op('sp', lambda e, dst=dst, b=b, r_=r_: e.dma_start(out=dst, in_=xlast[:, :, b, r_], allow_slow_non_contiguous=True), r=['xlast'], dsem='xlast', chain=False)
    for b in range(nseq):
        dv = (ssm_out[j] if prompt else ssm_out[j, b]).rearrange("h p n -> (h p) n")
        for gq in range(4):
            pb = 2 + gq % 2
            for qq in range(4):
                P.op('pe', lambda e, b=b, gq=gq, qq=qq, pb=pb: e.transpose(k.ps[pb][:, qq * 128:(qq + 1) * 128], ST[b][:, gq, qq * 128:(qq + 1) * 128], k.identF[:]),
                     r=['ST%d' % b, 'identF'], w=['ps%d' % pb])
            P.op('dve', lambda e, pb=pb: e.tensor_copy(sout[:].rearrange("p q n -> p (q n)"), k.ps[pb][:, :]), r=['ps%d' % pb], w=['sout'])
            P.op('sp', lambda e, dv=dv, gq=gq: e.dma_start(out=dv[gq * 512:(gq + 1) * 512, :].rearrange("(q r) n -> r q n", r=128), in_=sout[:]),
                 r=['sout'], dsem='sout')


TWO_PI = 6.283185307179586
GELU_C = 0.044715
GELU_S = 1.5957691216057308
I32 = mybir.dt.int32


def s5_phase(k, l, j, grp):
    P = k.P
    P.phase_begin()
    prompt = grp == 'p'
    nseq = 1 if prompt else NSS
    Lseq = SEQ if prompt else LS
    NTOK = nseq * Lseq
    Lh = 512 if prompt else LS
    nsegs = Lseq // Lh
    SC = nseq * Lh if not prompt else Lh
    nstep = NTOK // SC
    o_bT = P.sb([128, 4, NTOK], BF16, 'o_bT')
    k.o_bT = o_bT
    k.o_bT_bytes = P.cur - P.pers
    phase_common(k, l, (0,), nwbuf=1)
    ncolb = 512 if prompt else NSTOK
    hT = P.sb([128, 8, ncolb], BF16, 'hT')
    uT = P.sb([128, 4, NTOK], BF16, 'uT')
    junk = [P.sb([128, D], BF16, 'junk0')] * 2
    hn = [P.sb([128, D], BF16, 'hn0')] * 2
    LR = P.sb([128, 32], F32, 'LR')
    LI = P.sb([128, 32], F32, 'LI')
    DT = P.sb([128, 32], F32, 'DT')
    RHO = P.sb([128, 32], F32, 'RHO')
    TH = P.sb([128, 32], F32, 'TH')
    K1 = P.sb([128, 32], F32, 'K1')
    K2 = P.sb([128, 32], F32, 'K2')
    K1s = P.sb([128, 32], F32, 'K1s')
    K2s = P.sb([128, 32], F32, 'K2s')
    pt = [P.sb([128, 32], F32, 'pt%d' % i) for i in range(8)]
    pti = P.sb([128, 32], I32, 'pti')
    X1 = P.sb([128, 32, 16], F32, 'X1')
    X2 = P.sb([128, 32, 16], F32, 'X2')
    BB = P.sb([128, 32, 16], F32, 'BB')
    BBs = P.sb([128, 32, 16], F32, 'BBs')
    Bblk = P.sb([128, 32, 128], BF16, 'Bblk')
    Bblks = P.sb([128, 32, 128], BF16, 'Bblks')
    Cnat = P.sb([128, 4, 2, 64], F32, 'Cnat')
    CC = P.sb([128, 4, 128], F32, 'CC')
    Cblk = P.sb([128, 32, 128], BF16, 'Cblk')
    Dv = P.sb([128, 4], F32, 'Dv')
    bglu = P.sb([128, 4], F32, 'bglu')
    H0 = P.sb([128, nseq, 32], F32, 'H0')
    Hlast = P.sb([128, nseq, 32], F32, 'Hlast')
    carry = P.sb([128, 2], F32, 'carry')
    COS = P.sb([128, Lh], F32, 'COS')
    SIN = P.sb([128, Lh], F32, 'SIN')
    RH = P.sb([128, Lh], F32, 'RH')
    phi = P.sb([128, Lh], F32, 'phi')
    qi = P.sb([128, Lh], I32, 'qi')
    qf = P.sb([128, Lh], F32, 'qf')
    s2 = P.sb([128, Lh], F32, 's2')
    s4 = P.sb([128, Lh], F32, 's4')
    a1 = P.sb([128, SC], F32, 'a1')
    a2 = P.sb([128, SC], F32, 'a2')
    Sp = P.sb([128, SC], F32, 'Sp')
    gb = P.sb([128, SC], BF16, 'gb')
    Hb = P.sb([128, SC], BF16, 'Hb')
    gl = [P.sb([128, ncolb], F32, 'gl%d' % i) for i in range(2)]
    swp = k.swp

    for half in range(2):
        rows = slice(64 * half, 64 * half + 64)
        P.op('sp', lambda e, rows=rows: e.dma_start(out=LR[rows, :], in_=k.s5_lambda_re[j].rearrange("g p -> p g"), allow_slow_non_contiguous=True), w=['LR'], dsem='s5p', chain=False)
        P.op('sp', lambda e, rows=rows: e.dma_start(out=LI[rows, :], in_=k.s5_lambda_im[j].rearrange("g p -> p g"), allow_slow_non_contiguous=True), w=['LI'], dsem='s5p', chain=False)
        bre = k.s5_b_re[j].rearrange("g p c -> p g c")
        bim = k.s5_b_im[j].rearrange("g p c -> p g c")
        P.op('sp', lambda e, rows=rows, half=half, bre=bre, bim=bim: e.dma_start(out=X1[rows, :, :], in_=(bre if half == 0 else bim)), w=['X1'], dsem='s5p', chain=False)
        P.op('sp', lambda e, rows=rows, half=half, bre=bre, bim=bim: e.dma_start(out=X2[rows, :, :], in_=(bim if half == 0 else bre)), w=['X2'], dsem='s5p', chain=False)
    P.op('sp', lambda e: e.dma_start(out=DT[:], in_=k.s5_log_dt[j:j + 1, :].to_broadcast([128, 32])), w=['DT'], dsem='s5p', chain=False)
    P.op('sp', lambda e: e.dma_start(out=Cnat[:, :, 0, :], in_=k.s5_c_re[j].rearrange("g c p -> (g c) p").rearrange("(m r) p -> r m p", r=128)), w=['Cnat'], dsem='s5p', chain=False)
    P.op('sp', lambda e: e.dma_start(out=Cnat[:, :, 1, :], in_=k.s5_c_im[j].rearrange("g c p -> (g c) p").rearrange("(m r) p -> r m p", r=128)), w=['Cnat'], dsem='s5p', chain=False)
    P.op('sp', lambda e: e.dma_start(out=Dv[:], in_=k.s5_d[j].rearrange("g c -> (g c)").rearrange("(m r) -> r m", r=128), allow_slow_non_contiguous=True), w=['Dv'], dsem='s5p', chain=False)
    P.op('sp', lambda e: e.dma_start(out=bglu[:], in_=k.s5_b_glu[j].rearrange("(m r) -> r m", r=128), allow_slow_non_contiguous=True), w=['bglu'], dsem='s5p', chain=False)
    if prompt:
        P.op('dve', lambda e: e.memset(H0[:], 0.0), w=['H0'])
    else:
        for b in range(NSS):
            for ri in range(2):
                P.op('sp', lambda e, b=b, ri=ri: e.dma_start(out=H0[64 * ri:64 * ri + 64, b, :], in_=k.state_s5[j, b].rearrange("g p r -> p g r")[:, :, ri],
                                                             allow_slow_non_contiguous=True), w=['H0'], dsem='s5p', chain=False)

    def reduce_sincos(eng_note, ang, n, qi_, qf_, r_, s2_, s4_, cos_out, sin_out, keys):
        P.op('dve', lambda e: e.tensor_scalar(out=qi_, in0=ang, scalar1=1.0 / TWO_PI, scalar2=None, op0=ALU.mult), r=keys['ang'], w=keys['qi'])
        P.op('dve', lambda e: e.tensor_copy(qf_, qi_), r=keys['qi'], w=keys['qf'])
        P.op('dve', lambda e: e.scalar_tensor_tensor(out=r_, in0=qf_, scalar=-TWO_PI, in1=ang, op0=ALU.mult, op1=ALU.add), r=keys['qf'] + keys['ang'], w=keys['r'])
        P.op('act', lambda e: e.activation(out=s4_, in_=r_, func=AF.Sin, scale=0.25), r=keys['r'], w=keys['s4'])
        P.op('act', lambda e: e.activation(out=s2_, in_=r_, func=AF.Sin, scale=0.5), r=keys['r'], w=keys['s2'])
        P.op('pool', lambda e: e.tensor_tensor(out=s4_, in0=s4_, in1=s4_, op=ALU.mult), r=keys['s4'], w=keys['s4'])
        P.op('pool', lambda e: e.tensor_scalar(out=s4_, in0=s4_, scalar1=-2.0, scalar2=1.0, op0=ALU.mult, op1=ALU.add), r=keys['s4'], w=keys['s4'])
        P.op('dve', lambda e: e.scalar_tensor_tensor(out=sin_out, in0=s2_, scalar=2.0, in1=s4_, op0=ALU.mult, op1=ALU.mult), r=keys['s2'] + keys['s4'], w=keys['sin'])
        P.op('pool', lambda e: e.tensor_tensor(out=s2_, in0=s2_, in1=s2_, op=ALU.mult), r=keys['s2'] + keys['sin'], w=keys['s2'])
        P.op('pool', lambda e: e.tensor_scalar(out=cos_out, in0=s2_, scalar1=-2.0, scalar2=1.0, op0=ALU.mult, op1=ALU.add), r=keys['s2'], w=keys['cos'])

    P.op('act', lambda e: e.activation(out=DT[:], in_=DT[:], func=AF.Exp), r=['DT'], w=['DT'])
    P.op('dve', lambda e: e.tensor_tensor(out=pt[0][:], in0=LR[:], in1=DT[:], op=ALU.mult), r=['LR', 'DT'], w=['pt0'])
    P.op('dve', lambda e: e.tensor_tensor(out=TH[:], in0=LI[:], in1=DT[:], op=ALU.mult), r=['LI', 'DT'], w=['TH'])
    P.op('act', lambda e: e.activation(out=RHO[:], in_=pt[0][:], func=AF.Exp), r=['pt0'], w=['RHO'])
    kk = {'ang': ['TH'], 'qi': ['pti'], 'qf': ['pt1'], 'r': ['pt2'], 's4': ['pt3'], 's2': ['pt4'], 'sin': ['pt5'], 'cos': ['pt6']}
    reduce_sincos('p', TH[:], 32, pti[:], pt[1][:], pt[2][:], pt[4][:], pt[3][:], pt[6][:], pt[5][:], kk)
    P.op('dve', lambda e: e.tensor_tensor(out=pt[0][:], in0=RHO[:], in1=pt[6][:], op=ALU.mult), r=['RHO', 'pt6'], w=['pt0'])
    P.op('dve', lambda e: e.tensor_scalar(out=pt[0][:], in0=pt[0][:], scalar1=-1.0, scalar2=None, op0=ALU.add), r=['pt0'], w=['pt0'])
    P.op('dve', lambda e: e.tensor_tensor(out=pt[1][:], in0=RHO[:], in1=pt[5][:], op=ALU.mult), r=['RHO', 'pt5', 'pt1'], w=['pt1'])
    P.op('dve', lambda e: e.tensor_tensor(out=pt[2][:], in0=LR[:], in1=LR[:], op=ALU.mult), r=['LR', 'pt2'], w=['pt2'])
    P.op('dve', lambda e: e.tensor_tensor(out=pt[3][:], in0=LI[:], in1=LI[:], op=ALU.mult), r=['LI', 'pt3'], w=['pt3'])
    P.op('dve', lambda e: e.tensor_tensor(out=pt[2][:], in0=pt[2][:], in1=pt[3][:], op=ALU.add), r=['pt2', 'pt3'], w=['pt2'])
    P.op('dve', lambda e: e.reciprocal(out=pt[2][:], in_=pt[2][:]), r=['pt2'], w=['pt2'])
    P.op('dve', lambda e: e.tensor_tensor(out=pt[3][:], in0=pt[0][:], in1=LR[:], op=ALU.mult), r=['pt0', 'LR', 'pt3'], w=['pt3'])
    P.op('dve', lambda e: e.tensor_tensor(out=pt[4][:], in0=pt[1][:], in1=LI[:], op=ALU.mult), r=['pt1', 'LI', 'pt4'], w=['pt4'])
    P.op('dve', lambda e: e.tensor_tensor(out=pt[3][:], in0=pt[3][:], in1=pt[4][:], op=ALU.add), r=['pt3', 'pt4'], w=['pt3'])
    P.op('dve', lambda e: e.tensor_tensor(out=K1[:], in0=pt[3][:], in1=pt[2][:], op=ALU.mult), r=['pt3', 'pt2'], w=['K1'])
    P.op('dve', lambda e: e.tensor_tensor(out=pt[3][:], in0=pt[1][:], in1=LR[:], op=ALU.mult), r=['pt1', 'LR', 'pt3'], w=['pt3'])
    P.op('dve', lambda e: e.tensor_tensor(out=pt[4][:], in0=pt[0][:], in1=LI[:], op=ALU.mult), r=['pt0', 'LI', 'pt4'], w=['pt4'])
    P.op('dve', lambda e: e.tensor_tensor(out=pt[3][:], in0=pt[3][:], in1=pt[4][:], op=ALU.subtract), r=['pt3', 'pt4'], w=['pt3'])
    P.op('dve', lambda e: e.tensor_tensor(out=pt[7][:], in0=pt[3][:], in1=pt[2][:], op=ALU.mult), r=['pt3', 'pt2'], w=['pt7'])
    P.op('dve', lambda e: e.tensor_scalar(out=K2[0:64, :], in0=pt[7][0:64, :], scalar1=-1.0, scalar2=None, op0=ALU.mult), r=['pt7'], w=['K2'])
    P.op('dve', lambda e: e.tensor_copy(K2[64:128, :], pt[7][64:128, :]), r=['pt7'], w=['K2'])
    P.op('dve', lambda e: e.tensor_scalar(out=K1s[0:64, :], in0=K1[0:64, :], scalar1=-1.0, scalar2=None, op0=ALU.mult), r=['K1'], w=['K1s'])
    P.op('dve', lambda e: e.tensor_copy(K1s[64:128, :], K1[64:128, :]), r=['K1'], w=['K1s'])
    P.op('dve', lambda e: e.tensor_scalar(out=K2s[:], in0=pt[7][:], scalar1=-1.0, scalar2=None, op0=ALU.mult), r=['pt7'], w=['K2s'])

    def bc(t):
        return t[:].unsqueeze(2).to_broadcast([128, 32, 16])
    P.op('dve', lambda e: e.tensor_tensor(out=BB[:], in0=X1[:], in1=bc(K1), op=ALU.mult), r=['X1', 'K1'], w=['BB'])
    P.op('dve', lambda e: e.tensor_tensor(out=BBs[:], in0=X2[:], in1=bc(K2), op=ALU.mult), r=['X2', 'K2'], w=['BBs'])
    P.op('dve', lambda e: e.tensor_tensor(out=BB[:], in0=BB[:], in1=BBs[:], op=ALU.add), r=['BB', 'BBs'], w=['BB'])
    P.op('dve', lambda e: e.tensor_tensor(out=BBs[:], in0=X2[:], in1=bc(K1s), op=ALU.mult), r=['X2', 'K1s', 'BBs'], w=['BBs'])
    P.op('dve', lambda e: e.tensor_tensor(out=X2[:], in0=X1[:], in1=bc(K2s), op=ALU.mult), r=['X1', 'K2s', 'X2'], w=['X2'])
    P.op('dve', lambda e: e.tensor_tensor(out=BBs[:], in0=BBs[:], in1=X2[:], op=ALU.add), r=['BBs', 'X2'], w=['BBs'])
    P.op('pool', lambda e: e.memset(Cblk[:], 0.0), w=['Cblk'])
    for src, dstt, dkey in ((BB, Bblk, 'Bblk'), (BBs, Bblks, 'Bblks')):
        for m in range(4):
            pb = m % 2
            P.op('pe', lambda e, src=src, m=m, pb=pb: e.transpose(k.ps[pb][:, 0:128], src[:, 8 * m:8 * m + 8, :].rearrange("p g c -> p (g c)"), k.identF[:]),
                 r=[('BB' if src is BB else 'BBs'), 'identF'], w=['ps%d' % pb])
            for gl_ in range(8):
                P.op('act', lambda e, dstt=dstt, m=m, gl_=gl_, pb=pb: e.activation(out=dstt[:, 8 * m + gl_, :], in_=k.ps[pb][:, 0:128], func=AF.Copy,
                                                                                  scale=k.rowmask[:, gl_:gl_ + 1]), r=['ps%d' % pb, 'rowmask'], w=[dkey])
    for m in range(4):
        pb = 2 + m % 2
        P.op('pe', lambda e, m=m, pb=pb: e.transpose(k.ps[pb][:, 0:128], Cnat[:, m, :, :].rearrange("p r q -> p (r q)"), k.identF[:]), r=['Cnat', 'identF'], w=['ps%d' % pb])
        P.op('act', lambda e, m=m, pb=pb: e.activation(out=CC[0:64, m, :], in_=k.ps[pb][0:64, 0:128], func=AF.Copy), r=['ps%d' % pb], w=['CC'])
        P.op('act', lambda e, m=m, pb=pb: e.activation(out=CC[64:128, m, :], in_=k.ps[pb][64:128, 0:128], func=AF.Copy, scale=-1.0), r=['ps%d' % pb], w=['CC'])
        for gl_ in range(8):
            P.op('dve', lambda e, m=m, gl_=gl_: e.tensor_copy(Cblk[:, 8 * m + gl_, 16 * gl_:16 * gl_ + 16], CC[:, m, 16 * gl_:16 * gl_ + 16]), r=['CC'], w=['Cblk'])
    wv_in = k.w_in_even[j].rearrange("(kc p) n -> p kc n", p=128)
    nblk = NTOK // ncolb
    for bi in range(nblk):
        tiles = k.ptiles[4 * bi:4 * bi + 4] if prompt else [k.stile]
        norm_transpose(k, tiles, 0, hT, 'hT', junk, hn)
        wb, wkey = next_wbuf(k)
        wv = wb[:].rearrange("p (kc n) -> p kc n", kc=8)
        wload(k, P, ('evu', j), wv, wkey, wv_in[:, :, 1536:2048])
        for m in range(4):
            pb = m % 2
            for kc in range(8):
                P.op('pe', lambda e, pb=pb, kc=kc, m=m, wv=wv: e.matmul(k.ps[pb][:, 0:ncolb], lhsT=wv[:, kc, m * 128:(m + 1) * 128], rhs=hT[:, kc, 0:ncolb],
                                                                       start=(kc == 0), stop=(kc == 7)), r=[wkey, 'hT'], w=['ps%d' % pb])
            if m % 2 == 0:
                P.op('act', lambda e, pb=pb, m=m, bi=bi: e.activation(out=uT[:, m, bi * ncolb:(bi + 1) * ncolb], in_=k.ps[pb][:, 0:ncolb], func=AF.Copy), r=['ps%d' % pb], w=['uT'])
            else:
                P.op('dve', lambda e, pb=pb, m=m, bi=bi: e.tensor_copy(uT[:, m, bi * ncolb:(bi + 1) * ncolb], k.ps[pb][:, 0:ncolb]), r=['ps%d' % pb], w=['uT'])
    psy = [4, 5, 6, 7]
    for g in range(32):
        m = g // 8
        P.op('dve', lambda e, g=g: e.tensor_scalar(out=phi[:], in0=k.iota1[:, 0:Lh], scalar1=TH[:, g:g + 1], scalar2=None, op0=ALU.mult), r=['iota1', 'TH'], w=['phi'])
        kk = {'ang': ['phi'], 'qi': ['qi'], 'qf': ['qf'], 'r': ['qf'], 's4': ['s4'], 's2': ['s2'], 'sin': ['SIN'], 'cos': ['COS']}
        reduce_sincos('g', phi[:], Lh, qi[:], qf[:], qf[:], s2[:], s4[:], COS[:], SIN[:], kk)
        P.op('pool', lambda e, g=g: e.tensor_scalar(out=RH[:], in0=k.iota1[:, 0:Lh], scalar1=0.0, scalar2=RHO[:, g:g + 1], op0=ALU.mult, op1=ALU.add), r=['iota1', 'RHO'], w=['RH'])
        cosb = COS[:, 0:Lh].unsqueeze(1).to_broadcast([128, SC // Lh, Lh])
        sinb = SIN[:, 0:Lh].unsqueeze(1).to_broadcast([128, SC // Lh, Lh])

        def v3(ap):
            return ap.rearrange("p (b t) -> p b t", t=Lh)

        def s_mm(st, g=g, m=m):
            pa, pb_ = (0, 1) if st % 2 == 0 else (2, 3)
            c0 = st * SC
            P.op('pe', lambda e, c0=c0, pa=pa: e.matmul(k.ps[pa][:, 0:SC], lhsT=Bblk[:, g, :], rhs=uT[:, m, c0:c0 + SC], start=True, stop=True), r=['Bblk', 'uT'], w=['ps%d' % pa])
            P.op('pe', lambda e, c0=c0, pb_=pb_: e.matmul(k.ps[pb_][:, 0:SC], lhsT=Bblks[:, g, :], rhs=uT[:, m, c0:c0 + SC], start=True, stop=True), r=['Bblks', 'uT'], w=['ps%d' % pb_])
        s_mm(0)
        for st in range(nstep):
            pa, pb_ = (0, 1) if st % 2 == 0 else (2, 3)
            P.op('dve', lambda e, pa=pa: e.tensor_tensor(out=v3(a1[:, 0:SC]), in0=v3(k.ps[pa][:, 0:SC]), in1=cosb, op=ALU.mult), r=['ps%d' % pa, 'COS'], w=['a1'])
            P.op('dve', lambda e, pb_=pb_: e.tensor_tensor(out=v3(a2[:, 0:SC]), in0=v3(k.ps[pb_][:, 0:SC]), in1=sinb, op=ALU.mult), r=['ps%d' % pb_, 'SIN'], w=['a2'])
            P.op('dve', lambda e: e.tensor_tensor(out=Sp[:, 0:SC], in0=a1[:, 0:SC], in1=a2[:, 0:SC], op=ALU.subtract), r=['a1', 'a2'], w=['Sp'])
            if st + 1 < nstep:
                s_mm(st + 1)
            for b in range(SC // Lh):
                if prompt:
                    init = H0[:, 0, g:g + 1] if st == 0 else carry[:, 0:1]
                    ikey = 'H0' if st == 0 else 'carry'
                else:
                    init = H0[:, b, g:g + 1]
                    ikey = 'H0'
                P.op('dve', lambda e, b=b, init=init: e.tensor_tensor_scan(out=gb[:, b * Lh:(b + 1) * Lh], data0=RH[:, 0:Lh], data1=Sp[:, b * Lh:(b + 1) * Lh],
                                                                           initial=init, op0=ALU.mult, op1=ALU.add), r=['RH', 'Sp', ikey], w=['gb'])
            P.op('pe', lambda e, pa=pa: e.matmul(k.ps[pa][:, 0:SC], lhsT=swp[:, :], rhs=gb[:, 0:SC], start=True, stop=True), r=['gb', 'swp'], w=['ps%d' % pa])
            P.op('pool', lambda e: e.tensor_tensor(out=v3(a2[:, 0:SC]), in0=v3(gb[:, 0:SC]), in1=cosb, op=ALU.mult), r=['gb', 'COS'], w=['a2'])
            P.op('dve', lambda e, pa=pa: e.tensor_tensor(out=v3(a1[:, 0:SC]), in0=v3(k.ps[pa][:, 0:SC]), in1=sinb, op=ALU.mult), r=['ps%d' % pa, 'SIN'], w=['a1'])
            P.op('dve', lambda e: e.tensor_tensor(out=Hb[:, 0:SC], in0=a1[:, 0:SC], in1=a2[:, 0:SC], op=ALU.add), r=['a1', 'a2'], w=['Hb'])
            if prompt:
                if st < nstep - 1:
                    P.op('dve', lambda e: e.tensor_tensor(out=carry[:, 0:1], in0=a1[:, SC - 1:SC], in1=a2[:, SC - 1:SC], op=ALU.add), r=['a1', 'a2'], w=['carry'])
                else:
                    P.op('dve', lambda e, g=g: e.tensor_tensor(out=Hlast[:, 0, g:g + 1], in0=a1[:, SC - 1:SC], in1=a2[:, SC - 1:SC], op=ALU.add), r=['a1', 'a2'], w=['Hlast'])
            else:
                P.op('dve', lambda e, g=g: e.tensor_tensor(out=Hlast[:, :, g:g + 1], in0=v3(a1[:, 0:SC])[:, :, Lh - 1:Lh], in1=v3(a2[:, 0:SC])[:, :, Lh - 1:Lh], op=ALU.add),
                     r=['a1', 'a2'], w=['Hlast'])
            P.op('pe', lambda e, g=g, st=st: e.matmul(k.ps[psy[st]][:, 0:SC], lhsT=Cblk[:, g, :], rhs=Hb[:, 0:SC], start=(g % 8 == 0), stop=(g % 8 == 7)),
                 r=['Cblk', 'Hb'], w=['ps%d' % psy[st]])
        if g % 8 == 7:
            for st in range(nstep):
                c0 = st * SC
                ga, gbk = gl[0][:, 0:SC], gl[1][:, 0:SC]
                P.op('dve', lambda e, m=m, st=st, c0=c0, ga=ga: e.scalar_tensor_tensor(out=ga, in0=uT[:, m, c0:c0 + SC], scalar=Dv[:, m:m + 1], in1=k.ps[psy[st]][:, 0:SC],
                                                                                      op0=ALU.mult, op1=ALU.add), r=['uT', 'Dv', 'ps%d' % psy[st]], w=['gl0'])
                P.op('pool', lambda e, ga=ga, gbk=gbk: e.tensor_tensor(out=gbk, in0=ga, in1=ga, op=ALU.mult), r=['gl0'], w=['gl1'])
                P.op('pool', lambda e, gbk=gbk: e.tensor_scalar(out=gbk, in0=gbk, scalar1=GELU_C, scalar2=1.0, op0=ALU.mult, op1=ALU.add), r=['gl1'], w=['gl1'])
                P.op('pool', lambda e, ga=ga, gbk=gbk: e.tensor_tensor(out=gbk, in0=gbk, in1=ga, op=ALU.mult), r=['gl1', 'gl0'], w=['gl1'])
                P.op('act', lambda e, gbk=gbk: e.activation(out=gbk, in_=gbk, func=AF.Sigmoid, scale=GELU_S), r=['gl1'], w=['gl1'])
                P.op('dve', lambda e, m=m, c0=c0, ga=ga, gbk=gbk: e.tensor_tensor(out=uT[:, m, c0:c0 + SC], in0=ga, in1=gbk, op=ALU.mult), r=['gl0', 'gl1'], w=['uT'])
    wb, wkey = next_wbuf(k)
    wg = wb[:, 0:2048].rearrange("p (kc n) -> p kc n", kc=4)
    P.op('pool', lambda e: e.dma_start(out=wg, in_=k.s5_w_glu[j].rearrange("(kc p) n -> p kc n", p=128)), w=[wkey], dsem=wkey, hoist=True)
    cnt = 0
    for c0 in range(0, NTOK, ncolb):
        for mo in range(4):
            pb = cnt % 2
            cnt += 1
            for kc in range(4):
                P.op('pe', lambda e, pb=pb, kc=kc, mo=mo, c0=c0: e.matmul(k.ps[pb][:, 0:ncolb], lhsT=wg[:, kc, mo * 128:(mo + 1) * 128], rhs=uT[:, kc, c0:c0 + ncolb],
                                                                        start=(kc == 0), stop=(kc == 3)), r=[wkey, 'uT'], w=['ps%d' % pb])
            gs = gl[pb][:, 0:ncolb]
            P.op('act', lambda e, pb=pb, mo=mo, gs=gs: e.activation(out=gs, in_=k.ps[pb][:, 0:ncolb], func=AF.Sigmoid, bias=bglu[:, mo:mo + 1]), r=['ps%d' % pb, 'bglu'], w=['gl%d' % pb])
            P.op('dve', lambda e, mo=mo, c0=c0, gs=gs: e.tensor_tensor(out=o_bT[:, mo, c0:c0 + ncolb], in0=gs, in1=uT[:, mo, c0:c0 + ncolb], op=ALU.mult), r=['gl%d' % pb, 'uT'], w=['o_bT'])
    s5o = k.s5_p if prompt else k.s5_s
    for b in range(nseq):
        for ri in range(2):
            dst = (s5o[j] if prompt else s5o[j, b]).rearrange("g p r -> p g r")[:, :, ri]
            P.op('sp', lambda e, dst=dst, b=b, ri=ri: e.dma_start(out=dst, in_=Hlast[64 * ri:64 * ri + 64, b, :], allow_slow_non_contiguous=True), r=['Hlast'], dsem='s5o', chain=False)
    return o_bT


def attn_phase_p(k, l, j):
    P = k.P
    P.phase_begin(keep=k.o_bT_bytes)
    o_bT = k.o_bT
    phase_common(k, l, (0, 1), nwbuf=2)
    KT = P.sb([128, 4, SEQ], BF16, 'KT')
    Vb = P.sb([128, NT, 512], BF16, 'Vb')
    QT = P.sb([128, 4, 512], BF16, 'QT')
    oaT = P.sb([128, 4, 512], BF16, 'oaT')
    hT = P.sb([128, 8, 512], BF16, 'hT')
    G = P.sb([128, 2304], BF16, 'G')
    kaug = P.sb([128, 128], BF16, 'kaug')
    qaug = P.sb([128, 512], BF16, 'qaug')
    ones_b = P.sb([128, 128], BF16, 'ones_b')
    Pt = [P.sb([128, 512], BF16, 'Pt%d' % i) for i in range(3)]
    kst = [P.sb([128, 512], F32, 'kst%d' % i) for i in range(2)]
    rb = P.sb([128, 512], F32, 'rb')
    junk = [P.sb([128, D], BF16, 'junk0')] * 2
    hn = [P.sb([128, D], BF16, 'hn0')] * 2
    mtmp = P.sb([128, 4, D], F32, 'mtmp')
    ssh = P.sb([128, 16], F32, 'ssh')
    tmp_t = [P.sb([128, D], F32, 'tmpt0')] * 2
    P.op('pool', lambda e: e.dma_start(out=G[:], in_=k.c_G[:, :]), w=['G'], dsem='G')
    P.op('dve', lambda e: e.memset(kaug[:], 0.0), w=['kaug'])
    P.op('dve', lambda e: e.memset(qaug[:], 0.0), w=['qaug'])
    P.op('pool', lambda e: e.dma_start(out=kaug[0:4, :], in_=k.c_kaug[:, :]), w=['kaug'], dsem='kaug')
    P.op('pool', lambda e: e.dma_start(out=qaug[0:4, :], in_=k.c_qaug[:, :]), w=['qaug'], dsem='qaug')
    P.op('dve', lambda e: e.memset(ones_b[:], 1.0), w=['ones_b'])
    wv_in = k.w_in_even[j].rearrange("(kc p) n -> p kc n", p=128)
    wov = k.w_out_even[j].rearrange("(kc p) n -> p kc n", p=128)
    sctr = 0
    for qc in range(4):
        tiles = k.ptiles[4 * qc:4 * qc + 4]
        norm_transpose(k, tiles, 0, hT, 'hT', junk, hn)
        wb, wkey = next_wbuf(k)
        wv = wb[:].rearrange("p (kc n) -> p kc n", kc=8)
        wload(k, P, ('evqkv', j, 0), wv, wkey, wv_in[:, :, 0:512])
        for m in range(4):
            pb = m % 2
            for kc in range(8):
                P.op('pe', lambda e, pb=pb, kc=kc, m=m, wv=wv: e.matmul(k.ps[pb][:, :], lhsT=wv[:, kc, m * 128:(m + 1) * 128], rhs=hT[:, kc, :], start=(kc == 0), stop=(kc == 7)),
                     r=[wkey, 'hT'], w=['ps%d' % pb])
            P.op('act', lambda e, pb=pb, m=m: e.activation(out=QT[:, m, :], in_=k.ps[pb][:, :], func=AF.Copy, scale=k.qscale[:, m:m + 1]), r=['ps%d' % pb, 'qscale'], w=['QT'])
        wb, wkey = next_wbuf(k)
        wv = wb[:].rearrange("p (kc n) -> p kc n", kc=8)
        wload(k, P, ('evqkv', j, 1), wv, wkey, wv_in[:, :, 512:1024])
        for m in range(4):
            pb = m % 2
            for kc in range(8):
                P.op('pe', lambda e, pb=pb, kc=kc, m=m, wv=wv: e.matmul(k.ps[pb][:, :], lhsT=wv[:, kc, m * 128:(m + 1) * 128], rhs=hT[:, kc, :], start=(kc == 0), stop=(kc == 7)),
                     r=[wkey, 'hT'], w=['ps%d' % pb])
            P.op('dve', lambda e, pb=pb, m=m, qc=qc: e.tensor_copy(KT[:, m, qc * 512:(qc + 1) * 512], k.ps[pb][:, :]), r=['ps%d' % pb], w=['KT'])
        for ti in range(4):
            pb = 2 + ti % 2
            for kc in range(8):
                P.op('pe', lambda e, pb=pb, kc=kc, ti=ti, wv=wv: e.matmul(k.ps[pb][:, :], lhsT=hT[:, kc, ti * 128:(ti + 1) * 128], rhs=wv[:, kc, :], start=(kc == 0), stop=(kc == 7)),
                     r=[wkey, 'hT'], w=['ps%d' % pb])
            ks = kst[ti % 2]
            P.op('dve', lambda e, pb=pb, ks=ks: e.tensor_copy(ks[:, :], k.ps[pb][:, :]), r=['ps%d' % pb], w=['kst%d' % (ti % 2)])
            row0 = (4 * qc + ti) * 128
            P.op('sp', lambda e, ks=ks, row0=row0: e.dma_start(out=k.k_p[j, row0:row0 + 128, :], in_=ks[:, :]), r=['kst%d' % (ti % 2)], dsem='kst%d' % (ti % 2))
        wb, wkey = next_wbuf(k)
        wv = wb[:].rearrange("p (kc n) -> p kc n", kc=8)
        wload(k, P, ('evqkv', j, 2), wv, wkey, wv_in[:, :, 1024:1536])
        for ti in range(4):
            pb = 2 + ti % 2
            for kc in range(8):
                P.op('pe', lambda e, pb=pb, kc=kc, ti=ti, wv=wv: e.matmul(k.ps[pb][:, :], lhsT=hT[:, kc, ti * 128:(ti + 1) * 128], rhs=wv[:, kc, :], start=(kc == 0), stop=(kc == 7)),
                     r=[wkey, 'hT'], w=['ps%d' % pb])
            ks = kst[ti % 2]
            P.op('dve', lambda e, pb=pb, ks=ks: e.tensor_copy(ks[:, :], k.ps[pb][:, :]), r=['ps%d' % pb], w=['kst%d' % (ti % 2)])
            P.op('act', lambda e, pb=pb, ti=ti, qc=qc: e.activation(out=Vb[:, 4 * qc + ti, :], in_=k.ps[pb][:, :], func=AF.Copy), r=['ps%d' % pb], w=['Vb'])
            row0 = (4 * qc + ti) * 128
            P.op('sp', lambda e, ks=ks, row0=row0: e.dma_start(out=k.v_p[j, row0:row0 + 128, :], in_=ks[:, :]), r=['kst%d' % (ti % 2)], dsem='kst%d' % (ti % 2))
        nj = 4 * qc + 4
        items = [(h, jt) for h in range(8) for jt in range(nj)]

        def geom(h, jt):
            t0 = max(512 * qc, 128 * jt)
            off = t0 - 512 * qc
            return off, 512 - off, 128 * jt - 512 * qc, t0 - 128 * jt + 128

        def scores(idx):
            h, jt = items[idx]
            m, hh = h // 2, h % 2
            rows = slice(64 * hh, 64 * hh + 64)
            off, N, cj, x0 = geom(h, jt)
            sb = idx % 3
            P.op('pe', lambda e, sb=sb, m=m, rows=rows, jt=jt, off=off, N=N: e.matmul(
                k.ps[sb][:, 0:N], lhsT=KT[rows, m, jt * 128:(jt + 1) * 128], rhs=QT[rows, m, off:512], start=True, stop=False),
                r=['KT', 'QT'], w=['ps%d' % sb])
            P.op('pe', lambda e, sb=sb, off=off, N=N: e.matmul(k.ps[sb][:, 0:N], lhsT=kaug[:, :], rhs=qaug[:, off:512], start=False, stop=True),
                 r=['kaug', 'qaug'], w=['ps%d' % sb])
        scores(0)
        for idx, (h, jt) in enumerate(items):
            if idx + 1 < len(items):
                scores(idx + 1)
            m, hh = h // 2, h % 2
            rows = slice(64 * hh, 64 * hh + 64)
            slope = 2.0 ** (-(h + 1))
            pv = 3 + hh
            pd = 5 + hh
            off, N, cj, x0 = geom(h, jt)
            sb = idx % 3
            pt = Pt[sb]
            P.op('act', lambda e, sb=sb, pt=pt, N=N, slope=slope, cj=cj: e.activation(out=pt[:, 0:N], in_=k.ps[sb][:, 0:N], func=AF.Exp, scale=slope, bias=float(slope * cj)),
                 r=['ps%d' % sb], w=['Pt%d' % sb])
            P.op('pool' if idx % 2 == 0 else 'dve', lambda e, pt=pt, N=N, x0=x0: e.tensor_tensor(out=pt[:, 0:N], in0=pt[:, 0:N], in1=G[:, x0:x0 + N], op=ALU.mult), r=['Pt%d' % sb, 'G'], w=['Pt%d' % sb])
            f0, f1 = (jt == 0), (jt == nj - 1)
            P.op('pe', lambda e, pv=pv, pt=pt, jt=jt, m=m, off=off, N=N, f0=f0, f1=f1: e.matmul(k.ps[pv][:, off:512], lhsT=Vb[:, jt, m * 128:(m + 1) * 128], rhs=pt[:, 0:N],
                                                                                               start=f0, stop=f1), r=['Vb', 'Pt%d' % sb], w=['ps%d' % pv])
            P.op('pe', lambda e, pd=pd, pt=pt, off=off, N=N, f0=f0, f1=f1: e.matmul(k.ps[pd][:, off:512], lhsT=ones_b[:, :], rhs=pt[:, 0:N], start=f0, stop=f1),
                 r=['ones_b', 'Pt%d' % sb], w=['ps%d' % pd])
            if jt == nj - 1:
                P.op('dve', lambda e, pd=pd, rows=rows: e.reciprocal(out=rb[rows, :], in_=k.ps[pd][rows, :]), r=['ps%d' % pd], w=['rb%d' % hh])
                P.op('dve', lambda e, pv=pv, rows=rows, m=m: e.tensor_tensor(out=oaT[rows, m, :], in0=k.ps[pv][rows, :], in1=rb[rows, :], op=ALU.mult),
                     r=['ps%d' % pv, 'rb%d' % hh], w=['oaT'])

        def src(kc, col, np_, qc=qc):
            if kc < 4:
                return oaT[:, kc, col:col + np_], 'oaT'
            return o_bT[:, kc - 4, qc * 512 + col:qc * 512 + col + np_], 'o_bT'
        proj_out(k, tiles, src, None, wov, 8, mtmp, ssh, junk, wtag=('evo', j))
        post_norm_add(k, tiles, 1, mtmp, ssh, tmp_t)


def attn_phase_s(k, l, j):
    P = k.P
    P.phase_begin(keep=k.o_bT_bytes)
    o_bT = k.o_bT
    phase_common(k, l, (0, 1), nwbuf=2)
    NK = 17
    hT = P.sb([128, 8, NSTOK], BF16, 'hT')
    QT = P.sb([128, 4, NSTOK], BF16, 'QT')
    KTn = P.sb([128, 4, NSTOK], BF16, 'KTn')
    Vn = P.sb([128, 512], BF16, 'Vn')
    kc_nat = P.sb([128, 16, 512], BF16, 'kc_nat')
    Vc = P.sb([128, 16, 512], BF16, 'Vc')
    KTb = P.sb([128, 4, 2048], BF16, 'KTb')
    oaT = P.sb([128, 4, NSTOK], BF16, 'oaT')
    Gs = P.sb([128, NK, 64], BF16, 'Gs')
    Gsn = P.sb([128, NSS, 64], BF16, 'Gsn')
    kaug = P.sb([4, NK, 128], BF16, 'kaug')
    qaug = P.sb([4, 64], BF16, 'qaug')
    ones_b = P.sb([128, 128], BF16, 'ones_b')
    Pall = P.sb([128, NK, 64], BF16, 'Pall')
    kst = [P.sb([128, 512], F32, 'kst%d' % i) for i in range(2)]
    rb = P.sb([128, 64], F32, 'rb')
    junk = [P.sb([128, D], BF16, 'junk0')] * 2
    hn = [P.sb([128, D], BF16, 'hn0')] * 2
    mtmp = P.sb([128, 1, D], F32, 'mtmp')
    ssh = P.sb([128, 16], F32, 'ssh')
    tmp_t = [P.sb([128, D], F32, 'tmpt0')] * 2
    P.op('pool', lambda e: e.dma_start(out=Gs[:], in_=k.c_Gs[:, :, :]), w=['Gs'], dsem='G')
    P.op('pool', lambda e: e.dma_start(out=Gsn[0:NSTOK, :, :], in_=k.c_Gsn[:, :, :]), w=['Gsn'], dsem='G')
    P.op('pool', lambda e: e.dma_start(out=kaug[:], in_=k.c_kaug_s[:, :, :]), w=['kaug'], dsem='kaug')
    P.op('pool', lambda e: e.dma_start(out=qaug[:], in_=k.c_qaug_s[:, :]), w=['qaug'], dsem='qaug')
    P.op('dve', lambda e: e.memset(ones_b[:], 1.0), w=['ones_b'])
    wv_in = k.w_in_even[j].rearrange("(kc p) n -> p kc n", p=128)
    wov = k.w_out_even[j].rearrange("(kc p) n -> p kc n", p=128)
    tiles = [k.stile]
    T = NSTOK
    norm_transpose(k, tiles, 0, hT, 'hT', junk, hn)
    for which in range(3):
        wb, wkey = next_wbuf(k)
        wv = wb[:].rearrange("p (kc n) -> p kc n", kc=8)
        wload(k, P, ('evqkv', j, which), wv, wkey, wv_in[:, :, 512 * which:512 * which + 512])
        if which < 2:
            for m in range(4):
                pb = m % 2
                for kc in range(8):
                    P.op('pe', lambda e, pb=pb, kc=kc, m=m, wv=wv: e.matmul(k.ps[pb][:, 0:T], lhsT=wv[:, kc, m * 128:(m + 1) * 128], rhs=hT[:, kc, 0:T], start=(kc == 0), stop=(kc == 7)),
                         r=[wkey, 'hT'], w=['ps%d' % pb])
                if which == 0:
                    P.op('act', lambda e, pb=pb, m=m: e.activation(out=QT[:, m, :], in_=k.ps[pb][:, 0:T], func=AF.Copy, scale=0.125), r=['ps%d' % pb], w=['QT'])
                else:
                    P.op('dve', lambda e, pb=pb, m=m: e.tensor_copy(KTn[:, m, :], k.ps[pb][:, 0:T]), r=['ps%d' % pb], w=['KTn'])
        if which >= 1:
            pb = 2 + which % 2
            for kc in range(8):
                P.op('pe', lambda e, pb=pb, kc=kc, wv=wv: e.matmul(k.ps[pb][0:T, :], lhsT=hT[:, kc, 0:T], rhs=wv[:, kc, :], start=(kc == 0), stop=(kc == 7)),
                     r=[wkey, 'hT'], w=['ps%d' % pb])
            ks = kst[which % 2]
            P.op('dve', lambda e, pb=pb, ks=ks: e.tensor_copy(ks[0:T, :], k.ps[pb][0:T, :]), r=['ps%d' % pb], w=['kst%d' % (which % 2)])
            dst = k.k_s if which == 1 else k.v_s
            P.op('sp', lambda e, ks=ks, dst=dst: e.dma_start(out=dst[j, :, :], in_=ks[0:T, :]), r=['kst%d' % (which % 2)], dsem='kst%d' % (which % 2))
            if which == 2:
                P.op('act', lambda e, pb=pb: e.activation(out=Vn[0:T, :], in_=k.ps[pb][0:T, :], func=AF.Copy), r=['ps%d' % pb], w=['Vn'])
    for b in range(NSS):
        ckv = k.cache_k[j, b].rearrange("(t p) f -> p t f", p=128)
        cvv = k.cache_v[j, b].rearrange("(t p) f -> p t f", p=128)
        for q4 in range(4):
            P.op('pool', lambda e, ckv=ckv, q4=q4: e.dma_start(out=kc_nat[:, 4 * q4:4 * q4 + 4, :], in_=ckv[:, 4 * q4:4 * q4 + 4, :]), w=['kc_nat'], dsem='kc%d' % q4)
        for q4 in range(4):
            P.op('pool', lambda e, cvv=cvv, q4=q4: e.dma_start(out=Vc[:, 4 * q4:4 * q4 + 4, :], in_=cvv[:, 4 * q4:4 * q4 + 4, :]), w=['Vc'], dsem='vc%d' % q4)
        cnt = 0
        for m in range(4):
            for j8 in range(2):
                pb = cnt % 2
                cnt += 1
                for jj in range(8):
                    jt = 8 * j8 + jj
                    P.op('pe', lambda e, pb=pb, jj=jj, jt=jt, m=m: e.transpose(k.psb[pb][:, jj * 128:(jj + 1) * 128], kc_nat[:, jt, m * 128:(m + 1) * 128], k.ident[:, :]),
                         r=['kc_nat', 'ident'], w=['ps%d' % pb])
                if cnt % 2 == 0:
                    P.op('act', lambda e, pb=pb, m=m, j8=j8: e.activation(out=KTb[:, m, 1024 * j8:1024 * j8 + 1024], in_=k.psb[pb][:, :], func=AF.Copy), r=['ps%d' % pb], w=['KTb'])
                else:
                    P.op('dve', lambda e, pb=pb, m=m, j8=j8: e.tensor_copy(KTb[:, m, 1024 * j8:1024 * j8 + 1024], k.psb[pb][:, :]), r=['ps%d' % pb], w=['KTb'])
        for jt in range(NK):
            M = 128 if jt < 16 else NSTOK
            sb = 2 + jt % 2
            for h in range(8):
                m, hh = h // 2, h % 2
                rows = slice(64 * hh, 64 * hh + 64)
                lhs = KTb[rows, m, jt * 128:(jt + 1) * 128] if jt < 16 else KTn[rows, m, 0:NSTOK]
                lk = 'KTb' if jt < 16 else 'KTn'
                P.op('pe', lambda e, sb=sb, lhs=lhs, rows=rows, m=m, h=h, M=M, b=b: e.matmul(
                    k.ps[sb][0:M, h * 8:(h + 1) * 8], lhsT=lhs, rhs=QT[rows, m, 8 * b:8 * b + 8], start=True, stop=False), r=[lk, 'QT'], w=['ps%d' % sb])
                P.op('pe', lambda e, sb=sb, jt=jt, h=h, M=M: e.matmul(k.ps[sb][0:M, h * 8:(h + 1) * 8], lhsT=kaug[0:4, jt, 0:M], rhs=qaug[0:4, h * 8:(h + 1) * 8],
                                                                     start=False, stop=True), r=['kaug', 'qaug'], w=['ps%d' % sb])
            P.op('act', lambda e, sb=sb, jt=jt, M=M: e.activation(out=Pall[0:M, jt, :], in_=k.ps[sb][0:M, 0:64], func=AF.Exp), r=['ps%d' % sb], w=['Pall'])
            gmask = Gs[0:M, jt, :] if jt < 16 else Gsn[0:M, b, :]
            P.op('pool', lambda e, jt=jt, M=M, gmask=gmask: e.tensor_tensor(out=Pall[0:M, jt, :], in0=Pall[0:M, jt, :], in1=gmask, op=ALU.mult),
                 r=['Pall', 'Gs', 'Gsn'], w=['Pall'])
        for h in range(8):
            m = h // 2
            for jt in range(NK):
                M = 128 if jt < 16 else NSTOK
                lhs = Vc[:, jt, m * 128:(m + 1) * 128] if jt < 16 else Vn[0:NSTOK, m * 128:(m + 1) * 128]
                lk = 'Vc' if jt < 16 else 'Vn'
                f0, f1 = (jt == 0), (jt == NK - 1)
                P.op('pe', lambda e, lhs=lhs, h=h, jt=jt, M=M, f0=f0, f1=f1: e.matmul(k.ps[4][:, h * 8:(h + 1) * 8], lhsT=lhs, rhs=Pall[0:M, jt, h * 8:(h + 1) * 8], start=f0, stop=f1),
                     r=[lk, 'Pall'], w=['ps4'])
            for jt in range(NK):
                M = 128 if jt < 16 else NSTOK
                f0, f1 = (jt == 0), (jt == NK - 1)
                P.op('pe', lambda e, h=h, jt=jt, M=M, f0=f0, f1=f1: e.matmul(k.ps[5][:, h * 8:(h + 1) * 8], lhsT=ones_b[0:M, :], rhs=Pall[0:M, jt, h * 8:(h + 1) * 8], start=f0, stop=f1),
                     r=['ones_b', 'Pall'], w=['ps5'])
        P.op('dve', lambda e: e.reciprocal(out=rb[:, :], in_=k.ps[5][:, 0:64]), r=['ps5'], w=['rb'])
        for hh in range(2):
            rows = slice(64 * hh, 64 * hh + 64)
            P.op('dve', lambda e, rows=rows, hh=hh, b=b: e.tensor_tensor(
                out=oaT[rows, :, 8 * b:8 * b + 8], in0=k.ps[4][rows, 0:64].rearrange("p (m x t) -> p m x t", x=2, t=8)[:, :, hh, :],
                in1=rb[rows, :].rearrange("p (m x t) -> p m x t", x=2, t=8)[:, :, hh, :], op=ALU.mult), r=['ps4', 'rb'], w=['oaT'])

    def src(kc, col, np_):
        if kc < 4:
            return oaT[:, kc, col:col + np_], 'oaT'
        return o_bT[:, kc - 4, col:col + np_], 'o_bT'
    proj_out(k, tiles, src, None, wov, 8, mtmp, ssh, junk, wtag=('evo', j))
    post_norm_add(k, tiles, 1, mtmp, ssh, tmp_t)


_CACHE = {}


def host_constants():
    c = {}
    i128 = np.arange(128)
    c['c_maskp'] = (i128[None, :] >= i128[:, None]).astype(np.float32)
    se = np.zeros((128, 128), np.float32)
    se[127, :] = 1.0
    c['c_selendp'] = se
    t = np.arange(NSTOK)
    same = (t[:, None] // LS) == (t[None, :] // LS)
    c['c_masks'] = (same & (t[None, :] >= t[:, None])).astype(np.float32)
    c['c_selends'] = (t[:, None] == (LS * (t[None, :] // LS) + LS - 1)).astype(np.float32)
    sb = np.zeros((NSTOK, NSS, 128), np.float32)
    for b in range(NSS):
        sb[LS * b + LS - 1, b, :] = 1.0
    c['c_selendBs'] = sb
    sc = np.zeros((128, NSS, NSTOK), np.float32)
    sr = np.zeros((NSTOK, NSS), np.float32)
    for b in range(NSS):
        sc[:, b, LS * b:LS * (b + 1)] = 1.0
        sr[LS * b:LS * (b + 1), b] = 1.0
    c['c_seqcol'] = sc
    c['c_seqrow'] = sr
    c['c_negp'] = (c['c_maskp'] - 1.0) * 30000.0
    c['c_negs'] = (c['c_masks'] - 1.0) * 30000.0

    def cmult(d):
        d = np.asarray(d)
        return (((d >= 0) & (d <= 128)).astype(np.float32) + ((d >= 0) & (d <= 512) & (d % 4 == 0)).astype(np.float32)
                + ((d >= 0) & (d <= 2048) & (d % 16 == 0)).astype(np.float32))
    rl = np.arange(128)
    xx = np.arange(2304)
    c['c_G'] = cmult(xx[None, :] - 128 - rl[:, None]).astype(np.float32)
    ka = np.zeros((4, 128), np.float32)
    ka[0] = rl
    ka[1] = 1.0
    ka[2] = 1.0
    c['c_kaug'] = ka
    tl = np.arange(512)
    qa = np.zeros((4, 512), np.float32)
    qa[0] = 1.0
    qa[1] = -128.0 * (tl // 128)
    qa[2] = -(tl % 128)
    c['c_qaug'] = qa
    qs = np.zeros((128, 4), np.float32)
    for m in range(4):
        for p in range(128):
            qs[p, m] = 2.0 ** (2 * m + p // 64 - 2)
    c['c_qscale'] = qs
    sw = np.zeros((128, 128), np.float32)
    for p in range(64):
        sw[64 + p, p] = -1.0
        sw[p, 64 + p] = 1.0
    c['c_swp'] = sw
    c['c_rowmask'] = (rl[:, None] // 16 == np.arange(8)[None, :]).astype(np.float32)
    c['c_iota1'] = np.broadcast_to(np.arange(1, 513, dtype=np.float32)[None, :], (128, 512)).copy()
    jt = np.arange(17)
    tq = np.arange(LS)
    d = 2048 + tq[None, None, :] - (128 * jt[None, :, None] + rl[:, None, None])
    valid = (128 * jt[None, :, None] + rl[:, None, None]) < 2048 + LS
    gs = cmult(d) * valid
    c['c_Gs'] = np.repeat(gs[:, :, None, :], 8, axis=2).reshape(128, 17, 64).astype(np.float32)
    kas = np.zeros((4, 17, 128), np.float32)
    kas[0] = jt[:, None]
    kas[1] = rl[None, :]
    kas[2] = 1.0
    kas[3] = 1.0
    kas[1, 16, :] = rl % LS
    c['c_kaug_s'] = kas
    rk = np.arange(NSTOK)
    gn = np.zeros((NSTOK, NSS, 8, LS), np.float32)
    for b in range(NSS):
        dd = tq[None, :] - (rk[:, None] % LS)
        gn[:, b, :, :] = (cmult(dd) * ((rk[:, None] // LS) == b))[:, None, :]
    c['c_Gsn'] = gn.reshape(NSTOK, NSS, 64)
    qas = np.zeros((4, 8, LS), np.float32)
    for h in range(8):
        sl = 2.0 ** (-(h + 1))
        qas[0, h, :] = sl * 128.0
        qas[1, h, :] = sl
        qas[2, h, :] = -sl * 2048.0
        qas[3, h, :] = -sl * tq
    c['c_qaug_s'] = qas.reshape(4, 64)
    return c


def make_in_maps(inputs):
    f = lambda a: np.ascontiguousarray(np.asarray(a, dtype=np.float32))
    norms = f(np.stack([inputs['norm_mix_pre'], inputs['norm_mix_post'], inputs['norm_mlp_pre'], inputs['norm_mlp_post']]))
    shared = {
        'norms': norms,
        'w_mlp_up': f(inputs['w_mlp_up']),
        'w_mlp_down': f(inputs['w_mlp_down']),
        'identf': np.eye(128, dtype=np.float32),
        'w_in_odd': f(inputs['w_in_odd']), 'w_out_odd': f(inputs['w_out_odd']),
        'conv_w': f(inputs['conv_w']), 'conv_b': f(inputs['conv_b']), 'dt_bias': f(inputs['dt_bias']),
        'a_log': f(inputs['a_log']), 'd_skip': f(inputs['d_skip']), 'gnorm_w': f(inputs['gnorm_w']),
    }
    for nm in ('w_in_even', 'w_out_even', 's5_lambda_re', 's5_lambda_im', 's5_log_dt', 's5_b_re', 's5_b_im', 's5_c_re', 's5_c_im',
               's5_d', 's5_w_glu', 's5_b_glu'):
        shared[nm] = f(inputs[nm])
    shared.update(host_constants())
    maps = []
    for c in range(NCORES):
        m = dict(shared)
        m['x_p'] = f(inputs['x_prompt'][c])
        m['x_s'] = f(inputs['x_sample'][NSS * c:NSS * (c + 1)]).reshape(NSTOK, D)
        m['state_conv'] = f(inputs['state_conv'][:, NSS * c:NSS * (c + 1)])
        m['state_s5'] = f(inputs['state_s5'][:, NSS * c:NSS * (c + 1)])
        m['cache_k'] = f(inputs['cache_k'][:, NSS * c:NSS * (c + 1)]).reshape(2, NSS, 2048, 512)
        m['cache_v'] = f(inputs['cache_v'][:, NSS * c:NSS * (c + 1)]).reshape(2, NSS, 2048, 512)
        m['state_ssm'] = f(inputs['state_ssm'][:, NSS * c:NSS * (c + 1)])
        maps.append(m)
    return maps


def kernel(**inputs):
    cfg = {}
    if 'nc' not in _CACHE:
        _CACHE['nc'] = build(cfg)
    nc, k = _CACHE['nc']
    maps = make_in_maps(inputs)
    res = run_bass_kernel_spmd(nc, maps, core_ids=list(range(NCORES)))
    r = res.results
    C = range(NCORES)
    st = lambda name, shp: np.stack([r[c][name].reshape(shp) for c in C], axis=1)
    cat = lambda name, shp: np.concatenate([r[c][name].reshape(shp) for c in C], axis=1)
    y_p = np.stack([r[c]['y_p'] for c in C])
    y_s = np.concatenate([r[c]['y_s'].reshape(NSS, LS, D) for c in C])
    k_p = st('k_p', (2, SEQ, 8, 64))
    v_p = st('v_p', (2, SEQ, 8, 64))
    s5_p = st('s5_p', (2, 32, 64, 2))
    conv_p = st('conv_p', (2, 3, NXBC))
    ssm_p = st('ssm_p', (2, 32, 64, 128))
    k_s = cat('k_s', (2, NSS, LS, 8, 64))
    v_s = cat('v_s', (2, NSS, LS, 8, 64))
    s5_s = cat('s5_s', (2, NSS, 32, 64, 2))
    conv_s = cat('conv_s', (2, NSS, 3, NXBC))
    ssm_s = cat('ssm_s', (2, NSS, 32, 64, 128))
    return (y_p, y_s, k_p, v_p, s5_p, conv_p, ssm_p, k_s, v_s, s5_s, conv_s, ssm_s)
```
